# Optimizing a Trainium2 kernel written in Bass

```python
import jax, jax.numpy as jnp
from jax import lax
import numpy as np

D_MODEL = 1024
BATCH = 4
SEQ = 4096
DEPTH = 2

HEAD_DIM = 64
W_SSM = 512
W_NA = 512
W_DIL = 512
MIX_WIDTH = W_SSM + W_NA + W_DIL
SSM_GROUP = 16
SSM_GROUPS = W_SSM // SSM_GROUP
SSM_STATE = 64
N_DIRS = 2
NA_HEADS = W_NA // HEAD_DIM
DIL_HEADS = W_DIL // HEAD_DIM
GRID_W = 64
NA_MAX_ROWS = 8
NA_COLS = 16
DIL_PATTERNS = ((128, 1), (512, 4), (2048, 16))
DIL_PAD = 1024
Q_BLOCK = 128
T5_BUCKETS = 32
T5_MAX_DIST = 1024
DT_MIN = 1e-3
DT_MAX = 1e-1
RMS_EPS = 1e-6
NEG_INF = -1e30
IN_COLS = 2 * W_SSM + 4 * W_NA + 4 * W_DIL
SPLIT_SIZES = (W_SSM, W_SSM, W_NA, W_NA, W_NA, W_NA, W_DIL, W_DIL, W_DIL, W_DIL)

kernel_name = "hybrid_s5_natten_dilated_encoder"


def rmsnorm(x, g):
    xf = x.astype(jnp.float32)
    y = xf * lax.rsqrt(jnp.mean(xf * xf, axis=-1, keepdims=True) + RMS_EPS)
    return (y * g.astype(jnp.float32)).astype(x.dtype)


def _t5_bucket(rel):
    nb = T5_BUCKETS // 2
    max_exact = nb // 2
    n = np.abs(rel)
    large = max_exact + (np.log(np.maximum(n, 1) / max_exact) / np.log(T5_MAX_DIST / max_exact)
                         * (nb - max_exact)).astype(np.int32)
    large = np.minimum(large, nb - 1)
    return (np.where(rel > 0, nb, 0) + np.where(n < max_exact, n, large)).astype(np.int32)


def _ssm_combine(left, right):
    a_l, b_l = left
    a_r, b_r = right
    return a_r * a_l, a_r * b_l + b_r


def ssm_branch(xa, lam_re, lam_im, log_dt, b_re, b_im, c_re, c_im, d_skip, glu_w, glu_b):
    f32 = jnp.float32
    bsz, s, _ = xa.shape
    u = xa.astype(f32).reshape(bsz, s, SSM_GROUPS, SSM_GROUP)
    y = d_skip.astype(f32).reshape(SSM_GROUPS, SSM_GROUP) * u
    for direction in range(N_DIRS):
        lam = lax.complex(lam_re[direction].astype(f32), lam_im[direction].astype(f32))
        dt = jnp.exp(log_dt[direction].astype(f32))[:, None]
        lam_bar = jnp.exp(lam * dt)
        b = lax.complex(b_re[direction].astype(f32), b_im[direction].astype(f32))
        b_bar = ((lam_bar - 1.0) / lam)[..., None] * b
        bu = jnp.einsum('gpc,bsgc->bsgp', b_bar, u)
        decay = jnp.broadcast_to(lam_bar, bu.shape)
        _, h = lax.associative_scan(_ssm_combine, (decay, bu), axis=1, reverse=(direction == 1))
        c = lax.complex(c_re[direction].astype(f32), c_im[direction].astype(f32))
        y = y + jnp.einsum('gcp,bsgp->bsgc', c, h).real
    y = y.reshape(bsz, s, W_SSM)
    g = jax.nn.gelu(y)
    out = g * jax.nn.sigmoid(g @ glu_w.astype(f32) + glu_b.astype(f32))
    return out.astype(xa.dtype)


def na_branch(q, k, v, rpb):
    bsz, s, _ = q.shape
    rows = s // GRID_W
    wr = min(NA_MAX_ROWS, rows)

    def heads(t):
        return t.reshape(bsz, rows, GRID_W, NA_HEADS, HEAD_DIM).transpose(0, 3, 1, 2, 4)

    q5 = heads(q) * (HEAD_DIM ** -0.5)
    k5 = heads(k)
    v5 = heads(v)
    j = np.arange(GRID_W)
    col_start = np.clip(j - NA_COLS // 2, 0, GRID_W - NA_COLS)
    col_idx = col_start[:, None] + np.arange(NA_COLS)[None, :]
    col_rel = col_idx - j[:, None] + (NA_COLS - 1)
    rpb_cols = rpb[:, :, col_rel]

    def row_fn(r):
        rs = jnp.clip(r - wr // 2, 0, rows - wr)
        qr = lax.dynamic_index_in_dim(q5, r, axis=2, keepdims=False)
        kr = lax.dynamic_slice_in_dim(k5, rs, wr, axis=2)[:, :, :, col_idx]
        vr = lax.dynamic_slice_in_dim(v5, rs, wr, axis=2)[:, :, :, col_idx]
        row_rel = rs + jnp.arange(wr) - r + (NA_MAX_ROWS - 1)
        bias = jnp.take(rpb_cols, row_rel, axis=1).transpose(0, 2, 1, 3)
        logits = jnp.einsum('bhjd,bhrjcd->bhjrc', qr, kr).astype(jnp.float32) + bias.astype(jnp.float32)
        p = jax.nn.softmax(logits.reshape(bsz, NA_HEADS, GRID_W, wr * NA_COLS), axis=-1)
        p = p.reshape(logits.shape).astype(v.dtype)
        return jnp.einsum('bhjrc,bhrjcd->bhjd', p, vr)

    out = lax.map(row_fn, jnp.arange(rows))
    return out.transpose(1, 0, 3, 2, 4).reshape(bsz, s, W_NA)


def dilated_branch(q, k, v, t5_bias):
    bsz, s, _ = q.shape

    def heads(t):
        return t.reshape(bsz, s, DIL_HEADS, HEAD_DIM).transpose(0, 2, 1, 3)

    pad = ((0, 0), (0, 0), (DIL_PAD, DIL_PAD), (0, 0))
    qh = heads(q) * (HEAD_DIM ** -0.5)
    kp = jnp.pad(heads(k), pad)
    vp = jnp.pad(heads(v), pad)
    qi = np.arange(Q_BLOCK)
    patterns = []
    for w, d in DIL_PATTERNS:
        half = (w // 2) // d
        steps = np.arange(-half, half + 1)
        rel = d * steps
        bias = t5_bias[_t5_bucket(rel)].T
        idx = qi[:, None] + d * (steps + half)[None, :]
        patterns.append((d, half, rel, idx, bias))

    def block_fn(s0):
        qb = lax.dynamic_slice_in_dim(qh, s0, Q_BLOCK, axis=2)
        lses, outs = [], []
        for d, half, rel, idx, bias in patterns:
            seg_len = Q_BLOCK + 2 * half * d
            start = s0 + DIL_PAD - half * d
            kg = lax.dynamic_slice_in_dim(kp, start, seg_len, axis=2)[:, :, idx]
            vg = lax.dynamic_slice_in_dim(vp, start, seg_len, axis=2)[:, :, idx]
            logits = jnp.einsum('bhqd,bhqkd->bhqk', qb, kg).astype(jnp.float32) + bias[:, None, :].astype(jnp.float32)
            pos = s0 + qi[:, None] + rel[None, :]
            logits = jnp.where((pos >= 0) & (pos < s), logits, NEG_INF)
            m = jnp.max(logits, axis=-1, keepdims=True)
            p = jnp.exp(logits - m)
            den = jnp.sum(p, axis=-1, keepdims=True)
            outs.append(jnp.einsum('bhqk,bhqkd->bhqd', p, vg.astype(jnp.float32)) / den)
            lses.append(m + jnp.log(den))
        wts = jax.nn.softmax(jnp.stack(lses), axis=0)
        return jnp.sum(wts * jnp.stack(outs), axis=0).astype(q.dtype)

    out = lax.map(block_fn, jnp.arange(s // Q_BLOCK) * Q_BLOCK)
    return out.transpose(1, 0, 3, 2, 4).reshape(bsz, s, W_DIL)


def setup_inputs(seed: int = 0) -> dict:
    key = jax.random.key(seed)
    ks = jax.random.split(key, 20)
    f32 = jnp.float32
    G, P, C = SSM_GROUPS, SSM_STATE, SSM_GROUP
    nrm = lambda k, shape, sc: jax.random.normal(k, shape, f32) * sc
    lam_im_base = jnp.pi * jnp.arange(P, dtype=f32)
    return {
        "x": nrm(ks[0], (BATCH, SEQ, D_MODEL), 1.0),
        "norm_g": 1.0 + nrm(ks[1], (DEPTH, D_MODEL), 0.02),
        "w_in": nrm(ks[2], (DEPTH, D_MODEL, IN_COLS), D_MODEL ** -0.5),
        "w_out": nrm(ks[3], (DEPTH, MIX_WIDTH, D_MODEL), MIX_WIDTH ** -0.5),
        "ssm_lam_re": -0.5 + nrm(ks[4], (DEPTH, N_DIRS, G, P), 0.01),
        "ssm_lam_im": lam_im_base + nrm(ks[5], (DEPTH, N_DIRS, G, P), 0.01),
        "ssm_log_dt": jax.random.uniform(ks[6], (DEPTH, N_DIRS, G), f32, np.log(DT_MIN), np.log(DT_MAX)),
        "ssm_b_re": nrm(ks[7], (DEPTH, N_DIRS, G, P, C), (2 * C) ** -0.5),
        "ssm_b_im": nrm(ks[8], (DEPTH, N_DIRS, G, P, C), (2 * C) ** -0.5),
        "ssm_c_re": nrm(ks[9], (DEPTH, N_DIRS, G, C, P), P ** -0.5),
        "ssm_c_im": nrm(ks[10], (DEPTH, N_DIRS, G, C, P), P ** -0.5),
        "ssm_d": nrm(ks[11], (DEPTH, W_SSM), 1.0),
        "glu_w": nrm(ks[12], (DEPTH, W_SSM, W_SSM), W_SSM ** -0.5),
        "glu_b": nrm(ks[13], (DEPTH, W_SSM), 0.01),
        "na_rpb": nrm(ks[14], (DEPTH, NA_HEADS, 2 * NA_MAX_ROWS - 1, 2 * NA_COLS - 1), 0.1),
        "t5_bias": nrm(ks[15], (T5_BUCKETS, DIL_HEADS), 0.1),
        "final_g": 1.0 + nrm(ks[16], (D_MODEL,), 0.02),
    }


def reference(x, norm_g, w_in, w_out, ssm_lam_re, ssm_lam_im, ssm_log_dt, ssm_b_re, ssm_b_im,
              ssm_c_re, ssm_c_im, ssm_d, glu_w, glu_b, na_rpb, t5_bias, final_g):
    split_points = [int(v) for v in np.cumsum(SPLIT_SIZES)[:-1]]
    for l in range(DEPTH):
        h = rmsnorm(x, norm_g[l])
        proj = h @ w_in[l]
        xa, za, qb, kb, vb, zb, qc, kc, vc, zc = jnp.split(proj, split_points, axis=-1)
        ya = ssm_branch(xa, ssm_lam_re[l], ssm_lam_im[l], ssm_log_dt[l], ssm_b_re[l], ssm_b_im[l],
                        ssm_c_re[l], ssm_c_im[l], ssm_d[l], glu_w[l], glu_b[l]) * jax.nn.silu(za)
        yb = na_branch(qb, kb, vb, na_rpb[l]) * jax.nn.silu(zb)
        yc = dilated_branch(qc, kc, vc, t5_bias) * jax.nn.silu(zc)
        x = x + jnp.concatenate([ya, yb, yc], axis=-1) @ w_out[l]
    return rmsnorm(x, final_g)
```

```python
import numpy as np
from contextlib import ExitStack
import concourse.bass as bass
import concourse.mybir as mybir
from concourse.bass_utils import run_bass_kernel_spmd

F32 = mybir.dt.float32
BF16 = mybir.dt.bfloat16
ALU = mybir.AluOpType
AF = mybir.ActivationFunctionType

S_LEN = 4096
D = 1024
NEG = -30000.0
MAGIC = 12582912.0
TWO_PI = float(2 * np.pi)
NE = 83
ATTACH_WAITS = True


class Buf:
    __slots__ = ("t", "lw", "rd", "name", "psum")

    def __init__(self, t, name="", psum=False):
        self.t = t
        self.lw = {}
        self.rd = {}
        self.name = name
        self.psum = psum

    def __getitem__(self, k):
        return self.t[k]


class Sched:
    NDMA = 56
    NHW = 40

    def __init__(self, nc, es):
        self.nc = nc
        self.es = es
        self.names = ["sp", "pe", "act", "dve", "pool"]
        self.sem = {k: es.enter_context(nc.semaphore("s_" + k)) for k in self.names}
        self.cnt = {k: 0 for k in self.names}
        self.dsem = [es.enter_context(nc.semaphore("d%d" % i)) for i in range(self.NDMA)]
        self.dcnt = [0] * self.NDMA
        self.dnext = {"hw": 0, "sw": self.NHW}
        self.waited = {}
        self.prog = {k: [] for k in self.names}
        self.scope = es
        self.nins = 0
        self.pend = {k: False for k in self.names}
        self.bsem = es.enter_context(nc.semaphore("s_bar"))
        self.bcnt = 0

    def sb(self, name, shape, dt):
        self.uid = getattr(self, "uid", 0) + 1
        name = "sb%d_%s" % (self.uid, name)
        return Buf(self.scope.enter_context(self.nc.sbuf_tensor(name, shape, dt)), name)

    def ps(self, name, shape, dt=F32):
        return Buf(self.es.enter_context(self.nc.psum_tensor(name, shape, dt)), name, True)

    def _wait(self, e, tok):
        if tok is None:
            return
        kind, idx, val = tok
        if kind == "c" and e == "pe" and idx == "pe":
            return
        if kind == "c":
            assert val <= self.cnt[idx], "wait on a not-yet-recorded signalling op (%s waits %s %d > %d)" % (e, idx, val, self.cnt[idx])
        key = (e, kind, idx)
        if self.waited.get(key, 0) >= val:
            return
        self.waited[key] = val
        sem = self.sem[idx] if kind == "c" else self.dsem[idx]
        self.prog[e].append(("w", sem, val))

    def op(self, e, fn, reads=(), writes=(), dma=False, sig=True, par=False, cc=False):
        reads = list(reads)
        writes = list(writes)
        for b in list(reads):
            if b.psum:
                reads.remove(b)
                if b not in writes:
                    writes.append(b)
        need = {}
        for b in reads:
            for k2, val in b.lw.items():
                if need.get(k2, 0) < val:
                    need[k2] = val
        for b in writes:
            for k2, val in b.lw.items():
                if par and dma and k2[0] == "d":
                    continue
                if need.get(k2, 0) < val:
                    need[k2] = val
            for k2, val in b.rd.items():
                if need.get(k2, 0) < val:
                    need[k2] = val
        for (kind, idx), val in need.items():
            self._wait(e, (kind, idx, val))
        if dma:
            kind_ = "sw" if e == "pool" else "hw"
            i = self.dnext[kind_]
            if kind_ == "hw":
                self.dnext[kind_] = (i + 1) % self.NHW
            else:
                self.dnext[kind_] = self.NHW + (i + 1 - self.NHW) % (self.NDMA - self.NHW)
            if self.dcnt[i] > 0:
                self._wait(e, ("d", i, self.dcnt[i]))
            inc = 1 if cc else 16
            self.dcnt[i] += inc
            self.prog[e].append(("o", fn, self.dsem[i], inc))
            tok = ("d", i, self.dcnt[i])
        elif sig:
            self.cnt[e] += 1
            self.prog[e].append(("o", fn, self.sem[e], 1))
            tok = ("c", e, self.cnt[e])
            self.pend[e] = False
        else:
            self.prog[e].append(("n", fn))
            tok = ("c", e, self.cnt[e] + 1)
            self.pend[e] = True
        self.nins += 1
        for b in reads:
            k = (tok[0], tok[1])
            if b.rd.get(k, 0) < tok[2]:
                b.rd[k] = tok[2]
        for b in writes:
            if par and dma:
                b.lw = {k: v for k, v in b.lw.items() if k[0] == "d"}
                b.lw[(tok[0], tok[1])] = tok[2]
            else:
                b.lw = {(tok[0], tok[1]): tok[2]}
            b.rd = {}
        return tok

    def barrier(self):
        assert not any(self.pend.values()), self.pend
        for i in range(self.NDMA):
            if self.dcnt[i] > 0:
                self._wait("sp", ("d", i, self.dcnt[i]))
        self.bcnt += 1
        self.prog["sp"].append(("s", self.bsem, 1))
        for e in self.names:
            for e2 in self.names:
                if e2 != e and self.cnt[e2] > 0:
                    self._wait(e, ("c", e2, self.cnt[e2]))
            if e != "sp":
                self.prog[e].append(("w", self.bsem, self.bcnt))
                for i in range(self.NDMA):
                    if self.dcnt[i] > 0:
                        self.waited[(e, "d", i)] = max(self.waited.get((e, "d", i), 0), self.dcnt[i])

    def flush(self):
        self.barrier()
        prog = self.prog
        self.prog = {k: [] for k in self.names}

        def mk(name):
            def body(eng):
                items = prog[name]
                n = len(items)
                i = 0
                while i < n:
                    it = items[i]
                    if it[0] == "w":
                        nxt = items[i + 1] if i + 1 < n else None
                        if nxt is not None and nxt[0] in ("o", "n") and ATTACH_WAITS:
                            ins = nxt[1](eng)
                            ins._wait_ge(it[1], it[2])
                            if nxt[0] == "o":
                                ins.then_inc(nxt[2], nxt[3])
                            i += 2
                            continue
                        eng.wait_ge(it[1], it[2])
                    elif it[0] == "n":
                        it[1](eng)
                    elif it[0] == "s":
                        eng.sem_inc(it[1], it[2])
                    else:
                        it[1](eng).then_inc(it[2], it[3])
                    i += 1
            return body

        with self.nc.Block() as block:
            block.sync(mk("sp"))
            block.tensor(mk("pe"))
            block.scalar(mk("act"))
            block.vector(mk("dve"))
            block.gpsimd(mk("pool"))


def _t5_bucket(rel):
    nb = 16
    max_exact = 8
    n = np.abs(rel)
    large = max_exact + (np.log(np.maximum(n, 1) / max_exact) / np.log(1024 / max_exact) * (nb - max_exact)).astype(np.int32)
    large = np.minimum(large, nb - 1)
    return (np.where(rel > 0, nb, 0) + np.where(n < max_exact, n, large)).astype(np.int32)


def _gather_dil(t5_bias):
    out = np.full((8, 128, 3, 256), NEG, np.float32)
    kk = np.arange(128)[:, None]
    cc = np.arange(256)[None, :]
    rel = kk + 64 - cc
    valid = np.abs(rel) <= 64
    for pi, d in enumerate((1, 4, 16)):
        bkt = _t5_bucket(d * rel)
        for h in range(8):
            vals = t5_bias[bkt, h]
            out[h, :, pi, :] = np.where(valid, vals, np.float32(NEG))
    return np.ascontiguousarray(out.reshape(8, 128, 768))


def _rs(r):
    return min(max(r - 4, 0), 56)


NA_SBREP = [0, 3, 7]


def _na_slot(tab, slot):
    sb = NA_SBREP[tab]
    m = 4 * sb - 2 + slot
    if m < 0 or m > 31:
        return None
    rows = [rl for rl in range(8) if _rs(8 * sb + rl) <= 2 * m + 1 and _rs(8 * sb + rl) + 7 >= 2 * m]
    if not rows:
        return None
    assert rows == list(range(rows[0], rows[-1] + 1))
    return rows[0], rows[-1]


def _gather_na(rpb):
    out = np.full((8, 3, 128, 8, 8, 64), NEG, np.float32)
    kk = np.arange(128)
    j = np.arange(64)
    cs = np.clip(j - 8, 0, 48)
    kc = kk % 64
    colok = (kc[:, None] >= cs[None, :]) & (kc[:, None] <= cs[None, :] + 15)
    dc = np.clip(kc[:, None] - j[None, :] + 15, 0, 30)
    for tab in range(3):
        sb = NA_SBREP[tab]
        for slot in range(8):
            rr = _na_slot(tab, slot)
            if rr is None:
                continue
            m = 4 * sb - 2 + slot
            kr = 2 * m + kk // 64
            for rl in range(rr[0], rr[1] + 1):
                r = 8 * sb + rl
                rs = _rs(r)
                valid = ((kr >= rs) & (kr <= rs + 7))[:, None] & colok
                dr = np.clip(kr - r + 7, 0, 14)
                for h in range(8):
                    vals = rpb[h][dr[:, None], dc]
                    out[h, tab, :, slot, rl, :] = np.where(valid, vals, np.float32(NEG))
    return np.ascontiguousarray(out.reshape(8, 3, 128, 4096))


def _etab():
    e = np.zeros((128, NE), np.float32)
    for dirn in range(2):
        rows = slice(dirn * 64, dirn * 64 + 64)
        sg = -1.0 if dirn == 0 else 1.0
        for s in range(8):
            e[rows, s] = sg * s
        for m in range(5):
            for t in range(8):
                e[rows, 8 + m * 8 + t] = 8 * m + (t if dirn == 0 else -t)
        for jj in range(4):
            for s in range(8):
                e[rows, 48 + jj * 8 + s] = (32 - 8 * jj - s) if dirn == 0 else (8 * jj + s)
        e[rows, 80] = 1.0
        e[rows, 81] = 32.0
        e[rows, 82] = 64.0
    return e


def _masks():
    sp = np.arange(128) // 16
    mge = (sp[None, :] >= sp[:, None]).astype(np.float32)
    mle = (sp[None, :] <= sp[:, None]).astype(np.float32)
    return mge, mle


def build_program(nlayers=2, dbg=None, do_na=True, do_dil=True, do_ssm=True, npairs=4):
    nc = bass.Bass("TRN2", target_bir_lowering=False)

    def din(name, shape, dt=F32):
        return nc.dram_tensor(name, list(shape), dt, kind="ExternalInput").ap()

    x_d = din("x", [S_LEN, D])
    normg_d = din("norm_g", [2, D])
    finalg_d = din("final_g", [D])
    win_d = din("w_in", [2, D, 2560])
    rgroups = [[2 * i, 2 * i + 1] for i in range(npairs)]
    XA0, ZA0, NAQ, NAK, NAV, NAZ, DQ, DK, DV, DZ = 0, 256, 512, 768, 1024, 1280, 1536, 1792, 2048, 2304
    wout_d = din("w_out", [2, 1536, D])
    ident_d = din("ident", [128, 128])
    mge_d = din("mge", [128, 128])
    mle_d = din("mle", [128, 128])
    etab_d = din("etab", [128, NE])
    lre_d = din("lam_re_t", [2, 128, 16])
    lim_d = din("lam_im_t", [2, 128, 16])
    ldt_d = din("logdt_t", [2, 128, 16])
    bre_d = din("b_re_t", [2, 128, 256])
    bim_d = din("b_im_t", [2, 128, 256])
    cre_d = din("c_re_t", [2, 128, 256])
    cim_d = din("c_im_t", [2, 128, 256])
    dsk_d = din("dsk_t", [2, 128, 16])
    gluw_d = din("glu_w", [2, 512, 256])
    glub_d = din("glub_t", [2, 128, 2])
    gna_d = din("gna", [2, 4, 3, 128, 4096])
    gdil_d = din("gdil", [4, 128, 768])
    out_d = nc.dram_tensor("out", [S_LEN, D], F32, kind="ExternalOutput").ap()
    sk = "ExternalOutput" if dbg else "Internal"
    x1_d = nc.dram_tensor("x1s", [S_LEN, D], F32, kind=sk).ap()
    ysc_t = [nc.dram_tensor("ysc%d" % i, [128, S_LEN], BF16, kind="Internal").ap() for i in range(6)]
    ygat_t = [nc.dram_tensor("ygat%d" % i, [256, S_LEN], BF16, kind="Internal", addr_space="Local").ap() for i in range(6)]
    usc_d = nc.dram_tensor("usc", [2, 8, 16, 8, 512], BF16, kind="Internal").ap()
    ys2_t = [nc.dram_tensor("ys2_%d" % i, [128, S_LEN], BF16, kind="Internal").ap() for i in range(2)]
    ys2g_t = [nc.dram_tensor("ys2g%d" % i, [256, S_LEN], BF16, kind="Internal", addr_space="Local").ap() for i in range(2)]
    zsc_d = nc.dram_tensor("zsc", [2, 128, S_LEN], BF16, kind="Internal").ap()
    yscB = [Buf(None, "yscB%d" % i) for i in range(6)]
    ygatB = [Buf(None, "ygatB%d" % i) for i in range(6)]
    ys2B = [Buf(None, "ys2B%d" % i) for i in range(2)]
    ys2gB = [Buf(None, "ys2gB%d" % i) for i in range(2)]

    with ExitStack() as es:
        S = Sched(nc, es)
        PS = [S.ps("ps%d" % i, [128, 512], F32) for i in range(7)]
        PSB = S.ps("psb", [128, 1024], BF16)
        identf = S.sb("identf", [128, 128], F32)
        identb = S.sb("identb", [128, 128], BF16)
        onesb = S.sb("onesb", [128, 128], BF16)
        zerosb = S.sb("zerosb", [128, 128], BF16)
        mge = S.sb("mge", [128, 128], F32)
        mle = S.sb("mle", [128, 128], F32)
        wst = [S.sb("wst%d" % i, [128, 8 * 128], F32) for i in range(2)]
        wbb = [S.sb("wbb%d" % i, [128, 8 * 128], BF16) for i in range(2)]
        state = {"wi": 0, "ps": 0, "ev": 0}

        def dma(out, in_, reads=(), writes=(), q="sp", par=False):
            return S.op(q, lambda e: e.dma_start(out=out, in_=in_), reads=reads, writes=writes, dma=True, par=par)

        def allgather(src_ap, dst_ap, srcB, dstB):
            return S.op("pool", lambda e: e.collective_compute("AllGather", ALU.bypass, replica_groups=rgroups, ins=[src_ap], outs=[dst_ap]),
                        reads=[srcB], writes=[dstB], dma=True, cc=True)

        dma(identf[:], ident_d, writes=[identf])
        dma(mge[:], mge_d, writes=[mge])
        dma(mle[:], mle_d, writes=[mle])
        S.op("dve", lambda e: e.tensor_copy(out=identb[:], in_=identf[:]), reads=[identf], writes=[identb])
        S.op("dve", lambda e: e.memset(onesb[:], 1.0), writes=[onesb])
        S.op("dve", lambda e: e.memset(zerosb[:], 0.0), writes=[zerosb])

        def emit_pipeline(tasks, depth=2):
            queue = []
            for t in tasks:
                if t["init"] is not None:
                    t["init"]()
                t["S"]()
                t["E"]()
                queue.append(t)
                if len(queue) > depth:
                    p = queue.pop(0)
                    p["PV"]()
                    if p["fin"] is not None:
                        p["fin"]()
            for p in queue:
                p["PV"]()
                if p["fin"] is not None:
                    p["fin"]()

        wseq = []
        for l_ in range(nlayers):
            if do_na:
                for hp_ in range(2):
                    wseq += [(l_, NAQ + hp_ * 128), (l_, NAK + hp_ * 128), (l_, NAZ + hp_ * 128), (l_, NAV + hp_ * 128)]
            if do_dil:
                for hp_ in range(2):
                    wseq += [(l_, DQ + hp_ * 128), (l_, DK + hp_ * 128), (l_, DZ + hp_ * 128), (l_, DV + hp_ * 128)]
            if do_ssm:
                for ct_ in range(2):
                    wseq += [(l_, XA0 + ct_ * 128), (l_, ZA0 + ct_ * 128)]
        state["wptr"] = 0
        state["wissued"] = 0

        state["pcast"] = []

        def issue_w(k):
            l_, col0 = wseq[k]
            i = k % 2
            src = win_d[l_, :, col0:col0 + 128].rearrange("(j p) c -> p j c", p=128)
            dma(wst[i][:].rearrange("p (j c) -> p j c", j=8), src, writes=[wst[i]])
            state["pcast"].append(k)

        def do_cast():
            while state["pcast"]:
                k = state["pcast"].pop(0)
                i = k % 2
                if k % 2 == 0:
                    S.op("dve", lambda e, i=i: e.tensor_copy(out=wbb[i][:], in_=wst[i][:]), reads=[wst[i]], writes=[wbb[i]])
                else:
                    S.op("act", lambda e, i=i: e.activation(out=wbb[i][:], in_=wst[i][:], func=AF.Copy), reads=[wst[i]], writes=[wbb[i]])

        def load_w(l, col0):
            k = state["wptr"]
            assert wseq[k] == (l, col0), (wseq[k], l, col0)
            while state["wissued"] <= k:
                issue_w(state["wissued"])
                state["wissued"] += 1
            do_cast()
            if state["wissued"] <= min(k + 1, len(wseq) - 1):
                issue_w(state["wissued"])
                state["wissued"] += 1
            state["wptr"] = k + 1
            return wbb[k % 2]

        def evac_copy(out_ap, in_ap, rd, wr, scale=None):
            state["ev"] ^= 1
            if scale is not None or state["ev"]:
                sc = 1.0 if scale is None else scale
                S.op("act", lambda e: e.activation(out=out_ap, in_=in_ap, func=AF.Copy, scale=sc), reads=rd, writes=wr)
            else:
                S.op("dve", lambda e: e.tensor_copy(out=out_ap, in_=in_ap), reads=rd, writes=wr)

        for l in range(nlayers):
            xsrc = x_d if l == 0 else x1_d
            last = (l == nlayers - 1)
            with ExitStack() as scH:
                S.scope = scH
                hT = S.sb("hT", [128, 8 * S_LEN], BF16)
                hT3 = hT[:].rearrange("p (j t) -> p j t", j=8)

                with ExitStack() as ph:
                    S.scope = ph
                    gbc = S.sb("gbc", [128, D], F32)
                    xts = [S.sb("xt%d" % i, [128, D], F32) for i in range(4)]
                    junk = S.sb("junk", [128, D], BF16)
                    hbs = [S.sb("hb%d" % i, [128, D], BF16) for i in range(2)]
                    sts = [S.sb("st%d" % i, [128, 4 * 8], F32) for i in range(4)]
                    dma(gbc[:], normg_d[l].partition_broadcast(128), writes=[gbc])
                    for st_ in sts:
                        S.op("dve", lambda e, st_=st_: e.memset(st_[:], 0.0), writes=[st_])
                    for tt in range(32):
                        xt = xts[tt % 4]
                        hb = hbs[tt % 2]
                        st = sts[tt % 4]
                        c0 = (tt // 4) * 4
                        dma(xt[:], xsrc[tt * 128:(tt + 1) * 128, :], writes=[xt], q="sp")
                        S.op("act", lambda e, xt=xt, c0=c0, st=st: e.activation(out=junk[:], in_=xt[:], func=AF.Square, accum_out=st[:, c0:c0 + 1]),
                             reads=[xt], writes=[junk, st])
                        S.op("dve", lambda e, c0=c0, st=st: e.tensor_scalar(out=st[:, c0 + 1:c0 + 2], in0=st[:, c0:c0 + 1], scalar1=1.0 / D, scalar2=1e-6,
                                                                    op0=ALU.mult, op1=ALU.add), reads=[st], writes=[st])
                        S.op("act", lambda e, c0=c0, st=st: e.activation(out=st[:, c0 + 2:c0 + 3], in_=st[:, c0 + 1:c0 + 2], func=AF.Sqrt), reads=[st], writes=[st])
                        S.op("dve", lambda e, c0=c0, st=st: e.reciprocal(out=st[:, c0 + 3:c0 + 4], in_=st[:, c0 + 2:c0 + 3]), reads=[st], writes=[st])
                        S.op("dve", lambda e, xt=xt, hb=hb, c0=c0, st=st: e.scalar_tensor_tensor(out=hb[:], in0=xt[:], scalar=st[:, c0 + 3:c0 + 4], in1=gbc[:],
                                                                                          op0=ALU.mult, op1=ALU.mult), reads=[xt, st, gbc], writes=[hb])
                        for j in range(8):
                            S.op("pe", lambda e, hb=hb, j=j: e.transpose(out=PSB[:, j * 128:(j + 1) * 128], in_=hb[:, j * 128:(j + 1) * 128], identity=identb[:]),
                                 reads=[hb, identb], writes=[PSB], sig=(j == 7))
                        evac_copy(hT3[:, :, tt * 128:(tt + 1) * 128], PSB[:].rearrange("p (j t) -> p j t", j=8), [PSB], [hT])
                    S.flush()
                S.scope = scH

                def proj_fm(col0, evac):
                    wb = load_w(l, col0)
                    wb3 = wb[:].rearrange("p (j c) -> p j c", j=8)
                    for n in range(8):
                        ps = PS[5 + (n % 2)]
                        for j in range(8):
                            S.op("pe", lambda e, ps=ps, j=j, n=n: e.matmul(ps[:, 0:512], lhsT=wb3[:, j, :], rhs=hT3[:, j, n * 512:(n + 1) * 512],
                                                                           start=(j == 0), stop=(j == 7)), reads=[wb, hT], writes=[ps], sig=(j == 7))
                        evac(n, ps)
                        if n == 4:
                            do_cast()

                def tiles_from_fm(vT, tiles, vbuf, v3):
                    for g8 in range(0, len(tiles), 8):
                        for q in range(8):
                            tsl = tiles[g8 + q]
                            S.op("pe", lambda e, q=q, tsl=tsl: e.transpose(out=PSB[:, q * 128:(q + 1) * 128], in_=vT[:, tsl], identity=identb[:]),
                                 reads=[vT, identb], writes=[PSB], sig=(q == 7))
                        evac_copy(v3[:, g8:g8 + 8, :], PSB[:].rearrange("p (q c) -> p q c", q=8), [PSB], [vbuf])

                def proj_tm(col0, tiles, vbuf, v3, wb=None):
                    if wb is None:
                        wb = load_w(l, col0)
                    wb3 = wb[:].rearrange("p (j c) -> p j c", j=8)
                    for g4 in range(0, len(tiles), 4):
                        ps = PS[5 + ((g4 // 4) % 2)]
                        for q in range(4):
                            tsl = tiles[g4 + q]
                            for j in range(8):
                                S.op("pe", lambda e, ps=ps, j=j, q=q, tsl=tsl: e.matmul(ps[:, q * 128:(q + 1) * 128], lhsT=hT3[:, j, tsl], rhs=wb3[:, j, :],
                                                                                       start=(j == 0), stop=(j == 7)), reads=[wb, hT], writes=[ps], sig=(j == 7 and q == 3))
                        evac_copy(v3[:, g4:g4 + 4, :], ps[:, 0:512].rearrange("p (q c) -> p q c", q=4), [ps], [vbuf])
                        if g4 == 16:
                            do_cast()

                def attn_common(ph, colq, colk, colz, split_q=False):
                    kT = S.sb("kT", [128, S_LEN], BF16)
                    zT = S.sb("zT", [128, S_LEN], BF16)
                    yT = S.sb("yT", [128, S_LEN], BF16)
                    if split_q:
                        qz = [S.sb("qz%d" % i, [128, S_LEN], BF16) for i in range(2)]
                        S.op("pool", lambda e: e.memset(qz[0][64:128, :], 0.0), writes=[qz[0]])
                        S.op("pool", lambda e: e.memset(qz[1][0:64, :], 0.0), writes=[qz[1]])

                        def evq(n, ps):
                            for hh_ in range(2):
                                rs_ = slice(64 * hh_, 64 * hh_ + 64)
                                S.op("act", lambda e, hh_=hh_, rs_=rs_: e.activation(out=qz[hh_][rs_, n * 512:(n + 1) * 512], in_=ps[rs_, 0:512], func=AF.Copy, scale=0.125),
                                     reads=[ps], writes=[qz[hh_]])
                        proj_fm(colq, evq)
                        qT = qz
                    else:
                        qT = S.sb("qT", [128, S_LEN], BF16)
                        proj_fm(colq, lambda n, ps: S.op("act", lambda e: e.activation(out=qT[:, n * 512:(n + 1) * 512], in_=ps[:, 0:512], func=AF.Copy, scale=0.125),
                                                         reads=[ps], writes=[qT]))
                    proj_fm(colk, lambda n, ps: evac_copy(kT[:, n * 512:(n + 1) * 512], ps[:, 0:512], [ps], [kT]))
                    proj_fm(colz, lambda n, ps: S.op("act", lambda e: e.activation(out=zT[:, n * 512:(n + 1) * 512], in_=ps[:, 0:512], func=AF.Silu),
                                                     reads=[ps], writes=[zT]))
                    return qT, kT, zT, yT

                if do_na:
                    for hp in range(2):
                        with ExitStack() as ph:
                            S.scope = ph
                            gsts = [S.sb("gst%d" % i, [128, 4096], F32) for i in range(2)]
                            nbt = [[S.sb("nb%d_%d" % (hh_, t_), [128, 4096], BF16) for t_ in range(3)] for hh_ in range(2)]
                            gorder = [(0, 0), (0, 1), (0, 2), (1, 0), (1, 1), (1, 2)]
                            for i_ in range(2):
                                dma(gsts[i_][:], gna_d[l, hp * 2 + gorder[i_][0], gorder[i_][1]], writes=[gsts[i_]], q="act")
                            qT, kT, zT, yT = attn_common(ph, NAQ + hp * 128, NAK + hp * 128, NAZ + hp * 128)
                            vN = S.sb("vN", [128, 32 * 128], BF16)
                            vN3 = vN[:].rearrange("p (t c) -> p t c", t=32)
                            vT = yT
                            proj_fm(NAV + hp * 128, lambda n, ps: evac_copy(vT[:, n * 512:(n + 1) * 512], ps[:, 0:512], [ps], [vT]))
                            tiles_from_fm(vT, [slice(t * 128, (t + 1) * 128) for t in range(32)], vN, vN3)
                            PT = [S.sb("pt%d" % i, [128, 512], BF16) for i in range(3)]
                            SCB = [PS[0], PS[1], PS[6]]
                            dsb = S.sb("dsb", [128, 512], F32)
                            tnum = S.sb("tnum", [128, 512], F32)
                            for i_, (hh, t_) in enumerate(gorder):
                                gst = gsts[i_ % 2]
                                if i_ >= 2:
                                    dma(gst[:], gna_d[l, hp * 2 + hh, t_], writes=[gst], q="act")
                                S.op("dve", lambda e, hh=hh, t_=t_, gst=gst: e.tensor_copy(out=nbt[hh][t_][:], in_=gst[:]), reads=[gst], writes=[nbt[hh][t_]])
                            tcount = 0
                            tasks = []
                            for hh in range(2):
                                sl = slice(64 * hh, 64 * hh + 64)
                                for sb_ in range(8):
                                    tab = 0 if sb_ == 0 else (2 if sb_ == 7 else 1)
                                    nb = nbt[hh][tab]
                                    psn = PS[2 + sb_ % 2]
                                    psd = PS[4 + sb_ % 2]

                                    def f_init(psn=psn, psd=psd, nb=nb):
                                        for dst in (psn, psd):
                                            S.op("pe", lambda e, dst=dst, nb=nb: e.matmul(dst[:, 0:512], lhsT=zerosb[:], rhs=nb[:, 0:512], start=True, stop=False),
                                                 reads=[zerosb, nb], writes=[dst], sig=False)

                                    def f_fin(psn=psn, psd=psd, sl=sl, tok=slice(sb_ * 512, sb_ * 512 + 512)):
                                        S.op("act", lambda e: e.activation(out=dsb[sl, :], in_=psd[sl, :], func=AF.Copy), reads=[psd], writes=[dsb])
                                        S.op("dve", lambda e: e.reciprocal(out=dsb[sl, :], in_=dsb[sl, :]), reads=[dsb], writes=[dsb])
                                        S.op("dve", lambda e: e.tensor_tensor(out=tnum[sl, :], in0=psn[sl, :], in1=dsb[sl, :], op=ALU.mult), reads=[psn, dsb], writes=[tnum])
                                        S.op("pool", lambda e: e.tensor_tensor(out=yT[sl, tok], in0=tnum[sl, :], in1=zT[sl, tok], op=ALU.mult), reads=[tnum, zT], writes=[yT])

                                    slots = [(slot, _na_slot(tab, slot)) for slot in range(8)]
                                    slots = [(slot, rr) for slot, rr in slots if rr is not None]
                                    for si, (slot, (ra, rb)) in enumerate(slots):
                                        m = 4 * sb_ - 2 + slot
                                        c0 = ra * 64
                                        N = (rb - ra + 1) * 64
                                        q0 = 512 * sb_ + c0
                                        pss = SCB[tcount % 3]
                                        pt = PT[tcount % 3]
                                        tcount += 1
                                        lastt = (si == len(slots) - 1)

                                        def f_S(pss=pss, nb=nb, slot=slot, c0=c0, N=N, m=m, q0=q0, sl=sl):
                                            S.op("pe", lambda e: e.matmul(pss[:, 0:N], lhsT=identb[:], rhs=nb[:, slot * 512 + c0:slot * 512 + c0 + N], start=True, stop=False),
                                                 reads=[identb, nb], writes=[pss], sig=False)
                                            S.op("pe", lambda e: e.matmul(pss[:, 0:N], lhsT=kT[sl, m * 128:(m + 1) * 128], rhs=qT[sl, q0:q0 + N], start=False, stop=True),
                                                 reads=[kT, qT], writes=[pss])

                                        def f_E(pss=pss, pt=pt, N=N):
                                            S.op("act", lambda e: e.activation(out=pt[:, 0:N], in_=pss[:, 0:N], func=AF.Exp), reads=[pss], writes=[pt])

                                        def f_PV(psn=psn, psd=psd, pt=pt, m=m, c0=c0, N=N, lastt=lastt):
                                            S.op("pe", lambda e: e.matmul(psn[:, c0:c0 + N], lhsT=vN3[:, m, :], rhs=pt[:, 0:N], start=False, stop=lastt),
                                                 reads=[vN, pt], writes=[psn], sig=False)
                                            S.op("pe", lambda e: e.matmul(psd[:, c0:c0 + N], lhsT=onesb[:], rhs=pt[:, 0:N], start=False, stop=lastt),
                                                 reads=[onesb, pt], writes=[psd], sig=True)

                                        tasks.append({"init": f_init if si == 0 else None, "S": f_S, "E": f_E, "PV": f_PV, "fin": f_fin if lastt else None})
                            emit_pipeline(tasks)
                            dma(ysc_t[2 + hp], yT[:], reads=[yT], writes=[yscB[2 + hp]])
                            allgather(ysc_t[2 + hp], ygat_t[2 + hp], yscB[2 + hp], ygatB[2 + hp])
                            S.flush()
                        S.scope = scH

                if do_dil:
                    for hp in range(2):
                        with ExitStack() as ph:
                            S.scope = ph
                            qT, kT, zT, yT = attn_common(ph, DQ + hp * 128, DK + hp * 128, DZ + hp * 128, split_q=True)
                            vD, vD3 = [], []
                            vT = yT
                            proj_fm(DV + hp * 128, lambda n, ps: evac_copy(vT[:, n * 512:(n + 1) * 512], ps[:, 0:512], [ps], [vT]))
                            for pi, d in enumerate((1, 4, 16)):
                                vb = S.sb("vD%d" % pi, [128, 32 * 128], BF16)
                                v3 = vb[:].rearrange("p (t c) -> p t c", t=32)
                                nsub = 32 // d
                                tiles = []
                                for r in range(d):
                                    for m in range(nsub):
                                        s0 = r + d * 128 * m
                                        tiles.append(slice(s0, s0 + d * 127 + 1, d))
                                tiles_from_fm(vT, tiles, vb, v3)
                                vD.append(vb)
                                vD3.append(v3)
                            gst = S.sb("gstd", [128, 768], F32)
                            dbs = [S.sb("db%d" % i, [128, 768], BF16) for i in range(2)]
                            PT = [S.sb("ptd%d" % i, [128, 256], BF16) for i in range(3)]
                            SCB = [PS[0], PS[1], PS[6]]
                            accn = S.sb("accn", [128, S_LEN], F32)
                            accd = S.sb("accd", [128, S_LEN], F32)
                            for hh in range(2):
                                dma(gst[:], gdil_d[hp * 2 + hh], writes=[gst])
                                S.op("act", lambda e, hh=hh: e.activation(out=dbs[hh][:], in_=gst[:], func=AF.Exp), reads=[gst], writes=[dbs[hh]])
                            combo = 0
                            tasks = []
                            for hh in range(2):
                                db = dbs[hh]
                                sl = slice(64 * hh, 64 * hh + 64)
                                nseg_total = 8 + 8 + 16
                                segc = 0
                                for pi, d in enumerate((1, 4, 16)):
                                    n = S_LEN // d
                                    L = min(512, n)
                                    nsub = n // 128
                                    segi = 0
                                    for r in range(d):
                                        for s0 in range(0, n, L):
                                            psn = PS[2 + segi % 2]
                                            psd = PS[4 + segi % 2]
                                            segi += 1
                                            segc += 1
                                            last_seg_of_head = (segc == nseg_total)

                                            def f_init(psn=psn, psd=psd, db=db, L=L):
                                                for dst in (psn, psd):
                                                    S.op("pe", lambda e, dst=dst: e.matmul(dst[:, 0:L], lhsT=zerosb[:], rhs=db[:, 0:L], start=True, stop=False),
                                                         reads=[zerosb, db], writes=[dst], sig=False)

                                            def f_fin(psn=psn, psd=psd, sl=sl, pi=pi, L=L, d=d, t0_=r + d * s0, last_seg_of_head=last_seg_of_head):
                                                for acc, psrc in ((accn, psn), (accd, psd)):
                                                    view = acc[sl, t0_:t0_ + d * (L - 1) + 1:d]
                                                    pview = psrc[sl, 0:L]
                                                    if pi == 0:
                                                        evac_copy(view, pview, [psrc], [acc])
                                                    else:
                                                        S.op("dve", lambda e, view=view, pview=pview: e.tensor_tensor(out=view, in0=view, in1=pview, op=ALU.add),
                                                             reads=[psrc, acc], writes=[acc])
                                                if pi == 2:
                                                    tsl = slice(t0_, t0_ + d * (L - 1) + 1, d)
                                                    S.op("dve", lambda e: e.reciprocal(out=accd[sl, tsl], in_=accd[sl, tsl]), reads=[accd], writes=[accd])
                                                    S.op("dve", lambda e: e.tensor_tensor(out=accn[sl, tsl], in0=accn[sl, tsl], in1=accd[sl, tsl], op=ALU.mult), reads=[accn, accd], writes=[accn])
                                                    S.op("pool", lambda e: e.tensor_tensor(out=yT[sl, tsl], in0=accn[sl, tsl], in1=zT[sl, tsl], op=ALU.mult), reads=[accn, zT], writes=[yT])

                                            ms = [m for m in range(nsub) if max(s0, 128 * m - 64) < min(s0 + L, 128 * m + 192)]
                                            for mi, m in enumerate(ms):
                                                qa = max(s0, 128 * m - 64)
                                                qb = min(s0 + L, 128 * m + 192)
                                                N = qb - qa
                                                cb = qa - (128 * m - 64)
                                                ks = slice(r + d * 128 * m, r + d * (128 * m + 127) + 1, d)
                                                qs = slice(r + d * qa, r + d * (qb - 1) + 1, d)
                                                pss = SCB[combo % 3]
                                                pt = PT[combo % 3]
                                                combo += 1
                                                lastt = (mi == len(ms) - 1)
                                                o0 = qa - s0
                                                tix = r * nsub + m

                                                def f_S(pss=pss, N=N, ks=ks, qs=qs, qh=qT[hh]):
                                                    S.op("pe", lambda e: e.matmul(pss[:, 0:N], lhsT=kT[:, ks], rhs=qh[:, qs], start=True, stop=True), reads=[kT, qh], writes=[pss])

                                                def f_E(pss=pss, pt=pt, N=N, db=db, pi=pi, cb=cb, meng=("pool" if combo % 2 == 0 else "dve")):
                                                    S.op("act", lambda e: e.activation(out=pt[:, 0:N], in_=pss[:, 0:N], func=AF.Exp), reads=[pss], writes=[pt])
                                                    S.op(meng, lambda e: e.tensor_tensor(out=pt[:, 0:N], in0=pt[:, 0:N], in1=db[:, pi * 256 + cb:pi * 256 + cb + N], op=ALU.mult),
                                                         reads=[db], writes=[pt])

                                                def f_PV(psn=psn, psd=psd, pt=pt, tix=tix, pi=pi, o0=o0, N=N, lastt=lastt):
                                                    S.op("pe", lambda e: e.matmul(psn[:, o0:o0 + N], lhsT=vD3[pi][:, tix, :], rhs=pt[:, 0:N], start=False, stop=lastt),
                                                         reads=[vD[pi], pt], writes=[psn], sig=False)
                                                    S.op("pe", lambda e: e.matmul(psd[:, o0:o0 + N], lhsT=onesb[:], rhs=pt[:, 0:N], start=False, stop=lastt),
                                                         reads=[onesb, pt], writes=[psd], sig=True)

                                                tasks.append({"init": f_init if mi == 0 else None, "S": f_S, "E": f_E, "PV": f_PV, "fin": f_fin if lastt else None})
                            emit_pipeline(tasks)
                            dma(ysc_t[4 + hp], yT[:], reads=[yT], writes=[yscB[4 + hp]])
                            allgather(ysc_t[4 + hp], ygat_t[4 + hp], yscB[4 + hp], ygatB[4 + hp])
                            S.flush()
                        S.scope = scH

                if do_ssm:
                    with ExitStack() as ph:
                        S.scope = ph
                        ups = [S.sb("up%d" % i, [128, 8 * 512], BF16) for i in range(2)]
                        zts = [S.sb("zt%d" % i, [128, S_LEN], BF16) for i in range(2)]
                        for ct in range(2):
                            up = ups[ct % 2]
                            up3 = up[:].rearrange("p (s m) -> p s m", s=8)
                            proj_fm(XA0 + ct * 128, lambda n, ps, up=up, up3=up3: evac_copy(up3[:, :, n * 64:(n + 1) * 64],
                                                                                   ps[:, 0:512].rearrange("p (m s) -> p s m", s=8), [ps], [up]))
                            dma(usc_d[ct].rearrange("g c s m -> (g c) s m"), up3, reads=[up])
                            zt = zts[ct % 2]
                            proj_fm(ZA0 + ct * 128, lambda n, ps, zt=zt: S.op("act", lambda e: e.activation(out=zt[:, n * 512:(n + 1) * 512], in_=ps[:, 0:512], func=AF.Silu),
                                                                              reads=[ps], writes=[zt]))
                            dma(zsc_d[ct], zt[:], reads=[zt])
                        S.flush()
                    S.scope = scH
            S.scope = es

            if do_ssm:
                with ExitStack() as scS:
                    S.scope = scS
                    lre = S.sb("lre", [128, 16], F32)
                    lim = S.sb("lim", [128, 16], F32)
                    ldt = S.sb("ldt", [128, 16], F32)
                    ar = S.sb("ar", [128, 16], F32)
                    ai = S.sb("ai", [128, 16], F32)
                    bre = S.sb("bre", [128, 256], F32)
                    bim = S.sb("bim", [128, 256], F32)
                    cre = S.sb("cre", [128, 256], F32)
                    cim = S.sb("cim", [128, 256], F32)
                    dsk = S.sb("dsk", [128, 16], F32)
                    etab = S.sb("etab", [128, NE], F32)
                    for t, d_ in ((lre, lre_d), (lim, lim_d), (ldt, ldt_d), (bre, bre_d), (bim, bim_d), (cre, cre_d), (cim, cim_d), (dsk, dsk_d)):
                        dma(t[:], d_[l], writes=[t])
                    dma(etab[:], etab_d, writes=[etab])
                    S.op("act", lambda e: e.activation(out=ldt[:], in_=ldt[:], func=AF.Exp), reads=[ldt], writes=[ldt])
                    S.op("dve", lambda e: e.tensor_tensor(out=ar[:], in0=lre[:], in1=ldt[:], op=ALU.mult), reads=[lre, ldt], writes=[ar])
                    S.op("dve", lambda e: e.tensor_tensor(out=ai[:], in0=lim[:], in1=ldt[:], op=ALU.mult), reads=[lim, ldt], writes=[ai])
                    YPs = []
                    for half in range(1):
                        G0 = 0
                        gs = slice(G0, G0 + 16)
                        with ExitStack() as scHf:
                            S.scope = scHf
                            NW = 16 * NE
                            zr = S.sb("zr", [128, NW], F32)
                            zi = S.sb("zi", [128, NW], F32)
                            scT = ExitStack()
                            S.scope = scT
                            tA = S.sb("tA", [128, NW], F32)
                            tB = S.sb("tB", [128, NW], F32)
                            tC = S.sb("tC", [128, NW], F32)
                            tD = S.sb("tD", [128, NW], F32)
                            tI = S.sb("tI", [128, NW], mybir.dt.int32)

                            def v3(b):
                                return b[:].rearrange("p (g e) -> p g e", g=16)

                            ar_b = ar[:, gs].unsqueeze(2).to_broadcast([128, 16, NE])
                            ai_b = ai[:, gs].unsqueeze(2).to_broadcast([128, 16, NE])
                            et_b = etab[:].unsqueeze(1).to_broadcast([128, 16, NE])
                            S.op("dve", lambda e: e.tensor_tensor(out=v3(tA), in0=ar_b, in1=et_b, op=ALU.mult), reads=[ar, etab], writes=[tA])
                            S.op("act", lambda e: e.activation(out=tA[:], in_=tA[:], func=AF.Exp), reads=[tA], writes=[tA])
                            S.op("dve", lambda e: e.tensor_tensor(out=v3(tB), in0=ai_b, in1=et_b, op=ALU.mult), reads=[ai, etab], writes=[tB])
                            S.op("dve", lambda e: e.tensor_scalar(out=tB[:], in0=tB[:], scalar1=1.0 / TWO_PI, scalar2=None, op0=ALU.mult), reads=[tB], writes=[tB])

                            def sin_of(dst, shift):
                                S.op("dve", lambda e: e.tensor_scalar(out=tC[:], in0=tB[:], scalar1=shift, scalar2=None, op0=ALU.add), reads=[tB], writes=[tC])
                                S.op("dve", lambda e: e.tensor_copy(out=tI[:], in_=tC[:]), reads=[tC], writes=[tI])
                                S.op("dve", lambda e: e.tensor_copy(out=tD[:], in_=tI[:]), reads=[tI], writes=[tD])
                                S.op("dve", lambda e: e.tensor_tensor(out=tC[:], in0=tC[:], in1=tD[:], op=ALU.subtract), reads=[tC, tD], writes=[tC])
                                S.op("dve", lambda e: e.tensor_scalar(out=tD[:], in0=tC[:], scalar1=0.5, scalar2=None, op0=ALU.is_gt), reads=[tC], writes=[tD])
                                S.op("dve", lambda e: e.tensor_tensor(out=tC[:], in0=tC[:], in1=tD[:], op=ALU.subtract), reads=[tC, tD], writes=[tC])
                                S.op("dve", lambda e: e.tensor_scalar(out=tD[:], in0=tC[:], scalar1=-0.5, scalar2=None, op0=ALU.is_lt), reads=[tC], writes=[tD])
                                S.op("dve", lambda e: e.tensor_tensor(out=tC[:], in0=tC[:], in1=tD[:], op=ALU.add), reads=[tC, tD], writes=[tC])
                                S.op("act", lambda e: e.activation(out=dst[:], in_=tC[:], func=AF.Sin, scale=TWO_PI), reads=[tC], writes=[dst])

                            sin_of(zi, 0.0)
                            sin_of(zr, 0.25)
                            S.op("dve", lambda e: e.tensor_tensor(out=zr[:], in0=zr[:], in1=tA[:], op=ALU.mult), reads=[zr, tA], writes=[zr])
                            S.op("dve", lambda e: e.tensor_tensor(out=zi[:], in0=zi[:], in1=tA[:], op=ALU.mult), reads=[zi, tA], writes=[zi])
                            S.flush()
                            scT.close()
                            S.scope = scHf
                            zr3 = v3(zr)
                            zi3 = v3(zi)
                            cf = S.sb("cf", [128, 16 * 8], F32)
                            cf3 = cf[:].rearrange("p (k g) -> p k g", k=8)
                            lre_h = lre[:, gs]
                            lim_h = lim[:, gs]

                            def cop(o, a, b, op):
                                S.op("dve", lambda e: e.tensor_tensor(out=o, in0=a, in1=b, op=op), reads=[zr, zi, lre, lim, cf], writes=[cf])

                            S.op("dve", lambda e: e.tensor_scalar(out=cf3[:, 0, :], in0=zr3[:, :, 80], scalar1=-1.0, scalar2=None, op0=ALU.add), reads=[zr], writes=[cf])
                            cop(cf3[:, 1, :], lre_h, lre_h, ALU.mult)
                            cop(cf3[:, 2, :], lim_h, lim_h, ALU.mult)
                            cop(cf3[:, 1, :], cf3[:, 1, :], cf3[:, 2, :], ALU.add)
                            S.op("dve", lambda e: e.reciprocal(out=cf3[:, 1, :], in_=cf3[:, 1, :]), reads=[cf], writes=[cf])
                            cop(cf3[:, 2, :], cf3[:, 0, :], lre_h, ALU.mult)
                            cop(cf3[:, 3, :], zi3[:, :, 80], lim_h, ALU.mult)
                            cop(cf3[:, 2, :], cf3[:, 2, :], cf3[:, 3, :], ALU.add)
                            cop(cf3[:, 4, :], cf3[:, 2, :], cf3[:, 1, :], ALU.mult)
                            cop(cf3[:, 2, :], zi3[:, :, 80], lre_h, ALU.mult)
                            cop(cf3[:, 3, :], cf3[:, 0, :], lim_h, ALU.mult)
                            cop(cf3[:, 2, :], cf3[:, 2, :], cf3[:, 3, :], ALU.subtract)
                            cop(cf3[:, 5, :], cf3[:, 2, :], cf3[:, 1, :], ALU.mult)
                            Bbr = S.sb("Bbr", [128, 256], F32)
                            Bbi = S.sb("Bbi", [128, 256], F32)
                            tb1 = S.sb("tb1", [128, 256], F32)

                            def g3(ap):
                                return ap.rearrange("p (g c) -> p g c", g=16)

                            qr_b = cf3[:, 4, :].unsqueeze(2).to_broadcast([128, 16, 16])
                            qi_b = cf3[:, 5, :].unsqueeze(2).to_broadcast([128, 16, 16])
                            bre_h = g3(bre[:, G0 * 16:G0 * 16 + 256])
                            bim_h = g3(bim[:, G0 * 16:G0 * 16 + 256])

                            def bop(o, a, b, op, wr):
                                S.op("dve", lambda e: e.tensor_tensor(out=o, in0=a, in1=b, op=op), reads=[cf, bre, bim, tb1, Bbr, Bbi], writes=[wr])

                            bop(g3(Bbr[:]), qr_b, bre_h, ALU.mult, Bbr)
                            bop(g3(tb1[:]), qi_b, bim_h, ALU.mult, tb1)
                            bop(Bbr[:], Bbr[:], tb1[:], ALU.subtract, Bbr)
                            bop(g3(Bbi[:]), qr_b, bim_h, ALU.mult, Bbi)
                            bop(g3(tb1[:]), qi_b, bre_h, ALU.mult, tb1)
                            bop(Bbi[:], Bbi[:], tb1[:], ALU.add, Bbi)
                            A4 = S.sb("A4", [128, 64], F32)
                            A44 = A4[:].rearrange("p (g o i) -> p g o i", g=16, o=2)
                            S.op("dve", lambda e: e.tensor_copy(out=A44[:, :, 0, 0], in_=zr3[:, :, 81]), reads=[zr], writes=[A4])
                            S.op("dve", lambda e: e.tensor_copy(out=A44[:, :, 1, 1], in_=zr3[:, :, 81]), reads=[zr], writes=[A4])
                            S.op("dve", lambda e: e.tensor_copy(out=A44[:, :, 1, 0], in_=zi3[:, :, 81]), reads=[zi], writes=[A4])
                            S.op("dve", lambda e: e.tensor_scalar(out=A44[:, :, 0, 1], in0=zi3[:, :, 81], scalar1=-1.0, scalar2=None, op0=ALU.mult), reads=[zi], writes=[A4])

                            A2c = S.sb("A2c", [128, 128], F32)
                            for c_ in range(2):
                                A2v = A2c[:, c_ * 64:(c_ + 1) * 64].rearrange("p (g o i) -> p g o i", g=16, o=2)
                                S.op("dve", lambda e, A2v=A2v: e.tensor_copy(out=A2v[:, :, 0, 0], in_=zr3[:, :, 82]), reads=[zr], writes=[A2c])
                                S.op("dve", lambda e, A2v=A2v: e.tensor_copy(out=A2v[:, :, 1, 1], in_=zr3[:, :, 82]), reads=[zr], writes=[A2c])
                                S.op("dve", lambda e, A2v=A2v: e.tensor_copy(out=A2v[:, :, 1, 0], in_=zi3[:, :, 82]), reads=[zi], writes=[A2c])
                                S.op("dve", lambda e, A2v=A2v: e.tensor_scalar(out=A2v[:, :, 0, 1], in0=zi3[:, :, 82], scalar1=-1.0, scalar2=None, op0=ALU.mult), reads=[zi], writes=[A2c])
                            U = S.sb("U", [128, 16 * 512], BF16)
                            U3 = U[:].rearrange("p (g m) -> p g m", g=16)
                            for gb in range(2):
                                ct = gb
                                for s in range(8):
                                    dma(U3[16 * s:16 * s + 16, gb * 8:gb * 8 + 8, :], usc_d[ct][:, :, s, :].rearrange("g c m -> c g m"), writes=[U], par=True)
                            Hs = S.sb("Hs", [128, 128 * 32], F32)
                            Hs_gr = Hs[:].rearrange("p (k g r) -> p g r k", g=16, r=2)
                            Hs_k = Hs[:].rearrange("p (k g r) -> p k g r", g=16, r=2)
                            Hb = S.sb("Hb", [128, 32 * 128], BF16)
                            Hb4 = Hb[:].rearrange("p (g r k) -> p g r k", g=16, r=2)
                            Cg = g3

                            def cprod(dst_r, dst_i, zr_ap, zi_ap, Xr_ap, Xi_ap, t1, t2, neg_i, rd, wr):
                                S.op("dve", lambda e: e.tensor_tensor(out=t1, in0=zr_ap, in1=Xr_ap, op=ALU.mult), reads=rd, writes=wr)
                                S.op("pool", lambda e: e.tensor_tensor(out=t2, in0=zi_ap, in1=Xi_ap, op=ALU.mult), reads=rd, writes=wr)
                                S.op("dve", lambda e: e.tensor_tensor(out=dst_r, in0=t1, in1=t2, op=ALU.subtract), reads=rd + wr, writes=wr)
                                S.op("dve", lambda e: e.tensor_tensor(out=t1, in0=zr_ap, in1=Xi_ap, op=ALU.mult), reads=rd + wr, writes=wr)
                                S.op("pool", lambda e: e.tensor_tensor(out=t2, in0=zi_ap, in1=Xr_ap, op=ALU.mult), reads=rd + wr, writes=wr)
                                if neg_i:
                                    S.op("dve", lambda e: e.scalar_tensor_tensor(out=dst_i, in0=t1, scalar=-1.0, in1=t2, op0=ALU.mult, op1=ALU.subtract),
                                         reads=rd + wr, writes=wr)
                                else:
                                    S.op("dve", lambda e: e.tensor_tensor(out=dst_i, in0=t1, in1=t2, op=ALU.add), reads=rd + wr, writes=wr)

                            with ExitStack() as scP:
                                S.scope = scP
                                for gb in range(2):
                                    gsl = slice(gb * 8, gb * 8 + 8)
                                    t1 = S.sb("gt1", [128, 1024], F32)
                                    t2 = S.sb("gt2", [128, 1024], F32)
                                    PBr = S.sb("PBr", [128, 1024], BF16)
                                    PBi = S.sb("PBi", [128, 1024], BF16)
                                    Pw = S.sb("Pw", [128, 8 * 4 * 2 * 128], BF16)
                                    Pw5 = Pw[:].rearrange("p (g j r q) -> p g j r q", g=8, j=4, r=2)
                                    grp = [t1, t2, PBr, PBi]

                                    def v4(b):
                                        return b[:].rearrange("p (g s c) -> p g s c", g=8, s=8)

                                    Br_b = g3(Bbr[:])[:, gsl, :].unsqueeze(2).to_broadcast([128, 8, 8, 16])
                                    Bi_b = g3(Bbi[:])[:, gsl, :].unsqueeze(2).to_broadcast([128, 8, 8, 16])
                                    for j in range(4):
                                        zr_b = zr3[:, gsl, 48 + 8 * j:56 + 8 * j].unsqueeze(3).to_broadcast([128, 8, 8, 16])
                                        zi_b = zi3[:, gsl, 48 + 8 * j:56 + 8 * j].unsqueeze(3).to_broadcast([128, 8, 8, 16])
                                        cprod(v4(PBr), v4(PBi), zr_b, zi_b, Br_b, Bi_b, v4(t1), v4(t2), False, [zr, zi, Bbr, Bbi], grp)
                                        for g in range(8):
                                            for ri, PB in enumerate((PBr, PBi)):
                                                slot = (g % 4) * 2 + ri
                                                S.op("pe", lambda e, PB=PB, g=g, slot=slot: e.transpose(out=PSB[:, slot * 128:(slot + 1) * 128], in_=PB[:, g * 128:(g + 1) * 128],
                                                                                                       identity=identb[:]), reads=[PB, identb], writes=[PSB], sig=(g % 4 == 3 and ri == 1))
                                            if g % 4 == 3:
                                                g0 = g - 3
                                                evac_copy(Pw5[:, g0:g0 + 4, j, :, :], PSB[:].rearrange("p (g r q) -> p g r q", g=4, r=2), [PSB], [Pw])
                                    for g in range(8):
                                        gl = gb * 8 + g
                                        ps = PS[5 + ((g // 2) % 2)]
                                        for ri in range(2):
                                            slot = (g % 2) * 2 + ri
                                            for j in range(4):
                                                S.op("pe", lambda e, ps=ps, slot=slot, g=g, j=j, ri=ri, gl=gl, Pw5=Pw5: e.matmul(
                                                    ps[:, slot * 128:(slot + 1) * 128], lhsT=Pw5[:, g, j, ri, :], rhs=U3[:, gl, j:512:4], start=(j == 0), stop=(j == 3)),
                                                    reads=[Pw, U], writes=[ps], sig=(j == 3 and ri == 1 and g % 2 == 1))
                                        if g % 2 == 1:
                                            evac_copy(Hs_gr[:, gl - 1:gl + 1, :, :], ps[:, 0:512].rearrange("p (g r k) -> p g r k", g=2, r=2), [ps], [Hs])
                                S.flush()
                            S.scope = scHf

                            scSc = ExitStack()
                            S.scope = scSc
                            Hf = Buf(Hs.t, "Hf")
                            Hbk = Buf(Hs.t, "Hbk")

                            scn = S.sb("scn", [128, 127 * 64], F32)
                            sc5 = scn[:].rearrange("p (k g o i) -> p k g o i", k=127, g=16, o=2)
                            scF = Buf(scn.t, "scF")
                            scB = Buf(scn.t, "scB")
                            prf2 = S.sb("prf2", [128, 128], F32)
                            prb2 = S.sb("prb2", [128, 128], F32)
                            Hs2 = Hs[:].rearrange("p (kg r) -> p kg r", r=2)

                            def scan_pre(eng, sl, src_lo, dst_lo, Hx, scX):
                                for o in range(2):
                                    a_b = A44[sl][:, :, o, :].unsqueeze(1).to_broadcast([64, 127, 16, 2])
                                    S.op(eng, lambda e, o=o, a_b=a_b: e.tensor_tensor(out=sc5[sl, :, :, o, :], in0=a_b, in1=Hs_k[sl, src_lo:src_lo + 127, :, :], op=ALU.mult),
                                         reads=[A4, Hx], writes=[scX])
                                S.op(eng, lambda e: e.tensor_tensor(out=sc5[sl, :, :, :, 0], in0=sc5[sl, :, :, :, 0], in1=sc5[sl, :, :, :, 1], op=ALU.add), reads=[scX], writes=[scX])
                                S.op(eng, lambda e: e.tensor_tensor(out=Hs_k[sl, dst_lo:dst_lo + 127, :, :], in0=Hs_k[sl, dst_lo:dst_lo + 127, :, :], in1=sc5[sl, :, :, :, 0], op=ALU.add),
                                     reads=[scX], writes=[Hx])

                            def scan_step2(eng, kd, ks, sl, pr, Hx):
                                pr4 = pr[sl, :].rearrange("p (cg o i) -> p cg o i", o=2, i=2)
                                a4 = A2c[sl, :].rearrange("p (cg o i) -> p cg o i", o=2, i=2)
                                X = Hs2[sl, ks * 16:ks * 16 + 32, :].unsqueeze(2).to_broadcast([64, 32, 2, 2])
                                S.op(eng, lambda e: e.tensor_tensor(out=pr4, in0=a4, in1=X, op=ALU.mult), reads=[A2c, Hx], writes=[pr])
                                S.op(eng, lambda e: e.tensor_tensor(out=pr4[:, :, :, 0], in0=pr4[:, :, :, 0], in1=pr4[:, :, :, 1], op=ALU.add), reads=[pr], writes=[pr])
                                S.op(eng, lambda e: e.tensor_tensor(out=Hs2[sl, kd * 16:kd * 16 + 32, :], in0=Hs2[sl, kd * 16:kd * 16 + 32, :], in1=pr4[:, :, :, 0], op=ALU.add),
                                     reads=[pr], writes=[Hx])

                            scan_pre("dve", slice(0, 64), 0, 1, Hf, scF)
                            scan_pre("pool", slice(64, 128), 1, 0, Hbk, scB)
                            for m in range(1, 64):
                                scan_step2("dve", 2 * m, 2 * m - 2, slice(0, 64), prf2, Hf)
                                mb = 63 - m
                                scan_step2("pool", 2 * mb, 2 * mb + 2, slice(64, 128), prb2, Hbk)
                            S.op("dve", lambda e: e.tensor_copy(out=Hb4[0:64], in_=Hs_gr[0:64]), reads=[Hf], writes=[Hb])
                            S.op("dve", lambda e: e.tensor_copy(out=Hb4[64:128], in_=Hs_gr[64:128]), reads=[Hbk, Hb], writes=[Hb])
                            S.flush()
                            scSc.close()
                            S.scope = scHf

                            YP = S.sb("YP", [128, 16 * 512], BF16)
                            YP3 = YP[:].rearrange("p (g m) -> p g m", g=16)
                            YPg = [Buf(YP.t, "YPg0"), Buf(YP.t, "YPg1")]
                            with ExitStack() as scP:
                                S.scope = scP
                                t1 = S.sb("ht1", [128, 1024], F32)
                                t2 = S.sb("ht2", [128, 1024], F32)
                                gxs = [S.sb("gxs%d" % i, [128, 512], F32) for i in range(2)]
                                gus = [S.sb("gus%d" % i, [128, 512], F32) for i in range(2)]
                                for gb in range(2):
                                    gsl = slice(gb * 8, gb * 8 + 8)
                                    PBr = S.sb("PB0r", [128, 1024], BF16)
                                    PBi = S.sb("PB0i", [128, 1024], BF16)
                                    QCr = [S.sb("QCr%d" % m, [128, 1024], BF16) for m in range(5)]
                                    QCi = [S.sb("QCi%d" % m, [128, 1024], BF16) for m in range(5)]
                                    Tw = S.sb("Tw", [128, 8 * 7 * 128], BF16)
                                    Tw4 = Tw[:].rearrange("p (g d q) -> p g d q", g=8, d=7)
                                    tt0 = S.sb("tt0", [128, 128], F32)
                                    tt1 = S.sb("tt1", [128, 128], F32)

                                    def v4(b):
                                        return b[:].rearrange("p (g s c) -> p g s c", g=8, s=8)

                                    Br_b = g3(Bbr[:])[:, gsl, :].unsqueeze(2).to_broadcast([128, 8, 8, 16])
                                    Bi_b = g3(Bbi[:])[:, gsl, :].unsqueeze(2).to_broadcast([128, 8, 8, 16])
                                    zr_b = zr3[:, gsl, 0:8].unsqueeze(3).to_broadcast([128, 8, 8, 16])
                                    zi_b = zi3[:, gsl, 0:8].unsqueeze(3).to_broadcast([128, 8, 8, 16])
                                    cprod(v4(PBr), v4(PBi), zr_b, zi_b, Br_b, Bi_b, v4(t1), v4(t2), False, [zr, zi, Bbr, Bbi], [t1, t2, PBr, PBi])
                                    cre_h = g3(cre[:, G0 * 16:G0 * 16 + 256])[:, gsl, :].unsqueeze(2).to_broadcast([128, 8, 8, 16])
                                    cim_h = g3(cim[:, G0 * 16:G0 * 16 + 256])[:, gsl, :].unsqueeze(2).to_broadcast([128, 8, 8, 16])
                                    for m in range(5):
                                        zr_m = zr3[:, gsl, 8 + 8 * m:16 + 8 * m].unsqueeze(3).to_broadcast([128, 8, 8, 16])
                                        zi_m = zi3[:, gsl, 8 + 8 * m:16 + 8 * m].unsqueeze(3).to_broadcast([128, 8, 8, 16])
                                        cprod(v4(QCr[m]), v4(QCi[m]), zr_m, zi_m, cre_h, cim_h, v4(t1), v4(t2), True, [zr, zi, cre, cim], [t1, t2, QCr[m], QCi[m]])
                                    F_ = slice(0, 64)
                                    B_ = slice(64, 128)
                                    TwB = [Buf(Tw.t, "TwB%d" % g_) for g_ in range(8)]

                                    def tgen(g):
                                        gq = slice(g * 128, (g + 1) * 128)
                                        psX = PS[0 + (g % 2)]
                                        psY = PS[2 + (g % 2)]
                                        jobs = [(psX, 0, F_, 1), (psX, 1, F_, 2), (psX, 2, F_, 3), (psX, 3, F_, 0),
                                                (psY, 0, B_, 1), (psY, 1, B_, 2), (psY, 2, B_, 3), (psY, 3, B_, 0)]
                                        for (pz, slot, dsl, m) in jobs:
                                            S.op("pe", lambda e, pz=pz, slot=slot, dsl=dsl, m=m, gq=gq, PBr=PBr, QCr=QCr: e.matmul(pz[:, slot * 128:(slot + 1) * 128], lhsT=PBr[dsl, gq], rhs=QCr[m][dsl, gq],
                                                                                                               start=True, stop=False), reads=[PBr, QCr[m]], writes=[pz], sig=False)
                                            S.op("pe", lambda e, pz=pz, slot=slot, dsl=dsl, m=m, gq=gq, PBi=PBi, QCi=QCi: e.matmul(pz[:, slot * 128:(slot + 1) * 128], lhsT=PBi[dsl, gq], rhs=QCi[m][dsl, gq],
                                                                                                               start=False, stop=True), reads=[PBi, QCi[m]], writes=[pz], sig=(slot == 3))
                                        evac_copy(Tw4[:, g, 0:3, :], psX[:, 0:384].rearrange("p (d q) -> p d q", d=3), [psX], [TwB[g]])
                                        evac_copy(Tw4[:, g, 3:6, :], psY[:, 0:384].rearrange("p (d q) -> p d q", d=3), [psY], [TwB[g]])
                                        S.op("dve", lambda e, psX=psX, tt0=tt0: e.tensor_tensor(out=tt0[:], in0=psX[:, 384:512], in1=mge[:], op=ALU.mult), reads=[psX, mge], writes=[tt0])
                                        S.op("dve", lambda e, psY=psY, tt1=tt1: e.tensor_tensor(out=tt1[:], in0=psY[:, 384:512], in1=mle[:], op=ALU.mult), reads=[psY, mle], writes=[tt1])
                                        S.op("dve", lambda e, tt0=tt0, tt1=tt1: e.tensor_tensor(out=tt0[:], in0=tt0[:], in1=tt1[:], op=ALU.add), reads=[tt0, tt1], writes=[tt0])
                                        Gg = G0 + gb * 8 + g
                                        S.op("dve", lambda e, g=g, Gg=Gg, Tw4=Tw4, tt0=tt0: e.scalar_tensor_tensor(out=Tw4[:, g, 6, :], in0=identf[:], scalar=dsk[:, Gg:Gg + 1], in1=tt0[:],
                                                                                                 op0=ALU.mult, op1=ALU.add), reads=[identf, dsk, tt0], writes=[TwB[g]])
                                    tidx = {1: 0, 2: 1, 3: 2, -1: 3, -2: 4, -3: 5, 0: 6}
                                    def ycomp(g):
                                        gl = gb * 8 + g
                                        gq = slice(g * 128, (g + 1) * 128)
                                        ps = PS[5 + (g % 2)]
                                        psB = PS[4]
                                        for i in range(4):
                                            o0 = i * 128
                                            for j in range(4):
                                                S.op("pe", lambda e, ps=ps, o0=o0, g=g, i=i, j=j, gl=gl, Tw4=Tw4: e.matmul(ps[:, o0:o0 + 128], lhsT=Tw4[:, g, tidx[i - j], :], rhs=U3[:, gl, j:512:4],
                                                                                                                 start=(j == 0), stop=False), reads=[TwB[g], U], writes=[ps], sig=False)
                                            S.op("pe", lambda e, ps=ps, o0=o0, i=i, gq=gq, gl=gl, QCr=QCr: e.matmul(ps[:, o0 + 1:o0 + 128], lhsT=QCr[i][F_, gq], rhs=Hb4[F_, gl, 0, 0:127],
                                                                                                          start=False, stop=False), reads=[QCr[i], Hb], writes=[ps], sig=False)
                                            S.op("pe", lambda e, ps=ps, o0=o0, i=i, gq=gq, gl=gl, QCi=QCi: e.matmul(ps[:, o0 + 1:o0 + 128], lhsT=QCi[i][F_, gq], rhs=Hb4[F_, gl, 1, 0:127],
                                                                                                          start=False, stop=True), reads=[QCi[i], Hb], writes=[ps], sig=(i == 3))
                                        for i in range(4):
                                            o0 = i * 128
                                            S.op("pe", lambda e, psB=psB, o0=o0, i=i, gq=gq, gl=gl, QCr=QCr: e.matmul(psB[:, o0:o0 + 127], lhsT=QCr[4 - i][B_, gq], rhs=Hb4[B_, gl, 0, 1:128],
                                                                                                            start=True, stop=False), reads=[QCr[4 - i], Hb], writes=[psB], sig=False)
                                            S.op("pe", lambda e, psB=psB, o0=o0, i=i, gq=gq, gl=gl, QCi=QCi: e.matmul(psB[:, o0:o0 + 127], lhsT=QCi[4 - i][B_, gq], rhs=Hb4[B_, gl, 1, 1:128],
                                                                                                            start=False, stop=True), reads=[QCi[4 - i], Hb], writes=[psB], sig=(i == 3))
                                        xs = gxs[g % 2]
                                        us = gus[g % 2]
                                        S.op("act", lambda e, ps=ps, xs=xs: e.activation(out=xs[:], in_=ps[:, 0:512], func=AF.Copy), reads=[ps], writes=[xs])
                                        S.op("dve", lambda e, psB=psB, xs=xs: e.tensor_tensor(out=xs[:].rearrange("p (i k) -> p i k", i=4)[:, :, 0:127],
                                                                                             in0=xs[:].rearrange("p (i k) -> p i k", i=4)[:, :, 0:127],
                                                                                             in1=psB[:, 0:512].rearrange("p (i k) -> p i k", i=4)[:, :, 0:127], op=ALU.add),
                                             reads=[psB, xs], writes=[xs])
                                        S.op("dve", lambda e, xs=xs, us=us: e.tensor_tensor(out=us[:], in0=xs[:], in1=xs[:], op=ALU.mult), reads=[xs], writes=[us])
                                        S.op("dve", lambda e, us=us: e.tensor_scalar(out=us[:], in0=us[:], scalar1=0.044715, scalar2=1.0, op0=ALU.mult, op1=ALU.add), reads=[us], writes=[us])
                                        S.op("dve", lambda e, xs=xs, us=us: e.tensor_tensor(out=us[:], in0=us[:], in1=xs[:], op=ALU.mult), reads=[xs, us], writes=[us])
                                        S.op("act", lambda e, us=us: e.activation(out=us[:], in_=us[:], func=AF.Sigmoid, scale=1.5957691216057308), reads=[us], writes=[us])
                                        S.op("pool", lambda e, xs=xs, us=us, gl=gl: e.tensor_tensor(out=YP3[:, gl, :], in0=xs[:], in1=us[:], op=ALU.mult), reads=[xs, us], writes=[YPg[gb]])

                                    for g in range(8):
                                        tgen(g)
                                        if g >= 1:
                                            ycomp(g - 1)
                                    ycomp(7)
                                    ct = gb
                                    ys2v = ys2_t[ct].rearrange("(g c) (s m) -> g c s m", c=16, s=8)
                                    for s in range(8):
                                        dma(ys2v[:, :, s, :].rearrange("g c m -> c g m"), YP3[16 * s:16 * s + 16, gb * 8:gb * 8 + 8, :], reads=[YPg[gb]], writes=[ys2B[ct]], par=True)
                                    allgather(ys2_t[ct], ys2g_t[ct], ys2B[ct], ys2gB[ct])
                                S.flush()
                            S.scope = scHf
                        S.scope = scS

                    with ExitStack() as scG:
                        S.scope = scG
                        yg = S.sb("yg", [128, 4 * 4096], BF16)
                        yg4 = yg[:].rearrange("p (c t m) -> p c t m", c=4, t=8)
                        for ct in range(4):
                            dma(yg4[:, ct, :, :], ys2g_t[ct % 2][(ct // 2) * 128:(ct // 2 + 1) * 128, :].rearrange("p (t m) -> p t m", t=8), reads=[ys2gB[ct % 2]], writes=[yg], par=True)
                        ygl = S.sb("ygl", [128, 2 * 4096], BF16)
                        ygl4 = ygl[:].rearrange("p (c t m) -> p c t m", c=2, t=8)
                        for ct in range(2):
                            dma(ygl4[:, ct, :, :], ys2_t[ct].rearrange("p (t m) -> p t m", t=8), reads=[ys2B[ct]], writes=[ygl], par=True)
                        gws = S.sb("gws", [128, 4 * 256], F32)
                        gwb = S.sb("gwb", [128, 4 * 256], BF16)
                        gwb3 = gwb[:].rearrange("p (k o) -> p k o", k=4)
                        glb = S.sb("glb", [128, 2], F32)
                        dma(gws[:].rearrange("p (k o) -> p k o", k=4), gluw_d[l].rearrange("(k p) o -> p k o", p=128), writes=[gws])
                        dma(glb[:], glub_d[l], writes=[glb])
                        S.op("pool", lambda e: e.tensor_copy(out=gwb[:], in_=gws[:]), reads=[gws], writes=[gwb])
                        zts = [S.sb("zg%d" % i, [128, S_LEN], BF16) for i in range(2)]
                        yos = [S.sb("yo%d" % i, [128, S_LEN], BF16) for i in range(2)]
                        sgt = [S.sb("sgt%d" % i, [128, 512], BF16) for i in range(2)]
                        tmg = [S.sb("tmg%d" % i, [128, 512], BF16) for i in range(2)]
                        for ot in range(2):
                            zt = zts[ot % 2]
                            yo = yos[ot % 2]
                            dma(zt[:], zsc_d[ot], writes=[zt])
                            ztv = zt[:].rearrange("p (k i t) -> p t i k", i=4, t=8)
                            yov = yo[:].rearrange("p (k i t) -> p t i k", i=4, t=8)
                            for t in range(8):
                                ps = PS[5 + (t % 2)]
                                sg = sgt[t % 2]
                                tm = tmg[t % 2]
                                for kt in range(4):
                                    S.op("pe", lambda e, ps=ps, kt=kt, ot=ot, t=t: e.matmul(ps[:, 0:512], lhsT=gwb3[:, kt, ot * 128:(ot + 1) * 128], rhs=yg4[:, kt, t, :],
                                                                                           start=(kt == 0), stop=(kt == 3)), reads=[gwb, yg], writes=[ps], sig=(kt == 3))
                                S.op("act", lambda e, ps=ps, sg=sg, ot=ot: e.activation(out=sg[:], in_=ps[:, 0:512], func=AF.Sigmoid, bias=glb[:, ot:ot + 1], scale=1.0),
                                     reads=[ps, glb], writes=[sg])
                                S.op("dve", lambda e, sg=sg, tm=tm, ot=ot, t=t: e.tensor_tensor(out=tm[:], in0=ygl4[:, ot, t, :], in1=sg[:], op=ALU.mult), reads=[ygl, sg], writes=[tm])
                                S.op("pool", lambda e, tm=tm, t=t, yov=yov, ztv=ztv: e.tensor_tensor(out=yov[:, t, :, :], in0=tm[:].rearrange("p (i k) -> p i k", i=4),
                                                                                                   in1=ztv[:, t, :, :], op=ALU.mult), reads=[tm, zt], writes=[yo])
                            dma(ysc_t[ot], yo[:], reads=[yo], writes=[yscB[ot]])
                            allgather(ysc_t[ot], ygat_t[ot], yscB[ot], ygatB[ot])
                        S.flush()
                    S.scope = scS
                S.scope = es

            with ExitStack() as scO:
                S.scope = scO
                wos = [S.sb("wos%d" % i, [128, D], F32) for i in range(2)]
                wo = S.sb("wo", [128, 12 * D], BF16)
                wo3 = wo[:].rearrange("p (k o) -> p k o", k=12)
                for kt in range(12):
                    ws = wos[kt % 2]
                    dma(ws[:], wout_d[l, kt * 128:(kt + 1) * 128, :], writes=[ws])
                    S.op("pool", lambda e, ws=ws, kt=kt: e.tensor_copy(out=wo3[:, kt, :], in_=ws[:]), reads=[ws], writes=[wo])
                fg = S.sb("fg", [128, D], F32)
                if last:
                    dma(fg[:], finalg_d.partition_broadcast(128), writes=[fg])
                yts = [S.sb("yt%d" % i, [128, 12 * 512], BF16) for i in range(2)]
                ytSs = [Buf(yts[i].t, "ytS%d" % i) for i in range(2)]
                korder = [4, 5, 6, 7, 8, 9, 10, 11, 0, 1, 2, 3]
                xts = [S.sb("xo%d" % i, [128, D], F32) for i in range(4)]
                xns = [S.sb("xn%d" % i, [128, D], F32) for i in range(4)]
                junk = S.sb("junk2", [128, D], BF16)
                sts2 = [S.sb("st2_%d" % i, [128, 4 * 8], F32) for i in range(4)]
                for st_ in sts2:
                    S.op("dve", lambda e, st_=st_: e.memset(st_[:], 0.0), writes=[st_])
                xdst = out_d if last else x1_d
                for nb_ in range(8):
                    yt = yts[nb_ % 2]
                    yt3 = yt[:].rearrange("p (k n) -> p k n", k=12)
                    ytS = ytSs[nb_ % 2]
                    for kt in range(12):
                        s_, rr_ = divmod(kt, 2)
                        dma(yt3[:, kt, :], ygat_t[s_][rr_ * 128:(rr_ + 1) * 128, nb_ * 512:(nb_ + 1) * 512], reads=[ygatB[s_]], writes=[ytS if kt < 4 else yt], par=True)
                    for tq in range(4):
                        tt = nb_ * 4 + tq
                        xt = xts[tt % 4]
                        xn = xns[tt % 4]
                        dma(xt[:], xsrc[tt * 128:(tt + 1) * 128, :], writes=[xt], q="act")
                        for oh in range(2):
                            ps = PS[5 + oh]
                            for ki, kt in enumerate(korder):
                                S.op("pe", lambda e, ps=ps, kt=kt, ki=ki, oh=oh, tq=tq, yt3=yt3: e.matmul(ps[:, 0:512], lhsT=yt3[:, kt, tq * 128:(tq + 1) * 128], rhs=wo3[:, kt, oh * 512:(oh + 1) * 512],
                                                                                                         start=(ki == 0), stop=(ki == 11)), reads=[(ytS if kt < 4 else yt), wo], writes=[ps], sig=(ki == 11))
                            S.op("dve", lambda e, ps=ps, oh=oh, xt=xt, xn=xn: e.tensor_tensor(out=xn[:, oh * 512:(oh + 1) * 512], in0=ps[:, 0:512], in1=xt[:, oh * 512:(oh + 1) * 512], op=ALU.add),
                                 reads=[ps, xt], writes=[xn])
                        if last:
                            c0 = (tt // 4) * 4
                            st = sts2[tt % 4]
                            S.op("act", lambda e, xn=xn, c0=c0, st=st: e.activation(out=junk[:], in_=xn[:], func=AF.Square, accum_out=st[:, c0:c0 + 1]), reads=[xn], writes=[junk, st])
                            S.op("dve", lambda e, c0=c0, st=st: e.tensor_scalar(out=st[:, c0 + 1:c0 + 2], in0=st[:, c0:c0 + 1], scalar1=1.0 / D, scalar2=1e-6, op0=ALU.mult, op1=ALU.add),
                                 reads=[st], writes=[st])
                            S.op("act", lambda e, c0=c0, st=st: e.activation(out=st[:, c0 + 2:c0 + 3], in_=st[:, c0 + 1:c0 + 2], func=AF.Sqrt), reads=[st], writes=[st])
                            S.op("dve", lambda e, c0=c0, st=st: e.reciprocal(out=st[:, c0 + 3:c0 + 4], in_=st[:, c0 + 2:c0 + 3]), reads=[st], writes=[st])
                            S.op("dve", lambda e, xn=xn, c0=c0, st=st: e.scalar_tensor_tensor(out=xn[:], in0=xn[:], scalar=st[:, c0 + 3:c0 + 4], in1=fg[:], op0=ALU.mult, op1=ALU.mult),
                                 reads=[xn, st, fg], writes=[xn])
                        dma(xdst[tt * 128:(tt + 1) * 128, :], xn[:], reads=[xn], q="sp")
                S.flush()
            S.scope = es
        print("instructions recorded:", S.nins)
    return nc


def make_in_maps(inputs, ncores=8):
    f = np.float32
    mge, mle = _masks()
    ident = np.eye(128, dtype=f)
    etab = _etab()

    def dp(a):
        return np.ascontiguousarray(np.transpose(a, (0, 1, 3, 2)).reshape(2, 128, 32), f)

    lre = dp(inputs["ssm_lam_re"])
    lim = dp(inputs["ssm_lam_im"])
    ldt = np.ascontiguousarray(np.broadcast_to(inputs["ssm_log_dt"][:, :, None, :], (2, 2, 64, 32)).reshape(2, 128, 32), f)
    bre = np.ascontiguousarray(np.transpose(inputs["ssm_b_re"], (0, 1, 3, 2, 4)).reshape(2, 128, 512), f)
    bim = np.ascontiguousarray(np.transpose(inputs["ssm_b_im"], (0, 1, 3, 2, 4)).reshape(2, 128, 512), f)
    cre = np.ascontiguousarray(np.transpose(inputs["ssm_c_re"], (0, 1, 4, 2, 3)).reshape(2, 128, 512), f)
    cim = np.ascontiguousarray(np.transpose(inputs["ssm_c_im"], (0, 1, 4, 2, 3)).reshape(2, 128, 512), f)
    dd = inputs["ssm_d"].reshape(2, 32, 16)
    dsk = np.ascontiguousarray(np.broadcast_to(np.transpose(dd, (0, 2, 1))[:, None, :, :], (2, 8, 16, 32)).reshape(2, 128, 32), f)
    glub = np.ascontiguousarray(np.transpose(inputs["glu_b"].reshape(2, 4, 128), (0, 2, 1)), f)
    gna = np.stack([_gather_na(np.asarray(inputs["na_rpb"][l], f)) for l in range(2)])
    gdil = _gather_dil(np.asarray(inputs["t5_bias"], f))
    w_in = np.asarray(inputs["w_in"], f)
    w_out = np.asarray(inputs["w_out"], f)
    glu_w = np.asarray(inputs["glu_w"], f)
    gt = []
    for s_ in range(6):
        for rr in range(2):
            base = (0, 0, 4, 4, 8, 8)[s_]
            gt.append(base + 2 * rr + (s_ % 2))
    rows = np.concatenate([np.arange(t * 128, (t + 1) * 128) for t in gt])
    w_out_p = np.ascontiguousarray(w_out[:, rows, :])
    maps = []
    for c in range(ncores):
        b, r = divmod(c, 2)
        b = b % 4
        tl = (2 * r, 2 * r + 1)
        idx = np.concatenate([np.arange(base + t * 128, base + (t + 1) * 128)
                              for base in (0, 512, 1024, 1536, 2048, 2560, 3072, 3584, 4096, 4608) for t in tl])
        m = {
            "x": np.ascontiguousarray(inputs["x"][b], f),
            "norm_g": np.ascontiguousarray(inputs["norm_g"], f),
            "final_g": np.ascontiguousarray(inputs["final_g"], f),
            "w_in": np.ascontiguousarray(w_in[:, :, idx]),
            "w_out": w_out_p,
            "ident": ident, "mge": mge, "mle": mle, "etab": etab,
            "lam_re_t": np.ascontiguousarray(lre[:, :, 16 * r:16 * r + 16]),
            "lam_im_t": np.ascontiguousarray(lim[:, :, 16 * r:16 * r + 16]),
            "logdt_t": np.ascontiguousarray(ldt[:, :, 16 * r:16 * r + 16]),
            "b_re_t": np.ascontiguousarray(bre[:, :, 256 * r:256 * r + 256]),
            "b_im_t": np.ascontiguousarray(bim[:, :, 256 * r:256 * r + 256]),
            "c_re_t": np.ascontiguousarray(cre[:, :, 256 * r:256 * r + 256]),
            "c_im_t": np.ascontiguousarray(cim[:, :, 256 * r:256 * r + 256]),
            "dsk_t": np.ascontiguousarray(dsk[:, :, 16 * r:16 * r + 16]),
            "glu_w": np.ascontiguousarray(glu_w[:, :, 256 * r:256 * r + 256]),
            "glub_t": np.ascontiguousarray(glub[:, :, 2 * r:2 * r + 2]),
            "gna": np.ascontiguousarray(gna[:, 4 * r:4 * r + 4]),
            "gdil": np.ascontiguousarray(gdil[4 * r:4 * r + 4]),
        }
        maps.append(m)
    return maps


_CACHE = {}


def kernel(**inputs):
    if "nc" not in _CACHE:
        _CACHE["nc"] = build_program()
    nc = _CACHE["nc"]
    maps = make_in_maps(inputs, 8)
    res = run_bass_kernel_spmd(nc, maps, core_ids=list(range(8)))
    out = np.stack([np.asarray(res.results[2 * b]["out"], np.float32) for b in range(4)], axis=0)
    return out
```

```python
import numpy as np
from contextlib import ExitStack
import concourse.bass as bass
import concourse.mybir as mybir
from concourse.bass_utils import run_bass_kernel_spmd

F32 = mybir.dt.float32
BF16 = mybir.dt.bfloat16
ALU = mybir.AluOpType
AF = mybir.ActivationFunctionType

S_LEN = 4096
D = 1024
NEG = -30000.0
MAGIC = 12582912.0
TWO_PI = float(2 * np.pi)
NE = 83
ATTACH_WAITS = True


class Buf:
    __slots__ = ("t", "lw", "rd", "name", "psum")

    def __init__(self, t, name="", psum=False):
        self.t = t
        self.lw = {}
        self.rd = {}
        self.name = name
        self.psum = psum

    def __getitem__(self, k):
        return self.t[k]


class Sched:
    NDMA = 56
    NHW = 40

    def __init__(self, nc, es):
        self.nc = nc
        self.es = es
        self.names = ["sp", "pe", "act", "dve", "pool"]
        self.sem = {k: es.enter_context(nc.semaphore("s_" + k)) for k in self.names}
        self.cnt = {k: 0 for k in self.names}
        self.dsem = [es.enter_context(nc.semaphore("d%d" % i)) for i in range(self.NDMA)]
        self.dcnt = [0] * self.NDMA
        self.dnext = {"hw": 0, "sw": self.NHW}
        self.waited = {}
        self.prog = {k: [] for k in self.names}
        self.scope = es
        self.nins = 0
        self.pend = {k: False for k in self.names}
        self.bsem = es.enter_context(nc.semaphore("s_bar"))
        self.bcnt = 0

    def sb(self, name, shape, dt):
        self.uid = getattr(self, "uid", 0) + 1
        name = "sb%d_%s" % (self.uid, name)
        return Buf(self.scope.enter_context(self.nc.sbuf_tensor(name, shape, dt)), name)

    def ps(self, name, shape, dt=F32):
        return Buf(self.es.enter_context(self.nc.psum_tensor(name, shape, dt)), name, True)

    def _wait(self, e, tok):
        if tok is None:
            return
        kind, idx, val = tok
        if kind == "c" and e == "pe" and idx == "pe":
            return
        if kind == "c":
            assert val <= self.cnt[idx], "wait on a not-yet-recorded signalling op (%s waits %s %d > %d)" % (e, idx, val, self.cnt[idx])
        key = (e, kind, idx)
        if self.waited.get(key, 0) >= val:
            return
        self.waited[key] = val
        sem = self.sem[idx] if kind == "c" else self.dsem[idx]
        self.prog[e].append(("w", sem, val))

    def op(self, e, fn, reads=(), writes=(), dma=False, sig=True, par=False, cc=False):
        reads = list(reads)
        writes = list(writes)
        for b in list(reads):
            if b.psum:
                reads.remove(b)
                if b not in writes:
                    writes.append(b)
        need = {}
        for b in reads:
            for k2, val in b.lw.items():
                if need.get(k2, 0) < val:
                    need[k2] = val
        for b in writes:
            for k2, val in b.lw.items():
                if par and dma and k2[0] == "d":
                    continue
                if need.get(k2, 0) < val:
                    need[k2] = val
            for k2, val in b.rd.items():
                if need.get(k2, 0) < val:
                    need[k2] = val
        for (kind, idx), val in need.items():
            self._wait(e, (kind, idx, val))
        if dma:
            kind_ = "sw" if e == "pool" else "hw"
            i = self.dnext[kind_]
            if kind_ == "hw":
                self.dnext[kind_] = (i + 1) % self.NHW
            else:
                self.dnext[kind_] = self.NHW + (i + 1 - self.NHW) % (self.NDMA - self.NHW)
            if self.dcnt[i] > 0:
                self._wait(e, ("d", i, self.dcnt[i]))
            inc = 1 if cc else 16
            self.dcnt[i] += inc
            self.prog[e].append(("o", fn, self.dsem[i], inc))
            tok = ("d", i, self.dcnt[i])
        elif sig:
            self.cnt[e] += 1
            self.prog[e].append(("o", fn, self.sem[e], 1))
            tok = ("c", e, self.cnt[e])
            self.pend[e] = False
        else:
            self.prog[e].append(("n", fn))
            tok = ("c", e, self.cnt[e] + 1)
            self.pend[e] = True
        self.nins += 1
        for b in reads:
            k = (tok[0], tok[1])
            if b.rd.get(k, 0) < tok[2]:
                b.rd[k] = tok[2]
        for b in writes:
            if par and dma:
                b.lw = {k: v for k, v in b.lw.items() if k[0] == "d"}
                b.lw[(tok[0], tok[1])] = tok[2]
            else:
                b.lw = {(tok[0], tok[1]): tok[2]}
            b.rd = {}
        return tok

    def barrier(self):
        assert not any(self.pend.values()), self.pend
        for i in range(self.NDMA):
            if self.dcnt[i] > 0:
                self._wait("sp", ("d", i, self.dcnt[i]))
        self.bcnt += 1
        self.prog["sp"].append(("s", self.bsem, 1))
        for e in self.names:
            for e2 in self.names:
                if e2 != e and self.cnt[e2] > 0:
                    self._wait(e, ("c", e2, self.cnt[e2]))
            if e != "sp":
                self.prog[e].append(("w", self.bsem, self.bcnt))
                for i in range(self.NDMA):
                    if self.dcnt[i] > 0:
                        self.waited[(e, "d", i)] = max(self.waited.get((e, "d", i), 0), self.dcnt[i])

    def flush(self):
        self.barrier()
        prog = self.prog
        self.prog = {k: [] for k in self.names}

        def mk(name):
            def body(eng):
                items = prog[name]
                n = len(items)
                i = 0
                while i < n:
                    it = items[i]
                    if it[0] == "w":
                        nxt = items[i + 1] if i + 1 < n else None
                        if nxt is not None and nxt[0] in ("o", "n") and ATTACH_WAITS:
                            ins = nxt[1](eng)
                            ins._wait_ge(it[1], it[2])
                            if nxt[0] == "o":
                                ins.then_inc(nxt[2], nxt[3])
                            i += 2
                            continue
                        eng.wait_ge(it[1], it[2])
                    elif it[0] == "n":
                        it[1](eng)
                    elif it[0] == "s":
                        eng.sem_inc(it[1], it[2])
                    else:
                        it[1](eng).then_inc(it[2], it[3])
                    i += 1
            return body

        with self.nc.Block() as block:
            block.sync(mk("sp"))
            block.tensor(mk("pe"))
            block.scalar(mk("act"))
            block.vector(mk("dve"))
            block.gpsimd(mk("pool"))


def _t5_bucket(rel):
    nb = 16
    max_exact = 8
    n = np.abs(rel)
    large = max_exact + (np.log(np.maximum(n, 1) / max_exact) / np.log(1024 / max_exact) * (nb - max_exact)).astype(np.int32)
    large = np.minimum(large, nb - 1)
    return (np.where(rel > 0, nb, 0) + np.where(n < max_exact, n, large)).astype(np.int32)


def _gather_dil(t5_bias):
    out = np.full((8, 128, 3, 256), NEG, np.float32)
    kk = np.arange(128)[:, None]
    cc = np.arange(256)[None, :]
    rel = kk + 64 - cc
    valid = np.abs(rel) <= 64
    for pi, d in enumerate((1, 4, 16)):
        bkt = _t5_bucket(d * rel)
        for h in range(8):
            vals = t5_bias[bkt, h]
            out[h, :, pi, :] = np.where(valid, vals, np.float32(NEG))
    return np.ascontiguousarray(out.reshape(8, 128, 768))


def _rs(r):
    return min(max(r - 4, 0), 56)


NA_SBREP = [0, 3, 7]


def _na_slot(tab, slot):
    sb = NA_SBREP[tab]
    m = 4 * sb - 2 + slot
    if m < 0 or m > 31:
        return None
    rows = [rl for rl in range(8) if _rs(8 * sb + rl) <= 2 * m + 1 and _rs(8 * sb + rl) + 7 >= 2 * m]
    if not rows:
        return None
    assert rows == list(range(rows[0], rows[-1] + 1))
    return rows[0], rows[-1]


def _gather_na(rpb):
    out = np.full((8, 3, 128, 8, 8, 64), NEG, np.float32)
    kk = np.arange(128)
    j = np.arange(64)
    cs = np.clip(j - 8, 0, 48)
    kc = kk % 64
    colok = (kc[:, None] >= cs[None, :]) & (kc[:, None] <= cs[None, :] + 15)
    dc = np.clip(kc[:, None] - j[None, :] + 15, 0, 30)
    for tab in range(3):
        sb = NA_SBREP[tab]
        for slot in range(8):
            rr = _na_slot(tab, slot)
            if rr is None:
                continue
            m = 4 * sb - 2 + slot
            kr = 2 * m + kk // 64
            for rl in range(rr[0], rr[1] + 1):
                r = 8 * sb + rl
                rs = _rs(r)
                valid = ((kr >= rs) & (kr <= rs + 7))[:, None] & colok
                dr = np.clip(kr - r + 7, 0, 14)
                for h in range(8):
                    vals = rpb[h][dr[:, None], dc]
                    out[h, tab, :, slot, rl, :] = np.where(valid, vals, np.float32(NEG))
    return np.ascontiguousarray(out.reshape(8, 3, 128, 4096))


def _etab():
    e = np.zeros((128, NE), np.float32)
    for dirn in range(2):
        rows = slice(dirn * 64, dirn * 64 + 64)
        sg = -1.0 if dirn == 0 else 1.0
        for s in range(8):
            e[rows, s] = sg * s
        for m in range(5):
            for t in range(8):
                e[rows, 8 + m * 8 + t] = 8 * m + (t if dirn == 0 else -t)
        for jj in range(4):
            for s in range(8):
                e[rows, 48 + jj * 8 + s] = (32 - 8 * jj - s) if dirn == 0 else (8 * jj + s)
        e[rows, 80] = 1.0
        e[rows, 81] = 32.0
        e[rows, 82] = 64.0
    return e


def _masks():
    sp = np.arange(128) // 16
    mge = (sp[None, :] >= sp[:, None]).astype(np.float32)
    mle = (sp[None, :] <= sp[:, None]).astype(np.float32)
    return mge, mle


def build_program(nlayers=2, dbg=None, do_na=True, do_dil=True, do_ssm=True, npairs=4):
    nc = bass.Bass("TRN2", target_bir_lowering=False)

    def din(name, shape, dt=F32):
        return nc.dram_tensor(name, list(shape), dt, kind="ExternalInput").ap()

    x_d = din("x", [S_LEN, D])
    normg_d = din("norm_g", [2, D])
    finalg_d = din("final_g", [D])
    win_d = din("w_in", [2, D, 2560])
    rgroups = [[2 * i, 2 * i + 1] for i in range(npairs)]
    XA0, ZA0, NAQ, NAK, NAV, NAZ, DQ, DK, DV, DZ = 0, 256, 512, 768, 1024, 1280, 1536, 1792, 2048, 2304
    wout_d = din("w_out", [2, 1536, D])
    ident_d = din("ident", [128, 128])
    mge_d = din("mge", [128, 128])
    mle_d = din("mle", [128, 128])
    etab_d = din("etab", [128, NE])
    lre_d = din("lam_re_t", [2, 128, 16])
    lim_d = din("lam_im_t", [2, 128, 16])
    ldt_d = din("logdt_t", [2, 128, 16])
    bre_d = din("b_re_t", [2, 128, 256])
    bim_d = din("b_im_t", [2, 128, 256])
    cre_d = din("c_re_t", [2, 128, 256])
    cim_d = din("c_im_t", [2, 128, 256])
    dsk_d = din("dsk_t", [2, 128, 16])
    gluw_d = din("glu_w", [2, 512, 256])
    glub_d = din("glub_t", [2, 128, 2])
    gna_d = din("gna", [2, 4, 3, 128, 4096])
    gdil_d = din("gdil", [4, 128, 768])
    out_d = nc.dram_tensor("out", [S_LEN, D], F32, kind="ExternalOutput").ap()
    sk = "ExternalOutput" if dbg else "Internal"
    x1_d = nc.dram_tensor("x1s", [S_LEN, D], F32, kind=sk).ap()
    ysc_t = [nc.dram_tensor("ysc%d" % i, [128, S_LEN], BF16, kind="Internal").ap() for i in range(6)]
    ygat_t = [nc.dram_tensor("ygat%d" % i, [256, S_LEN], BF16, kind="Internal", addr_space="Local").ap() for i in range(6)]
    usc_d = nc.dram_tensor("usc", [2, 8, 16, 8, 512], BF16, kind="Internal").ap()
    ys2_t = [nc.dram_tensor("ys2_%d" % i, [128, S_LEN], BF16, kind="Internal").ap() for i in range(2)]
    ys2g_t = [nc.dram_tensor("ys2g%d" % i, [256, S_LEN], BF16, kind="Internal", addr_space="Local").ap() for i in range(2)]
    zsc_d = nc.dram_tensor("zsc", [2, 128, S_LEN], BF16, kind="Internal").ap()
    yscB = [Buf(None, "yscB%d" % i) for i in range(6)]
    ygatB = [Buf(None, "ygatB%d" % i) for i in range(6)]
    ys2B = [Buf(None, "ys2B%d" % i) for i in range(2)]
    ys2gB = [Buf(None, "ys2gB%d" % i) for i in range(2)]

    with ExitStack() as es:
        S = Sched(nc, es)
        PS = [S.ps("ps%d" % i, [128, 512], F32) for i in range(7)]
        PSB = S.ps("psb", [128, 1024], BF16)
        identf = S.sb("identf", [128, 128], F32)
        identb = S.sb("identb", [128, 128], BF16)
        onesb = S.sb("onesb", [128, 128], BF16)
        zerosb = S.sb("zerosb", [128, 128], BF16)
        mge = S.sb("mge", [128, 128], F32)
        mle = S.sb("mle", [128, 128], F32)
        wst = [S.sb("wst%d" % i, [128, 8 * 128], F32) for i in range(2)]
        wbb = [S.sb("wbb%d" % i, [128, 8 * 128], BF16) for i in range(2)]
        state = {"wi": 0, "ps": 0, "ev": 0}

        def dma(out, in_, reads=(), writes=(), q="sp", par=False):
            return S.op(q, lambda e: e.dma_start(out=out, in_=in_), reads=reads, writes=writes, dma=True, par=par)

        def allgather(src_ap, dst_ap, srcB, dstB):
            return S.op("pool", lambda e: e.collective_compute("AllGather", ALU.bypass, replica_groups=rgroups, ins=[src_ap], outs=[dst_ap]),
                        reads=[srcB], writes=[dstB], dma=True, cc=True)

        dma(identf[:], ident_d, writes=[identf])
        dma(mge[:], mge_d, writes=[mge])
        dma(mle[:], mle_d, writes=[mle])
        S.op("dve", lambda e: e.tensor_copy(out=identb[:], in_=identf[:]), reads=[identf], writes=[identb])
        S.op("dve", lambda e: e.memset(onesb[:], 1.0), writes=[onesb])
        S.op("dve", lambda e: e.memset(zerosb[:], 0.0), writes=[zerosb])

        def emit_pipeline(tasks, depth=2):
            queue = []
            for t in tasks:
                if t["init"] is not None:
                    t["init"]()
                t["S"]()
                t["E"]()
                queue.append(t)
                if len(queue) > depth:
                    p = queue.pop(0)
                    p["PV"]()
                    if p["fin"] is not None:
                        p["fin"]()
            for p in queue:
                p["PV"]()
                if p["fin"] is not None:
                    p["fin"]()

        wseq = []
        for l_ in range(nlayers):
            if do_na:
                for hp_ in range(2):
                    wseq += [(l_, NAQ + hp_ * 128), (l_, NAK + hp_ * 128), (l_, NAZ + hp_ * 128), (l_, NAV + hp_ * 128)]
            if do_dil:
                for hp_ in range(2):
                    wseq += [(l_, DQ + hp_ * 128), (l_, DK + hp_ * 128), (l_, DZ + hp_ * 128), (l_, DV + hp_ * 128)]
            if do_ssm:
                for ct_ in range(2):
                    wseq += [(l_, XA0 + ct_ * 128), (l_, ZA0 + ct_ * 128)]
        state["wptr"] = 0
        state["wissued"] = 0

        state["pcast"] = []

        def issue_w(k):
            l_, col0 = wseq[k]
            i = k % 2
            src = win_d[l_, :, col0:col0 + 128].rearrange("(j p) c -> p j c", p=128)
            dma(wst[i][:].rearrange("p (j c) -> p j c", j=8), src, writes=[wst[i]])
            state["pcast"].append(k)

        def do_cast():
            while state["pcast"]:
                k = state["pcast"].pop(0)
                i = k % 2
                if k % 2 == 0:
                    S.op("dve", lambda e, i=i: e.tensor_copy(out=wbb[i][:], in_=wst[i][:]), reads=[wst[i]], writes=[wbb[i]])
                else:
                    S.op("act", lambda e, i=i: e.activation(out=wbb[i][:], in_=wst[i][:], func=AF.Copy), reads=[wst[i]], writes=[wbb[i]])

        def load_w(l, col0):
            k = state["wptr"]
            assert wseq[k] == (l, col0), (wseq[k], l, col0)
            while state["wissued"] <= k:
                issue_w(state["wissued"])
                state["wissued"] += 1
            do_cast()
            if state["wissued"] <= min(k + 1, len(wseq) - 1):
                issue_w(state["wissued"])
                state["wissued"] += 1
            state["wptr"] = k + 1
            return wbb[k % 2]

        def evac_copy(out_ap, in_ap, rd, wr, scale=None):
            state["ev"] ^= 1
            if scale is not None or state["ev"]:
                sc = 1.0 if scale is None else scale
                S.op("act", lambda e: e.activation(out=out_ap, in_=in_ap, func=AF.Copy, scale=sc), reads=rd, writes=wr)
            else:
                S.op("dve", lambda e: e.tensor_copy(out=out_ap, in_=in_ap), reads=rd, writes=wr)

        for l in range(nlayers):
            xsrc = x_d if l == 0 else x1_d
            last = (l == nlayers - 1)
            with ExitStack() as scH:
                S.scope = scH
                hT = S.sb("hT", [128, 8 * S_LEN], BF16)
                hT3 = hT[:].rearrange("p (j t) -> p j t", j=8)

                with ExitStack() as ph:
                    S.scope = ph
                    gbc = S.sb("gbc", [128, D], F32)
                    xts = [S.sb("xt%d" % i, [128, D], F32) for i in range(4)]
                    junk = S.sb("junk", [128, D], BF16)
                    hbs = [S.sb("hb%d" % i, [128, D], BF16) for i in range(2)]
                    sts = [S.sb("st%d" % i, [128, 4 * 8], F32) for i in range(4)]
                    dma(gbc[:], normg_d[l].partition_broadcast(128), writes=[gbc])
                    for st_ in sts:
                        S.op("dve", lambda e, st_=st_: e.memset(st_[:], 0.0), writes=[st_])
                    for tt in range(32):
                        xt = xts[tt % 4]
                        hb = hbs[tt % 2]
                        st = sts[tt % 4]
                        c0 = (tt // 4) * 4
                        dma(xt[:], xsrc[tt * 128:(tt + 1) * 128, :], writes=[xt], q="sp")
                        S.op("act", lambda e, xt=xt, c0=c0, st=st: e.activation(out=junk[:], in_=xt[:], func=AF.Square, accum_out=st[:, c0:c0 + 1]),
                             reads=[xt], writes=[junk, st])
                        S.op("dve", lambda e, c0=c0, st=st: e.tensor_scalar(out=st[:, c0 + 1:c0 + 2], in0=st[:, c0:c0 + 1], scalar1=1.0 / D, scalar2=1e-6,
                                                                    op0=ALU.mult, op1=ALU.add), reads=[st], writes=[st])
                        S.op("act", lambda e, c0=c0, st=st: e.activation(out=st[:, c0 + 2:c0 + 3], in_=st[:, c0 + 1:c0 + 2], func=AF.Sqrt), reads=[st], writes=[st])
                        S.op("dve", lambda e, c0=c0, st=st: e.reciprocal(out=st[:, c0 + 3:c0 + 4], in_=st[:, c0 + 2:c0 + 3]), reads=[st], writes=[st])
                        S.op("dve", lambda e, xt=xt, hb=hb, c0=c0, st=st: e.scalar_tensor_tensor(out=hb[:], in0=xt[:], scalar=st[:, c0 + 3:c0 + 4], in1=gbc[:],
                                                                                          op0=ALU.mult, op1=ALU.mult), reads=[xt, st, gbc], writes=[hb])
                        for j in range(8):
                            S.op("pe", lambda e, hb=hb, j=j: e.transpose(out=PSB[:, j * 128:(j + 1) * 128], in_=hb[:, j * 128:(j + 1) * 128], identity=identb[:]),
                                 reads=[hb, identb], writes=[PSB], sig=(j == 7))
                        evac_copy(hT3[:, :, tt * 128:(tt + 1) * 128], PSB[:].rearrange("p (j t) -> p j t", j=8), [PSB], [hT])
                    S.flush()
                S.scope = scH

                def proj_fm(col0, evac):
                    wb = load_w(l, col0)
                    wb3 = wb[:].rearrange("p (j c) -> p j c", j=8)
                    for n in range(8):
                        ps = PS[5 + (n % 2)]
                        for j in range(8):
                            S.op("pe", lambda e, ps=ps, j=j, n=n: e.matmul(ps[:, 0:512], lhsT=wb3[:, j, :], rhs=hT3[:, j, n * 512:(n + 1) * 512],
                                                                           start=(j == 0), stop=(j == 7)), reads=[wb, hT], writes=[ps], sig=(j == 7))
                        evac(n, ps)
                        if n == 4:
                            do_cast()

                def tiles_from_fm(vT, tiles, vbuf, v3):
                    for g8 in range(0, len(tiles), 8):
                        for q in range(8):
                            tsl = tiles[g8 + q]
                            S.op("pe", lambda e, q=q, tsl=tsl: e.transpose(out=PSB[:, q * 128:(q + 1) * 128], in_=vT[:, tsl], identity=identb[:]),
                                 reads=[vT, identb], writes=[PSB], sig=(q == 7))
                        evac_copy(v3[:, g8:g8 + 8, :], PSB[:].rearrange("p (q c) -> p q c", q=8), [PSB], [vbuf])

                def proj_tm(col0, tiles, vbuf, v3, wb=None):
                    if wb is None:
                        wb = load_w(l, col0)
                    wb3 = wb[:].rearrange("p (j c) -> p j c", j=8)
                    for g4 in range(0, len(tiles), 4):
                        ps = PS[5 + ((g4 // 4) % 2)]
                        for q in range(4):
                            tsl = tiles[g4 + q]
                            for j in range(8):
                                S.op("pe", lambda e, ps=ps, j=j, q=q, tsl=tsl: e.matmul(ps[:, q * 128:(q + 1) * 128], lhsT=hT3[:, j, tsl], rhs=wb3[:, j, :],
                                                                                       start=(j == 0), stop=(j == 7)), reads=[wb, hT], writes=[ps], sig=(j == 7 and q == 3))
                        evac_copy(v3[:, g4:g4 + 4, :], ps[:, 0:512].rearrange("p (q c) -> p q c", q=4), [ps], [vbuf])
                        if g4 == 16:
                            do_cast()

                def attn_common(ph, colq, colk, colz, split_q=False):
                    kT = S.sb("kT", [128, S_LEN], BF16)
                    zT = S.sb("zT", [128, S_LEN], BF16)
                    yT = S.sb("yT", [128, S_LEN], BF16)
                    if split_q:
                        qz = [S.sb("qz%d" % i, [128, S_LEN], BF16) for i in range(2)]
                        S.op("pool", lambda e: e.memset(qz[0][64:128, :], 0.0), writes=[qz[0]])
                        S.op("pool", lambda e: e.memset(qz[1][0:64, :], 0.0), writes=[qz[1]])

                        def evq(n, ps):
                            for hh_ in range(2):
                                rs_ = slice(64 * hh_, 64 * hh_ + 64)
                                S.op("act", lambda e, hh_=hh_, rs_=rs_: e.activation(out=qz[hh_][rs_, n * 512:(n + 1) * 512], in_=ps[rs_, 0:512], func=AF.Copy, scale=0.125),
                                     reads=[ps], writes=[qz[hh_]])
                        proj_fm(colq, evq)
                        qT = qz
                    else:
                        qT = S.sb("qT", [128, S_LEN], BF16)
                        proj_fm(colq, lambda n, ps: S.op("act", lambda e: e.activation(out=qT[:, n * 512:(n + 1) * 512], in_=ps[:, 0:512], func=AF.Copy, scale=0.125),
                                                         reads=[ps], writes=[qT]))
                    proj_fm(colk, lambda n, ps: evac_copy(kT[:, n * 512:(n + 1) * 512], ps[:, 0:512], [ps], [kT]))
                    proj_fm(colz, lambda n, ps: S.op("act", lambda e: e.activation(out=zT[:, n * 512:(n + 1) * 512], in_=ps[:, 0:512], func=AF.Silu),
                                                     reads=[ps], writes=[zT]))
                    return qT, kT, zT, yT

                if do_na:
                    for hp in range(2):
                        with ExitStack() as ph:
                            S.scope = ph
                            gsts = [S.sb("gst%d" % i, [128, 4096], F32) for i in range(2)]
                            nbt = [[S.sb("nb%d_%d" % (hh_, t_), [128, 4096], BF16) for t_ in range(3)] for hh_ in range(2)]
                            gorder = [(0, 0), (0, 1), (0, 2), (1, 0), (1, 1), (1, 2)]
                            for i_ in range(2):
                                dma(gsts[i_][:], gna_d[l, hp * 2 + gorder[i_][0], gorder[i_][1]], writes=[gsts[i_]], q="act")
                            qT, kT, zT, yT = attn_common(ph, NAQ + hp * 128, NAK + hp * 128, NAZ + hp * 128)
                            vN = S.sb("vN", [128, 32 * 128], BF16)
                            vN3 = vN[:].rearrange("p (t c) -> p t c", t=32)
                            vT = yT
                            proj_fm(NAV + hp * 128, lambda n, ps: evac_copy(vT[:, n * 512:(n + 1) * 512], ps[:, 0:512], [ps], [vT]))
                            tiles_from_fm(vT, [slice(t * 128, (t + 1) * 128) for t in range(32)], vN, vN3)
                            PT = [S.sb("pt%d" % i, [128, 512], BF16) for i in range(3)]
                            SCB = [PS[0], PS[1], PS[6]]
                            dsb = S.sb("dsb", [128, 512], F32)
                            tnum = S.sb("tnum", [128, 512], F32)
                            for i_, (hh, t_) in enumerate(gorder):
                                gst = gsts[i_ % 2]
                                if i_ >= 2:
                                    dma(gst[:], gna_d[l, hp * 2 + hh, t_], writes=[gst], q="act")
                                S.op("dve", lambda e, hh=hh, t_=t_, gst=gst: e.tensor_copy(out=nbt[hh][t_][:], in_=gst[:]), reads=[gst], writes=[nbt[hh][t_]])
                            tcount = 0
                            tasks = []
                            for hh in range(2):
                                sl = slice(64 * hh, 64 * hh + 64)
                                for sb_ in range(8):
                                    tab = 0 if sb_ == 0 else (2 if sb_ == 7 else 1)
                                    nb = nbt[hh][tab]
                                    psn = PS[2 + sb_ % 2]
                                    psd = PS[4 + sb_ % 2]

                                    def f_init(psn=psn, psd=psd, nb=nb):
                                        for dst in (psn, psd):
                                            S.op("pe", lambda e, dst=dst, nb=nb: e.matmul(dst[:, 0:512], lhsT=zerosb[:], rhs=nb[:, 0:512], start=True, stop=False),
                                                 reads=[zerosb, nb], writes=[dst], sig=False)

                                    def f_fin(psn=psn, psd=psd, sl=sl, tok=slice(sb_ * 512, sb_ * 512 + 512)):
                                        S.op("act", lambda e: e.activation(out=dsb[sl, :], in_=psd[sl, :], func=AF.Copy), reads=[psd], writes=[dsb])
                                        S.op("dve", lambda e: e.reciprocal(out=dsb[sl, :], in_=dsb[sl, :]), reads=[dsb], writes=[dsb])
                                        S.op("dve", lambda e: e.tensor_tensor(out=tnum[sl, :], in0=psn[sl, :], in1=dsb[sl, :], op=ALU.mult), reads=[psn, dsb], writes=[tnum])
                                        S.op("pool", lambda e: e.tensor_tensor(out=yT[sl, tok], in0=tnum[sl, :], in1=zT[sl, tok], op=ALU.mult), reads=[tnum, zT], writes=[yT])

                                    slots = [(slot, _na_slot(tab, slot)) for slot in range(8)]
                                    slots = [(slot, rr) for slot, rr in slots if rr is not None]
                                    for si, (slot, (ra, rb)) in enumerate(slots):
                                        m = 4 * sb_ - 2 + slot
                                        c0 = ra * 64
                                        N = (rb - ra + 1) * 64
                                        q0 = 512 * sb_ + c0
                                        pss = SCB[tcount % 3]
                                        pt = PT[tcount % 3]
                                        tcount += 1
                                        lastt = (si == len(slots) - 1)

                                        def f_S(pss=pss, nb=nb, slot=slot, c0=c0, N=N, m=m, q0=q0, sl=sl):
                                            S.op("pe", lambda e: e.matmul(pss[:, 0:N], lhsT=identb[:], rhs=nb[:, slot * 512 + c0:slot * 512 + c0 + N], start=True, stop=False),
                                                 reads=[identb, nb], writes=[pss], sig=False)
                                            S.op("pe", lambda e: e.matmul(pss[:, 0:N], lhsT=kT[sl, m * 128:(m + 1) * 128], rhs=qT[sl, q0:q0 + N], start=False, stop=True),
                                                 reads=[kT, qT], writes=[pss])

                                        def f_E(pss=pss, pt=pt, N=N):
                                            S.op("act", lambda e: e.activation(out=pt[:, 0:N], in_=pss[:, 0:N], func=AF.Exp), reads=[pss], writes=[pt])

                                        def f_PV(psn=psn, psd=psd, pt=pt, m=m, c0=c0, N=N, lastt=lastt, first=(si == 0)):
                                            S.op("pe", lambda e: e.matmul(psn[:, c0:c0 + N], lhsT=vN3[:, m, :], rhs=pt[:, 0:N], start=first, stop=lastt, skip_group_check=True),
                                                 reads=[vN, pt], writes=[psn], sig=False)
                                            S.op("pe", lambda e: e.matmul(psd[:, c0:c0 + N], lhsT=onesb[:], rhs=pt[:, 0:N], start=first, stop=lastt, skip_group_check=True),
                                                 reads=[onesb, pt], writes=[psd], sig=True)

                                        tasks.append({"init": None, "S": f_S, "E": f_E, "PV": f_PV, "fin": f_fin if lastt else None})
                            emit_pipeline(tasks)
                            dma(ysc_t[2 + hp], yT[:], reads=[yT], writes=[yscB[2 + hp]])
                            allgather(ysc_t[2 + hp], ygat_t[2 + hp], yscB[2 + hp], ygatB[2 + hp])
                            S.flush()
                        S.scope = scH

                if do_dil:
                    for hp in range(2):
                        with ExitStack() as ph:
                            S.scope = ph
                            qT, kT, zT, yT = attn_common(ph, DQ + hp * 128, DK + hp * 128, DZ + hp * 128, split_q=True)
                            vD, vD3 = [], []
                            vT = yT
                            proj_fm(DV + hp * 128, lambda n, ps: evac_copy(vT[:, n * 512:(n + 1) * 512], ps[:, 0:512], [ps], [vT]))
                            for pi, d in enumerate((1, 4, 16)):
                                vb = S.sb("vD%d" % pi, [128, 32 * 128], BF16)
                                v3 = vb[:].rearrange("p (t c) -> p t c", t=32)
                                nsub = 32 // d
                                tiles = []
                                for r in range(d):
                                    for m in range(nsub):
                                        s0 = r + d * 128 * m
                                        tiles.append(slice(s0, s0 + d * 127 + 1, d))
                                tiles_from_fm(vT, tiles, vb, v3)
                                vD.append(vb)
                                vD3.append(v3)
                            gst = S.sb("gstd", [128, 768], F32)
                            dbs = [S.sb("db%d" % i, [128, 768], BF16) for i in range(2)]
                            PT = [S.sb("ptd%d" % i, [128, 256], BF16) for i in range(3)]
                            SCB = [PS[0], PS[1], PS[6]]
                            accn = S.sb("accn", [128, S_LEN], F32)
                            accd = S.sb("accd", [128, S_LEN], F32)
                            for hh in range(2):
                                dma(gst[:], gdil_d[hp * 2 + hh], writes=[gst])
                                S.op("pool", lambda e, hh=hh: e.tensor_copy(out=dbs[hh][:], in_=gst[:]), reads=[gst], writes=[dbs[hh]])
                            combo = 0
                            tasks = []
                            for hh in range(2):
                                db = dbs[hh]
                                sl = slice(64 * hh, 64 * hh + 64)
                                nseg_total = 8 + 8 + 16
                                segc = 0
                                for pi, d in enumerate((1, 4, 16)):
                                    n = S_LEN // d
                                    L = min(512, n)
                                    nsub = n // 128
                                    segi = 0
                                    for r in range(d):
                                        for s0 in range(0, n, L):
                                            psn = PS[2 + segi % 2]
                                            psd = PS[4 + segi % 2]
                                            segi += 1
                                            segc += 1
                                            last_seg_of_head = (segc == nseg_total)

                                            def f_init(psn=psn, psd=psd, db=db, L=L):
                                                for dst in (psn, psd):
                                                    S.op("pe", lambda e, dst=dst: e.matmul(dst[:, 0:L], lhsT=zerosb[:], rhs=db[:, 0:L], start=True, stop=False),
                                                         reads=[zerosb, db], writes=[dst], sig=False)

                                            def f_fin(psn=psn, psd=psd, sl=sl, pi=pi, L=L, d=d, t0_=r + d * s0, last_seg_of_head=last_seg_of_head):
                                                for acc, psrc in ((accn, psn), (accd, psd)):
                                                    view = acc[sl, t0_:t0_ + d * (L - 1) + 1:d]
                                                    pview = psrc[sl, 0:L]
                                                    if pi == 0:
                                                        evac_copy(view, pview, [psrc], [acc])
                                                    else:
                                                        S.op("dve", lambda e, view=view, pview=pview: e.tensor_tensor(out=view, in0=view, in1=pview, op=ALU.add),
                                                             reads=[psrc, acc], writes=[acc])
                                                if pi == 2:
                                                    tsl = slice(t0_, t0_ + d * (L - 1) + 1, d)
                                                    S.op("dve", lambda e: e.reciprocal(out=accd[sl, tsl], in_=accd[sl, tsl]), reads=[accd], writes=[accd])
                                                    S.op("dve", lambda e: e.tensor_tensor(out=accn[sl, tsl], in0=accn[sl, tsl], in1=accd[sl, tsl], op=ALU.mult), reads=[accn, accd], writes=[accn])
                                                    S.op("pool", lambda e: e.tensor_tensor(out=yT[sl, tsl], in0=accn[sl, tsl], in1=zT[sl, tsl], op=ALU.mult), reads=[accn, zT], writes=[yT])

                                            ms = [m for m in range(nsub) if max(s0, 128 * m - 64) < min(s0 + L, 128 * m + 192)]
                                            for mi, m in enumerate(ms):
                                                qa = max(s0, 128 * m - 64)
                                                qb = min(s0 + L, 128 * m + 192)
                                                N = qb - qa
                                                cb = qa - (128 * m - 64)
                                                ks = slice(r + d * 128 * m, r + d * (128 * m + 127) + 1, d)
                                                qs = slice(r + d * qa, r + d * (qb - 1) + 1, d)
                                                pss = SCB[combo % 3]
                                                pt = PT[combo % 3]
                                                combo += 1
                                                lastt = (mi == len(ms) - 1)
                                                o0 = qa - s0
                                                tix = r * nsub + m

                                                def f_S(pss=pss, db=db, pi=pi, cb=cb, N=N, ks=ks, qs=qs, qh=qT[hh]):
                                                    S.op("pe", lambda e: e.matmul(pss[:, 0:N], lhsT=identb[:], rhs=db[:, pi * 256 + cb:pi * 256 + cb + N], start=True, stop=False),
                                                         reads=[identb, db], writes=[pss], sig=False)
                                                    S.op("pe", lambda e: e.matmul(pss[:, 0:N], lhsT=kT[:, ks], rhs=qh[:, qs], start=False, stop=True), reads=[kT, qh], writes=[pss])

                                                def f_E(pss=pss, pt=pt, N=N):
                                                    S.op("act", lambda e: e.activation(out=pt[:, 0:N], in_=pss[:, 0:N], func=AF.Exp), reads=[pss], writes=[pt])

                                                def f_PV(psn=psn, psd=psd, pt=pt, tix=tix, pi=pi, o0=o0, N=N, lastt=lastt, first=(mi == 0)):
                                                    S.op("pe", lambda e: e.matmul(psn[:, o0:o0 + N], lhsT=vD3[pi][:, tix, :], rhs=pt[:, 0:N], start=first, stop=lastt, skip_group_check=True),
                                                         reads=[vD[pi], pt], writes=[psn], sig=False)
                                                    S.op("pe", lambda e: e.matmul(psd[:, o0:o0 + N], lhsT=onesb[:], rhs=pt[:, 0:N], start=first, stop=lastt, skip_group_check=True),
                                                         reads=[onesb, pt], writes=[psd], sig=True)

                                                tasks.append({"init": None, "S": f_S, "E": f_E, "PV": f_PV, "fin": f_fin if lastt else None})
                            emit_pipeline(tasks)
                            dma(ysc_t[4 + hp], yT[:], reads=[yT], writes=[yscB[4 + hp]])
                            allgather(ysc_t[4 + hp], ygat_t[4 + hp], yscB[4 + hp], ygatB[4 + hp])
                            S.flush()
                        S.scope = scH

                if do_ssm:
                    with ExitStack() as ph:
                        S.scope = ph
                        ups = [S.sb("up%d" % i, [128, 8 * 512], BF16) for i in range(2)]
                        zts = [S.sb("zt%d" % i, [128, S_LEN], BF16) for i in range(2)]
                        for ct in range(2):
                            up = ups[ct % 2]
                            up3 = up[:].rearrange("p (s m) -> p s m", s=8)
                            proj_fm(XA0 + ct * 128, lambda n, ps, up=up, up3=up3: evac_copy(up3[:, :, n * 64:(n + 1) * 64],
                                                                                   ps[:, 0:512].rearrange("p (m s) -> p s m", s=8), [ps], [up]))
                            dma(usc_d[ct].rearrange("g c s m -> (g c) s m"), up3, reads=[up])
                            zt = zts[ct % 2]
                            proj_fm(ZA0 + ct * 128, lambda n, ps, zt=zt: S.op("act", lambda e: e.activation(out=zt[:, n * 512:(n + 1) * 512], in_=ps[:, 0:512], func=AF.Silu),
                                                                              reads=[ps], writes=[zt]))
                            dma(zsc_d[ct], zt[:], reads=[zt])
                        S.flush()
                    S.scope = scH
            S.scope = es

            if do_ssm:
                with ExitStack() as scS:
                    S.scope = scS
                    lre = S.sb("lre", [128, 16], F32)
                    lim = S.sb("lim", [128, 16], F32)
                    ldt = S.sb("ldt", [128, 16], F32)
                    ar = S.sb("ar", [128, 16], F32)
                    ai = S.sb("ai", [128, 16], F32)
                    bre = S.sb("bre", [128, 256], F32)
                    bim = S.sb("bim", [128, 256], F32)
                    cre = S.sb("cre", [128, 256], F32)
                    cim = S.sb("cim", [128, 256], F32)
                    dsk = S.sb("dsk", [128, 16], F32)
                    etab = S.sb("etab", [128, NE], F32)
                    for t, d_ in ((lre, lre_d), (lim, lim_d), (ldt, ldt_d), (bre, bre_d), (bim, bim_d), (cre, cre_d), (cim, cim_d), (dsk, dsk_d)):
                        dma(t[:], d_[l], writes=[t])
                    dma(etab[:], etab_d, writes=[etab])
                    S.op("act", lambda e: e.activation(out=ldt[:], in_=ldt[:], func=AF.Exp), reads=[ldt], writes=[ldt])
                    S.op("dve", lambda e: e.tensor_tensor(out=ar[:], in0=lre[:], in1=ldt[:], op=ALU.mult), reads=[lre, ldt], writes=[ar])
                    S.op("dve", lambda e: e.tensor_tensor(out=ai[:], in0=lim[:], in1=ldt[:], op=ALU.mult), reads=[lim, ldt], writes=[ai])
                    YPs = []
                    for half in range(1):
                        G0 = 0
                        gs = slice(G0, G0 + 16)
                        with ExitStack() as scHf:
                            S.scope = scHf
                            NW = 16 * NE
                            zr = S.sb("zr", [128, NW], F32)
                            zi = S.sb("zi", [128, NW], F32)
                            scT = ExitStack()
                            S.scope = scT
                            tA = S.sb("tA", [128, NW], F32)
                            tB = S.sb("tB", [128, NW], F32)
                            tC = S.sb("tC", [128, NW], F32)
                            tD = S.sb("tD", [128, NW], F32)
                            tI = S.sb("tI", [128, NW], mybir.dt.int32)

                            def v3(b):
                                return b[:].rearrange("p (g e) -> p g e", g=16)

                            ar_b = ar[:, gs].unsqueeze(2).to_broadcast([128, 16, NE])
                            ai_b = ai[:, gs].unsqueeze(2).to_broadcast([128, 16, NE])
                            et_b = etab[:].unsqueeze(1).to_broadcast([128, 16, NE])
                            S.op("dve", lambda e: e.tensor_tensor(out=v3(tA), in0=ar_b, in1=et_b, op=ALU.mult), reads=[ar, etab], writes=[tA])
                            S.op("act", lambda e: e.activation(out=tA[:], in_=tA[:], func=AF.Exp), reads=[tA], writes=[tA])
                            S.op("dve", lambda e: e.tensor_tensor(out=v3(tB), in0=ai_b, in1=et_b, op=ALU.mult), reads=[ai, etab], writes=[tB])
                            S.op("dve", lambda e: e.tensor_scalar(out=tB[:], in0=tB[:], scalar1=1.0 / TWO_PI, scalar2=None, op0=ALU.mult), reads=[tB], writes=[tB])

                            def sin_of(dst, shift):
                                S.op("dve", lambda e: e.tensor_scalar(out=tC[:], in0=tB[:], scalar1=shift, scalar2=None, op0=ALU.add), reads=[tB], writes=[tC])
                                S.op("dve", lambda e: e.tensor_copy(out=tI[:], in_=tC[:]), reads=[tC], writes=[tI])
                                S.op("dve", lambda e: e.tensor_copy(out=tD[:], in_=tI[:]), reads=[tI], writes=[tD])
                                S.op("dve", lambda e: e.tensor_tensor(out=tC[:], in0=tC[:], in1=tD[:], op=ALU.subtract), reads=[tC, tD], writes=[tC])
                                S.op("dve", lambda e: e.tensor_scalar(out=tD[:], in0=tC[:], scalar1=0.5, scalar2=None, op0=ALU.is_gt), reads=[tC], writes=[tD])
                                S.op("dve", lambda e: e.tensor_tensor(out=tC[:], in0=tC[:], in1=tD[:], op=ALU.subtract), reads=[tC, tD], writes=[tC])
                                S.op("dve", lambda e: e.tensor_scalar(out=tD[:], in0=tC[:], scalar1=-0.5, scalar2=None, op0=ALU.is_lt), reads=[tC], writes=[tD])
                                S.op("dve", lambda e: e.tensor_tensor(out=tC[:], in0=tC[:], in1=tD[:], op=ALU.add), reads=[tC, tD], writes=[tC])
                                S.op("act", lambda e: e.activation(out=dst[:], in_=tC[:], func=AF.Sin, scale=TWO_PI), reads=[tC], writes=[dst])

                            sin_of(zi, 0.0)
                            sin_of(zr, 0.25)
                            S.op("dve", lambda e: e.tensor_tensor(out=zr[:], in0=zr[:], in1=tA[:], op=ALU.mult), reads=[zr, tA], writes=[zr])
                            S.op("dve", lambda e: e.tensor_tensor(out=zi[:], in0=zi[:], in1=tA[:], op=ALU.mult), reads=[zi, tA], writes=[zi])
                            S.flush()
                            scT.close()
                            S.scope = scHf
                            zr3 = v3(zr)
                            zi3 = v3(zi)
                            cf = S.sb("cf", [128, 16 * 8], F32)
                            cf3 = cf[:].rearrange("p (k g) -> p k g", k=8)
                            lre_h = lre[:, gs]
                            lim_h = lim[:, gs]

                            def cop(o, a, b, op):
                                S.op("dve", lambda e: e.tensor_tensor(out=o, in0=a, in1=b, op=op), reads=[zr, zi, lre, lim, cf], writes=[cf])

                            S.op("dve", lambda e: e.tensor_scalar(out=cf3[:, 0, :], in0=zr3[:, :, 80], scalar1=-1.0, scalar2=None, op0=ALU.add), reads=[zr], writes=[cf])
                            cop(cf3[:, 1, :], lre_h, lre_h, ALU.mult)
                            cop(cf3[:, 2, :], lim_h, lim_h, ALU.mult)
                            cop(cf3[:, 1, :], cf3[:, 1, :], cf3[:, 2, :], ALU.add)
                            S.op("dve", lambda e: e.reciprocal(out=cf3[:, 1, :], in_=cf3[:, 1, :]), reads=[cf], writes=[cf])
                            cop(cf3[:, 2, :], cf3[:, 0, :], lre_h, ALU.mult)
                            cop(cf3[:, 3, :], zi3[:, :, 80], lim_h, ALU.mult)
                            cop(cf3[:, 2, :], cf3[:, 2, :], cf3[:, 3, :], ALU.add)
                            cop(cf3[:, 4, :], cf3[:, 2, :], cf3[:, 1, :], ALU.mult)
                            cop(cf3[:, 2, :], zi3[:, :, 80], lre_h, ALU.mult)
                            cop(cf3[:, 3, :], cf3[:, 0, :], lim_h, ALU.mult)
                            cop(cf3[:, 2, :], cf3[:, 2, :], cf3[:, 3, :], ALU.subtract)
                            cop(cf3[:, 5, :], cf3[:, 2, :], cf3[:, 1, :], ALU.mult)
                            Bbr = S.sb("Bbr", [128, 256], F32)
                            Bbi = S.sb("Bbi", [128, 256], F32)
                            tb1 = S.sb("tb1", [128, 256], F32)

                            def g3(ap):
                                return ap.rearrange("p (g c) -> p g c", g=16)

                            qr_b = cf3[:, 4, :].unsqueeze(2).to_broadcast([128, 16, 16])
                            qi_b = cf3[:, 5, :].unsqueeze(2).to_broadcast([128, 16, 16])
                            bre_h = g3(bre[:, G0 * 16:G0 * 16 + 256])
                            bim_h = g3(bim[:, G0 * 16:G0 * 16 + 256])

                            def bop(o, a, b, op, wr):
                                S.op("dve", lambda e: e.tensor_tensor(out=o, in0=a, in1=b, op=op), reads=[cf, bre, bim, tb1, Bbr, Bbi], writes=[wr])

                            bop(g3(Bbr[:]), qr_b, bre_h, ALU.mult, Bbr)
                            bop(g3(tb1[:]), qi_b, bim_h, ALU.mult, tb1)
                            bop(Bbr[:], Bbr[:], tb1[:], ALU.subtract, Bbr)
                            bop(g3(Bbi[:]), qr_b, bim_h, ALU.mult, Bbi)
                            bop(g3(tb1[:]), qi_b, bre_h, ALU.mult, tb1)
                            bop(Bbi[:], Bbi[:], tb1[:], ALU.add, Bbi)
                            A4 = S.sb("A4", [128, 64], F32)
                            A44 = A4[:].rearrange("p (g o i) -> p g o i", g=16, o=2)
                            S.op("dve", lambda e: e.tensor_copy(out=A44[:, :, 0, 0], in_=zr3[:, :, 81]), reads=[zr], writes=[A4])
                            S.op("dve", lambda e: e.tensor_copy(out=A44[:, :, 1, 1], in_=zr3[:, :, 81]), reads=[zr], writes=[A4])
                            S.op("dve", lambda e: e.tensor_copy(out=A44[:, :, 1, 0], in_=zi3[:, :, 81]), reads=[zi], writes=[A4])
                            S.op("dve", lambda e: e.tensor_scalar(out=A44[:, :, 0, 1], in0=zi3[:, :, 81], scalar1=-1.0, scalar2=None, op0=ALU.mult), reads=[zi], writes=[A4])

                            A2c = S.sb("A2c", [128, 128], F32)
                            for c_ in range(2):
                                A2v = A2c[:, c_ * 64:(c_ + 1) * 64].rearrange("p (g o i) -> p g o i", g=16, o=2)
                                S.op("dve", lambda e, A2v=A2v: e.tensor_copy(out=A2v[:, :, 0, 0], in_=zr3[:, :, 82]), reads=[zr], writes=[A2c])
                                S.op("dve", lambda e, A2v=A2v: e.tensor_copy(out=A2v[:, :, 1, 1], in_=zr3[:, :, 82]), reads=[zr], writes=[A2c])
                                S.op("dve", lambda e, A2v=A2v: e.tensor_copy(out=A2v[:, :, 1, 0], in_=zi3[:, :, 82]), reads=[zi], writes=[A2c])
                                S.op("dve", lambda e, A2v=A2v: e.tensor_scalar(out=A2v[:, :, 0, 1], in0=zi3[:, :, 82], scalar1=-1.0, scalar2=None, op0=ALU.mult), reads=[zi], writes=[A2c])
                            U = S.sb("U", [128, 16 * 512], BF16)
                            U3 = U[:].rearrange("p (g m) -> p g m", g=16)
                            for gb in range(2):
                                ct = gb
                                for s in range(8):
                                    dma(U3[16 * s:16 * s + 16, gb * 8:gb * 8 + 8, :], usc_d[ct][:, :, s, :].rearrange("g c m -> c g m"), writes=[U], par=True)
                            Hs = S.sb("Hs", [128, 128 * 32], F32)
                            Hs_gr = Hs[:].rearrange("p (k g r) -> p g r k", g=16, r=2)
                            Hs_k = Hs[:].rearrange("p (k g r) -> p k g r", g=16, r=2)
                            Hb = S.sb("Hb", [128, 32 * 128], BF16)
                            Hb4 = Hb[:].rearrange("p (g r k) -> p g r k", g=16, r=2)
                            Cg = g3

                            def cprod(dst_r, dst_i, zr_ap, zi_ap, Xr_ap, Xi_ap, t1, t2, neg_i, rd, wr):
                                S.op("dve", lambda e: e.tensor_tensor(out=t1, in0=zr_ap, in1=Xr_ap, op=ALU.mult), reads=rd, writes=wr)
                                S.op("pool", lambda e: e.tensor_tensor(out=t2, in0=zi_ap, in1=Xi_ap, op=ALU.mult), reads=rd, writes=wr)
                                S.op("dve", lambda e: e.tensor_tensor(out=dst_r, in0=t1, in1=t2, op=ALU.subtract), reads=rd + wr, writes=wr)
                                S.op("dve", lambda e: e.tensor_tensor(out=t1, in0=zr_ap, in1=Xi_ap, op=ALU.mult), reads=rd + wr, writes=wr)
                                S.op("pool", lambda e: e.tensor_tensor(out=t2, in0=zi_ap, in1=Xr_ap, op=ALU.mult), reads=rd + wr, writes=wr)
                                if neg_i:
                                    S.op("dve", lambda e: e.scalar_tensor_tensor(out=dst_i, in0=t1, scalar=-1.0, in1=t2, op0=ALU.mult, op1=ALU.subtract),
                                         reads=rd + wr, writes=wr)
                                else:
                                    S.op("dve", lambda e: e.tensor_tensor(out=dst_i, in0=t1, in1=t2, op=ALU.add), reads=rd + wr, writes=wr)

                            with ExitStack() as scP:
                                S.scope = scP
                                for gb in range(2):
                                    gsl = slice(gb * 8, gb * 8 + 8)
                                    t1 = S.sb("gt1", [128, 1024], F32)
                                    t2 = S.sb("gt2", [128, 1024], F32)
                                    PBr = S.sb("PBr", [128, 1024], BF16)
                                    PBi = S.sb("PBi", [128, 1024], BF16)
                                    Pw = S.sb("Pw", [128, 8 * 4 * 2 * 128], BF16)
                                    Pw5 = Pw[:].rearrange("p (g j r q) -> p g j r q", g=8, j=4, r=2)
                                    grp = [t1, t2, PBr, PBi]

                                    def v4(b):
                                        return b[:].rearrange("p (g s c) -> p g s c", g=8, s=8)

                                    Br_b = g3(Bbr[:])[:, gsl, :].unsqueeze(2).to_broadcast([128, 8, 8, 16])
                                    Bi_b = g3(Bbi[:])[:, gsl, :].unsqueeze(2).to_broadcast([128, 8, 8, 16])
                                    for j in range(4):
                                        zr_b = zr3[:, gsl, 48 + 8 * j:56 + 8 * j].unsqueeze(3).to_broadcast([128, 8, 8, 16])
                                        zi_b = zi3[:, gsl, 48 + 8 * j:56 + 8 * j].unsqueeze(3).to_broadcast([128, 8, 8, 16])
                                        cprod(v4(PBr), v4(PBi), zr_b, zi_b, Br_b, Bi_b, v4(t1), v4(t2), False, [zr, zi, Bbr, Bbi], grp)
                                        for g in range(8):
                                            for ri, PB in enumerate((PBr, PBi)):
                                                slot = (g % 4) * 2 + ri
                                                S.op("pe", lambda e, PB=PB, g=g, slot=slot: e.transpose(out=PSB[:, slot * 128:(slot + 1) * 128], in_=PB[:, g * 128:(g + 1) * 128],
                                                                                                       identity=identb[:]), reads=[PB, identb], writes=[PSB], sig=(g % 4 == 3 and ri == 1))
                                            if g % 4 == 3:
                                                g0 = g - 3
                                                evac_copy(Pw5[:, g0:g0 + 4, j, :, :], PSB[:].rearrange("p (g r q) -> p g r q", g=4, r=2), [PSB], [Pw])
                                    for g in range(8):
                                        gl = gb * 8 + g
                                        ps = PS[5 + ((g // 2) % 2)]
                                        for ri in range(2):
                                            slot = (g % 2) * 2 + ri
                                            for j in range(4):
                                                S.op("pe", lambda e, ps=ps, slot=slot, g=g, j=j, ri=ri, gl=gl, Pw5=Pw5: e.matmul(
                                                    ps[:, slot * 128:(slot + 1) * 128], lhsT=Pw5[:, g, j, ri, :], rhs=U3[:, gl, j:512:4], start=(j == 0), stop=(j == 3)),
                                                    reads=[Pw, U], writes=[ps], sig=(j == 3 and ri == 1 and g % 2 == 1))
                                        if g % 2 == 1:
                                            evac_copy(Hs_gr[:, gl - 1:gl + 1, :, :], ps[:, 0:512].rearrange("p (g r k) -> p g r k", g=2, r=2), [ps], [Hs])
                                S.flush()
                            S.scope = scHf

                            scSc = ExitStack()
                            S.scope = scSc
                            Hf = Buf(Hs.t, "Hf")
                            Hbk = Buf(Hs.t, "Hbk")

                            scn = S.sb("scn", [128, 127 * 64], F32)
                            sc5 = scn[:].rearrange("p (k g o i) -> p k g o i", k=127, g=16, o=2)
                            scF = Buf(scn.t, "scF")
                            scB = Buf(scn.t, "scB")
                            prf2 = S.sb("prf2", [128, 128], F32)
                            prb2 = S.sb("prb2", [128, 128], F32)
                            Hs2 = Hs[:].rearrange("p (kg r) -> p kg r", r=2)

                            def scan_pre(eng, sl, src_lo, dst_lo, Hx, scX):
                                for o in range(2):
                                    a_b = A44[sl][:, :, o, :].unsqueeze(1).to_broadcast([64, 127, 16, 2])
                                    S.op(eng, lambda e, o=o, a_b=a_b: e.tensor_tensor(out=sc5[sl, :, :, o, :], in0=a_b, in1=Hs_k[sl, src_lo:src_lo + 127, :, :], op=ALU.mult),
                                         reads=[A4, Hx], writes=[scX])
                                S.op(eng, lambda e: e.tensor_tensor(out=sc5[sl, :, :, :, 0], in0=sc5[sl, :, :, :, 0], in1=sc5[sl, :, :, :, 1], op=ALU.add), reads=[scX], writes=[scX])
                                S.op(eng, lambda e: e.tensor_tensor(out=Hs_k[sl, dst_lo:dst_lo + 127, :, :], in0=Hs_k[sl, dst_lo:dst_lo + 127, :, :], in1=sc5[sl, :, :, :, 0], op=ALU.add),
                                     reads=[scX], writes=[Hx])

                            def scan_step2(eng, kd, ks, sl, pr, Hx):
                                pr4 = pr[sl, :].rearrange("p (cg o i) -> p cg o i", o=2, i=2)
                                a4 = A2c[sl, :].rearrange("p (cg o i) -> p cg o i", o=2, i=2)
                                X = Hs2[sl, ks * 16:ks * 16 + 32, :].unsqueeze(2).to_broadcast([64, 32, 2, 2])
                                S.op(eng, lambda e: e.tensor_tensor(out=pr4, in0=a4, in1=X, op=ALU.mult), reads=[A2c, Hx], writes=[pr])
                                S.op(eng, lambda e: e.tensor_tensor(out=pr4[:, :, :, 0], in0=pr4[:, :, :, 0], in1=pr4[:, :, :, 1], op=ALU.add), reads=[pr], writes=[pr])
                                S.op(eng, lambda e: e.tensor_tensor(out=Hs2[sl, kd * 16:kd * 16 + 32, :], in0=Hs2[sl, kd * 16:kd * 16 + 32, :], in1=pr4[:, :, :, 0], op=ALU.add),
                                     reads=[pr], writes=[Hx])

                            scan_pre("dve", slice(0, 64), 0, 1, Hf, scF)
                            scan_pre("pool", slice(64, 128), 1, 0, Hbk, scB)
                            for m in range(1, 64):
                                scan_step2("dve", 2 * m, 2 * m - 2, slice(0, 64), prf2, Hf)
                                mb = 63 - m
                                scan_step2("pool", 2 * mb, 2 * mb + 2, slice(64, 128), prb2, Hbk)
                            S.op("dve", lambda e: e.tensor_copy(out=Hb4[0:64], in_=Hs_gr[0:64]), reads=[Hf], writes=[Hb])
                            S.op("dve", lambda e: e.tensor_copy(out=Hb4[64:128], in_=Hs_gr[64:128]), reads=[Hbk, Hb], writes=[Hb])
                            S.flush()
                            scSc.close()
                            S.scope = scHf

                            YP = S.sb("YP", [128, 16 * 512], BF16)
                            YP3 = YP[:].rearrange("p (g m) -> p g m", g=16)
                            YPg = [Buf(YP.t, "YPg0"), Buf(YP.t, "YPg1")]
                            with ExitStack() as scP:
                                S.scope = scP
                                t1 = S.sb("ht1", [128, 1024], F32)
                                t2 = S.sb("ht2", [128, 1024], F32)
                                gxs = [S.sb("gxs%d" % i, [128, 512], F32) for i in range(2)]
                                gus = [S.sb("gus%d" % i, [128, 512], F32) for i in range(2)]
                                for gb in range(2):
                                    gsl = slice(gb * 8, gb * 8 + 8)
                                    PBr = S.sb("PB0r", [128, 1024], BF16)
                                    PBi = S.sb("PB0i", [128, 1024], BF16)
                                    QCr = [S.sb("QCr%d" % m, [128, 1024], BF16) for m in range(5)]
                                    QCi = [S.sb("QCi%d" % m, [128, 1024], BF16) for m in range(5)]
                                    Tw = S.sb("Tw", [128, 8 * 7 * 128], BF16)
                                    Tw4 = Tw[:].rearrange("p (g d q) -> p g d q", g=8, d=7)
                                    tt0 = S.sb("tt0", [128, 128], F32)
                                    tt1 = S.sb("tt1", [128, 128], F32)

                                    def v4(b):
                                        return b[:].rearrange("p (g s c) -> p g s c", g=8, s=8)

                                    Br_b = g3(Bbr[:])[:, gsl, :].unsqueeze(2).to_broadcast([128, 8, 8, 16])
                                    Bi_b = g3(Bbi[:])[:, gsl, :].unsqueeze(2).to_broadcast([128, 8, 8, 16])
                                    zr_b = zr3[:, gsl, 0:8].unsqueeze(3).to_broadcast([128, 8, 8, 16])
                                    zi_b = zi3[:, gsl, 0:8].unsqueeze(3).to_broadcast([128, 8, 8, 16])
                                    cprod(v4(PBr), v4(PBi), zr_b, zi_b, Br_b, Bi_b, v4(t1), v4(t2), False, [zr, zi, Bbr, Bbi], [t1, t2, PBr, PBi])
                                    cre_h = g3(cre[:, G0 * 16:G0 * 16 + 256])[:, gsl, :].unsqueeze(2).to_broadcast([128, 8, 8, 16])
                                    cim_h = g3(cim[:, G0 * 16:G0 * 16 + 256])[:, gsl, :].unsqueeze(2).to_broadcast([128, 8, 8, 16])
                                    for m in range(5):
                                        zr_m = zr3[:, gsl, 8 + 8 * m:16 + 8 * m].unsqueeze(3).to_broadcast([128, 8, 8, 16])
                                        zi_m = zi3[:, gsl, 8 + 8 * m:16 + 8 * m].unsqueeze(3).to_broadcast([128, 8, 8, 16])
                                        cprod(v4(QCr[m]), v4(QCi[m]), zr_m, zi_m, cre_h, cim_h, v4(t1), v4(t2), True, [zr, zi, cre, cim], [t1, t2, QCr[m], QCi[m]])
                                    F_ = slice(0, 64)
                                    B_ = slice(64, 128)
                                    TwB = [Buf(Tw.t, "TwB%d" % g_) for g_ in range(8)]

                                    def tgen(g):
                                        gq = slice(g * 128, (g + 1) * 128)
                                        psX = PS[0 + (g % 2)]
                                        psY = PS[2 + (g % 2)]
                                        jobs = [(psX, 0, F_, 1), (psX, 1, F_, 2), (psX, 2, F_, 3), (psX, 3, F_, 0),
                                                (psY, 0, B_, 1), (psY, 1, B_, 2), (psY, 2, B_, 3), (psY, 3, B_, 0)]
                                        for (pz, slot, dsl, m) in jobs:
                                            S.op("pe", lambda e, pz=pz, slot=slot, dsl=dsl, m=m, gq=gq, PBr=PBr, QCr=QCr: e.matmul(pz[:, slot * 128:(slot + 1) * 128], lhsT=PBr[dsl, gq], rhs=QCr[m][dsl, gq],
                                                                                                               start=True, stop=False), reads=[PBr, QCr[m]], writes=[pz], sig=False)
                                            S.op("pe", lambda e, pz=pz, slot=slot, dsl=dsl, m=m, gq=gq, PBi=PBi, QCi=QCi: e.matmul(pz[:, slot * 128:(slot + 1) * 128], lhsT=PBi[dsl, gq], rhs=QCi[m][dsl, gq],
                                                                                                               start=False, stop=True), reads=[PBi, QCi[m]], writes=[pz], sig=(slot == 3))
                                        evac_copy(Tw4[:, g, 0:3, :], psX[:, 0:384].rearrange("p (d q) -> p d q", d=3), [psX], [TwB[g]])
                                        evac_copy(Tw4[:, g, 3:6, :], psY[:, 0:384].rearrange("p (d q) -> p d q", d=3), [psY], [TwB[g]])
                                        S.op("dve", lambda e, psX=psX, tt0=tt0: e.tensor_tensor(out=tt0[:], in0=psX[:, 384:512], in1=mge[:], op=ALU.mult), reads=[psX, mge], writes=[tt0])
                                        S.op("dve", lambda e, psY=psY, tt1=tt1: e.tensor_tensor(out=tt1[:], in0=psY[:, 384:512], in1=mle[:], op=ALU.mult), reads=[psY, mle], writes=[tt1])
                                        S.op("dve", lambda e, tt0=tt0, tt1=tt1: e.tensor_tensor(out=tt0[:], in0=tt0[:], in1=tt1[:], op=ALU.add), reads=[tt0, tt1], writes=[tt0])
                                        Gg = G0 + gb * 8 + g
                                        S.op("dve", lambda e, g=g, Gg=Gg, Tw4=Tw4, tt0=tt0: e.scalar_tensor_tensor(out=Tw4[:, g, 6, :], in0=identf[:], scalar=dsk[:, Gg:Gg + 1], in1=tt0[:],
                                                                                                 op0=ALU.mult, op1=ALU.add), reads=[identf, dsk, tt0], writes=[TwB[g]])
                                    tidx = {1: 0, 2: 1, 3: 2, -1: 3, -2: 4, -3: 5, 0: 6}
                                    def ycomp(g):
                                        gl = gb * 8 + g
                                        gq = slice(g * 128, (g + 1) * 128)
                                        ps = PS[5 + (g % 2)]
                                        psB = PS[4]
                                        for i in range(4):
                                            o0 = i * 128
                                            for j in range(4):
                                                S.op("pe", lambda e, ps=ps, o0=o0, g=g, i=i, j=j, gl=gl, Tw4=Tw4: e.matmul(ps[:, o0:o0 + 128], lhsT=Tw4[:, g, tidx[i - j], :], rhs=U3[:, gl, j:512:4],
                                                                                                                 start=(j == 0), stop=False), reads=[TwB[g], U], writes=[ps], sig=False)
                                            S.op("pe", lambda e, ps=ps, o0=o0, i=i, gq=gq, gl=gl, QCr=QCr: e.matmul(ps[:, o0 + 1:o0 + 128], lhsT=QCr[i][F_, gq], rhs=Hb4[F_, gl, 0, 0:127],
                                                                                                          start=False, stop=False), reads=[QCr[i], Hb], writes=[ps], sig=False)
                                            S.op("pe", lambda e, ps=ps, o0=o0, i=i, gq=gq, gl=gl, QCi=QCi: e.matmul(ps[:, o0 + 1:o0 + 128], lhsT=QCi[i][F_, gq], rhs=Hb4[F_, gl, 1, 0:127],
                                                                                                          start=False, stop=True), reads=[QCi[i], Hb], writes=[ps], sig=(i == 3))
                                        for i in range(4):
                                            o0 = i * 128
                                            S.op("pe", lambda e, psB=psB, o0=o0, i=i, gq=gq, gl=gl, QCr=QCr: e.matmul(psB[:, o0:o0 + 127], lhsT=QCr[4 - i][B_, gq], rhs=Hb4[B_, gl, 0, 1:128],
                                                                                                            start=True, stop=False), reads=[QCr[4 - i], Hb], writes=[psB], sig=False)
                                            S.op("pe", lambda e, psB=psB, o0=o0, i=i, gq=gq, gl=gl, QCi=QCi: e.matmul(psB[:, o0:o0 + 127], lhsT=QCi[4 - i][B_, gq], rhs=Hb4[B_, gl, 1, 1:128],
                                                                                                            start=False, stop=True), reads=[QCi[4 - i], Hb], writes=[psB], sig=(i == 3))
                                        xs = gxs[g % 2]
                                        us = gus[g % 2]
                                        S.op("act", lambda e, ps=ps, xs=xs: e.activation(out=xs[:], in_=ps[:, 0:512], func=AF.Copy), reads=[ps], writes=[xs])
                                        S.op("dve", lambda e, psB=psB, xs=xs: e.tensor_tensor(out=xs[:].rearrange("p (i k) -> p i k", i=4)[:, :, 0:127],
                                                                                             in0=xs[:].rearrange("p (i k) -> p i k", i=4)[:, :, 0:127],
                                                                                             in1=psB[:, 0:512].rearrange("p (i k) -> p i k", i=4)[:, :, 0:127], op=ALU.add),
                                             reads=[psB, xs], writes=[xs])
                                        S.op("dve", lambda e, xs=xs, us=us: e.tensor_tensor(out=us[:], in0=xs[:], in1=xs[:], op=ALU.mult), reads=[xs], writes=[us])
                                        S.op("dve", lambda e, us=us: e.tensor_scalar(out=us[:], in0=us[:], scalar1=0.044715, scalar2=1.0, op0=ALU.mult, op1=ALU.add), reads=[us], writes=[us])
                                        S.op("dve", lambda e, xs=xs, us=us: e.tensor_tensor(out=us[:], in0=us[:], in1=xs[:], op=ALU.mult), reads=[xs, us], writes=[us])
                                        S.op("act", lambda e, us=us: e.activation(out=us[:], in_=us[:], func=AF.Sigmoid, scale=1.5957691216057308), reads=[us], writes=[us])
                                        S.op("pool", lambda e, xs=xs, us=us, gl=gl: e.tensor_tensor(out=YP3[:, gl, :], in0=xs[:], in1=us[:], op=ALU.mult), reads=[xs, us], writes=[YPg[gb]])

                                    for g in range(8):
                                        tgen(g)
                                        if g >= 1:
                                            ycomp(g - 1)
                                    ycomp(7)
                                    ct = gb
                                    ys2v = ys2_t[ct].rearrange("(g c) (s m) -> g c s m", c=16, s=8)
                                    for s in range(8):
                                        dma(ys2v[:, :, s, :].rearrange("g c m -> c g m"), YP3[16 * s:16 * s + 16, gb * 8:gb * 8 + 8, :], reads=[YPg[gb]], writes=[ys2B[ct]], par=True)
                                    allgather(ys2_t[ct], ys2g_t[ct], ys2B[ct], ys2gB[ct])
                                S.flush()
                            S.scope = scHf
                        S.scope = scS

                    with ExitStack() as scG:
                        S.scope = scG
                        yg = S.sb("yg", [128, 4 * 4096], BF16)
                        yg4 = yg[:].rearrange("p (c t m) -> p c t m", c=4, t=8)
                        for ct in range(4):
                            dma(yg4[:, ct, :, :], ys2g_t[ct % 2][(ct // 2) * 128:(ct // 2 + 1) * 128, :].rearrange("p (t m) -> p t m", t=8), reads=[ys2gB[ct % 2]], writes=[yg], par=True)
                        ygl = S.sb("ygl", [128, 2 * 4096], BF16)
                        ygl4 = ygl[:].rearrange("p (c t m) -> p c t m", c=2, t=8)
                        for ct in range(2):
                            dma(ygl4[:, ct, :, :], ys2_t[ct].rearrange("p (t m) -> p t m", t=8), reads=[ys2B[ct]], writes=[ygl], par=True)
                        gws = S.sb("gws", [128, 4 * 256], F32)
                        gwb = S.sb("gwb", [128, 4 * 256], BF16)
                        gwb3 = gwb[:].rearrange("p (k o) -> p k o", k=4)
                        glb = S.sb("glb", [128, 2], F32)
                        dma(gws[:].rearrange("p (k o) -> p k o", k=4), gluw_d[l].rearrange("(k p) o -> p k o", p=128), writes=[gws])
                        dma(glb[:], glub_d[l], writes=[glb])
                        S.op("pool", lambda e: e.tensor_copy(out=gwb[:], in_=gws[:]), reads=[gws], writes=[gwb])
                        zts = [S.sb("zg%d" % i, [128, S_LEN], BF16) for i in range(2)]
                        yos = [S.sb("yo%d" % i, [128, S_LEN], BF16) for i in range(2)]
                        sgt = [S.sb("sgt%d" % i, [128, 512], BF16) for i in range(2)]
                        tmg = [S.sb("tmg%d" % i, [128, 512], BF16) for i in range(2)]
                        for ot in range(2):
                            zt = zts[ot % 2]
                            yo = yos[ot % 2]
                            dma(zt[:], zsc_d[ot], writes=[zt])
                            ztv = zt[:].rearrange("p (k i t) -> p t i k", i=4, t=8)
                            yov = yo[:].rearrange("p (k i t) -> p t i k", i=4, t=8)
                            for t in range(8):
                                ps = PS[5 + (t % 2)]
                                sg = sgt[t % 2]
                                tm = tmg[t % 2]
                                for kt in range(4):
                                    S.op("pe", lambda e, ps=ps, kt=kt, ot=ot, t=t: e.matmul(ps[:, 0:512], lhsT=gwb3[:, kt, ot * 128:(ot + 1) * 128], rhs=yg4[:, kt, t, :],
                                                                                           start=(kt == 0), stop=(kt == 3)), reads=[gwb, yg], writes=[ps], sig=(kt == 3))
                                S.op("act", lambda e, ps=ps, sg=sg, ot=ot: e.activation(out=sg[:], in_=ps[:, 0:512], func=AF.Sigmoid, bias=glb[:, ot:ot + 1], scale=1.0),
                                     reads=[ps, glb], writes=[sg])
                                S.op("dve", lambda e, sg=sg, tm=tm, ot=ot, t=t: e.tensor_tensor(out=tm[:], in0=ygl4[:, ot, t, :], in1=sg[:], op=ALU.mult), reads=[ygl, sg], writes=[tm])
                                S.op("pool", lambda e, tm=tm, t=t, yov=yov, ztv=ztv: e.tensor_tensor(out=yov[:, t, :, :], in0=tm[:].rearrange("p (i k) -> p i k", i=4),
                                                                                                   in1=ztv[:, t, :, :], op=ALU.mult), reads=[tm, zt], writes=[yo])
                            dma(ysc_t[ot], yo[:], reads=[yo], writes=[yscB[ot]])
                            allgather(ysc_t[ot], ygat_t[ot], yscB[ot], ygatB[ot])
                        S.flush()
                    S.scope = scS
                S.scope = es

            with ExitStack() as scO:
                S.scope = scO
                wos = [S.sb("wos%d" % i, [128, D], F32) for i in range(2)]
                wo = S.sb("wo", [128, 12 * D], BF16)
                wo3 = wo[:].rearrange("p (k o) -> p k o", k=12)
                for kt in range(12):
                    ws = wos[kt % 2]
                    dma(ws[:], wout_d[l, kt * 128:(kt + 1) * 128, :], writes=[ws])
                    S.op("pool", lambda e, ws=ws, kt=kt: e.tensor_copy(out=wo3[:, kt, :], in_=ws[:]), reads=[ws], writes=[wo])
                fg = S.sb("fg", [128, D], F32)
                if last:
                    dma(fg[:], finalg_d.partition_broadcast(128), writes=[fg])
                yts = [S.sb("yt%d" % i, [128, 12 * 512], BF16) for i in range(2)]
                ytSs = [Buf(yts[i].t, "ytS%d" % i) for i in range(2)]
                korder = [4, 5, 6, 7, 8, 9, 10, 11, 0, 1, 2, 3]
                xts = [S.sb("xo%d" % i, [128, D], F32) for i in range(4)]
                xns = [S.sb("xn%d" % i, [128, D], F32) for i in range(4)]
                junk = S.sb("junk2", [128, D], BF16)
                sts2 = [S.sb("st2_%d" % i, [128, 4 * 8], F32) for i in range(4)]
                for st_ in sts2:
                    S.op("dve", lambda e, st_=st_: e.memset(st_[:], 0.0), writes=[st_])
                xdst = out_d if last else x1_d
                for nb_ in range(8):
                    yt = yts[nb_ % 2]
                    yt3 = yt[:].rearrange("p (k n) -> p k n", k=12)
                    ytS = ytSs[nb_ % 2]
                    for kt in range(12):
                        s_, rr_ = divmod(kt, 2)
                        dma(yt3[:, kt, :], ygat_t[s_][rr_ * 128:(rr_ + 1) * 128, nb_ * 512:(nb_ + 1) * 512], reads=[ygatB[s_]], writes=[ytS if kt < 4 else yt], par=True)
                    for tq in range(4):
                        tt = nb_ * 4 + tq
                        xt = xts[tt % 4]
                        xn = xns[tt % 4]
                        dma(xt[:], xsrc[tt * 128:(tt + 1) * 128, :], writes=[xt], q="act")
                        for oh in range(2):
                            ps = PS[5 + oh]
                            for ki, kt in enumerate(korder):
                                S.op("pe", lambda e, ps=ps, kt=kt, ki=ki, oh=oh, tq=tq, yt3=yt3: e.matmul(ps[:, 0:512], lhsT=yt3[:, kt, tq * 128:(tq + 1) * 128], rhs=wo3[:, kt, oh * 512:(oh + 1) * 512],
                                                                                                         start=(ki == 0), stop=(ki == 11)), reads=[(ytS if kt < 4 else yt), wo], writes=[ps], sig=(ki == 11))
                            S.op("dve", lambda e, ps=ps, oh=oh, xt=xt, xn=xn: e.tensor_tensor(out=xn[:, oh * 512:(oh + 1) * 512], in0=ps[:, 0:512], in1=xt[:, oh * 512:(oh + 1) * 512], op=ALU.add),
                                 reads=[ps, xt], writes=[xn])
                        if last:
                            c0 = (tt // 4) * 4
                            st = sts2[tt % 4]
                            S.op("act", lambda e, xn=xn, c0=c0, st=st: e.activation(out=junk[:], in_=xn[:], func=AF.Square, accum_out=st[:, c0:c0 + 1]), reads=[xn], writes=[junk, st])
                            S.op("dve", lambda e, c0=c0, st=st: e.tensor_scalar(out=st[:, c0 + 1:c0 + 2], in0=st[:, c0:c0 + 1], scalar1=1.0 / D, scalar2=1e-6, op0=ALU.mult, op1=ALU.add),
                                 reads=[st], writes=[st])
                            S.op("act", lambda e, c0=c0, st=st: e.activation(out=st[:, c0 + 2:c0 + 3], in_=st[:, c0 + 1:c0 + 2], func=AF.Sqrt), reads=[st], writes=[st])
                            S.op("dve", lambda e, c0=c0, st=st: e.reciprocal(out=st[:, c0 + 3:c0 + 4], in_=st[:, c0 + 2:c0 + 3]), reads=[st], writes=[st])
                            S.op("dve", lambda e, xn=xn, c0=c0, st=st: e.scalar_tensor_tensor(out=xn[:], in0=xn[:], scalar=st[:, c0 + 3:c0 + 4], in1=fg[:], op0=ALU.mult, op1=ALU.mult),
                                 reads=[xn, st, fg], writes=[xn])
                        dma(xdst[tt * 128:(tt + 1) * 128, :], xn[:], reads=[xn], q="sp")
                S.flush()
            S.scope = es
        print("instructions recorded:", S.nins)
    return nc


def make_in_maps(inputs, ncores=8):
    f = np.float32
    mge, mle = _masks()
    ident = np.eye(128, dtype=f)
    etab = _etab()

    def dp(a):
        return np.ascontiguousarray(np.transpose(a, (0, 1, 3, 2)).reshape(2, 128, 32), f)

    lre = dp(inputs["ssm_lam_re"])
    lim = dp(inputs["ssm_lam_im"])
    ldt = np.ascontiguousarray(np.broadcast_to(inputs["ssm_log_dt"][:, :, None, :], (2, 2, 64, 32)).reshape(2, 128, 32), f)
    bre = np.ascontiguousarray(np.transpose(inputs["ssm_b_re"], (0, 1, 3, 2, 4)).reshape(2, 128, 512), f)
    bim = np.ascontiguousarray(np.transpose(inputs["ssm_b_im"], (0, 1, 3, 2, 4)).reshape(2, 128, 512), f)
    cre = np.ascontiguousarray(np.transpose(inputs["ssm_c_re"], (0, 1, 4, 2, 3)).reshape(2, 128, 512), f)
    cim = np.ascontiguousarray(np.transpose(inputs["ssm_c_im"], (0, 1, 4, 2, 3)).reshape(2, 128, 512), f)
    dd = inputs["ssm_d"].reshape(2, 32, 16)
    dsk = np.ascontiguousarray(np.broadcast_to(np.transpose(dd, (0, 2, 1))[:, None, :, :], (2, 8, 16, 32)).reshape(2, 128, 32), f)
    glub = np.ascontiguousarray(np.transpose(inputs["glu_b"].reshape(2, 4, 128), (0, 2, 1)), f)
    gna = np.stack([_gather_na(np.asarray(inputs["na_rpb"][l], f)) for l in range(2)])
    gdil = _gather_dil(np.asarray(inputs["t5_bias"], f))
    w_in = np.asarray(inputs["w_in"], f)
    w_out = np.asarray(inputs["w_out"], f)
    glu_w = np.asarray(inputs["glu_w"], f)
    gt = []
    for s_ in range(6):
        for rr in range(2):
            base = (0, 0, 4, 4, 8, 8)[s_]
            gt.append(base + 2 * rr + (s_ % 2))
    rows = np.concatenate([np.arange(t * 128, (t + 1) * 128) for t in gt])
    w_out_p = np.ascontiguousarray(w_out[:, rows, :])
    maps = []
    for c in range(ncores):
        b, r = divmod(c, 2)
        b = b % 4
        tl = (2 * r, 2 * r + 1)
        idx = np.concatenate([np.arange(base + t * 128, base + (t + 1) * 128)
                              for base in (0, 512, 1024, 1536, 2048, 2560, 3072, 3584, 4096, 4608) for t in tl])
        m = {
            "x": np.ascontiguousarray(inputs["x"][b], f),
            "norm_g": np.ascontiguousarray(inputs["norm_g"], f),
            "final_g": np.ascontiguousarray(inputs["final_g"], f),
            "w_in": np.ascontiguousarray(w_in[:, :, idx]),
            "w_out": w_out_p,
            "ident": ident, "mge": mge, "mle": mle, "etab": etab,
            "lam_re_t": np.ascontiguousarray(lre[:, :, 16 * r:16 * r + 16]),
            "lam_im_t": np.ascontiguousarray(lim[:, :, 16 * r:16 * r + 16]),
            "logdt_t": np.ascontiguousarray(ldt[:, :, 16 * r:16 * r + 16]),
            "b_re_t": np.ascontiguousarray(bre[:, :, 256 * r:256 * r + 256]),
            "b_im_t": np.ascontiguousarray(bim[:, :, 256 * r:256 * r + 256]),
            "c_re_t": np.ascontiguousarray(cre[:, :, 256 * r:256 * r + 256]),
            "c_im_t": np.ascontiguousarray(cim[:, :, 256 * r:256 * r + 256]),
            "dsk_t": np.ascontiguousarray(dsk[:, :, 16 * r:16 * r + 16]),
            "glu_w": np.ascontiguousarray(glu_w[:, :, 256 * r:256 * r + 256]),
            "glub_t": np.ascontiguousarray(glub[:, :, 2 * r:2 * r + 2]),
            "gna": np.ascontiguousarray(gna[:, 4 * r:4 * r + 4]),
            "gdil": np.ascontiguousarray(gdil[4 * r:4 * r + 4]),
        }
        maps.append(m)
    return maps


_CACHE = {}


def kernel(**inputs):
    if "nc" not in _CACHE:
        _CACHE["nc"] = build_program()
    nc = _CACHE["nc"]
    maps = make_in_maps(inputs, 8)
    res = run_bass_kernel_spmd(nc, maps, core_ids=list(range(8)))
    out = np.stack([np.asarray(res.results[2 * b]["out"], np.float32) for b in range(4)], axis=0)
    return out
```

```python
import numpy as np
from contextlib import ExitStack
import concourse.bass as bass
import concourse.mybir as mybir
from concourse.bass_utils import run_bass_kernel_spmd

F32 = mybir.dt.float32
BF16 = mybir.dt.bfloat16
ALU = mybir.AluOpType
AF = mybir.ActivationFunctionType

S_LEN = 4096
D = 1024
NEG = -30000.0
MAGIC = 12582912.0
TWO_PI = float(2 * np.pi)
NE = 83
ATTACH_WAITS = True


class Buf:
    __slots__ = ("t", "lw", "rd", "name", "psum")

    def __init__(self, t, name="", psum=False):
        self.t = t
        self.lw = {}
        self.rd = {}
        self.name = name
        self.psum = psum

    def __getitem__(self, k):
        return self.t[k]


class Sched:
    NDMA = 56
    NHW = 40

    def __init__(self, nc, es):
        self.nc = nc
        self.es = es
        self.names = ["sp", "pe", "act", "dve", "pool"]
        self.sem = {k: es.enter_context(nc.semaphore("s_" + k)) for k in self.names}
        self.cnt = {k: 0 for k in self.names}
        self.dsem = [es.enter_context(nc.semaphore("d%d" % i)) for i in range(self.NDMA)]
        self.dcnt = [0] * self.NDMA
        self.dnext = {"hw": 0, "sw": self.NHW}
        self.waited = {}
        self.prog = {k: [] for k in self.names}
        self.scope = es
        self.nins = 0
        self.pend = {k: False for k in self.names}
        self.bsem = es.enter_context(nc.semaphore("s_bar"))
        self.bcnt = 0

    def sb(self, name, shape, dt):
        self.uid = getattr(self, "uid", 0) + 1
        name = "sb%d_%s" % (self.uid, name)
        return Buf(self.scope.enter_context(self.nc.sbuf_tensor(name, shape, dt)), name)

    def ps(self, name, shape, dt=F32):
        return Buf(self.es.enter_context(self.nc.psum_tensor(name, shape, dt)), name, True)

    def _wait(self, e, tok):
        if tok is None:
            return
        kind, idx, val = tok
        if kind == "c" and e == "pe" and idx == "pe":
            return
        if kind == "c":
            assert val <= self.cnt[idx], "wait on a not-yet-recorded signalling op (%s waits %s %d > %d)" % (e, idx, val, self.cnt[idx])
        key = (e, kind, idx)
        if self.waited.get(key, 0) >= val:
            return
        self.waited[key] = val
        sem = self.sem[idx] if kind == "c" else self.dsem[idx]
        self.prog[e].append(("w", sem, val))

    def op(self, e, fn, reads=(), writes=(), dma=False, sig=True, par=False, cc=False):
        reads = list(reads)
        writes = list(writes)
        for b in list(reads):
            if b.psum:
                reads.remove(b)
                if b not in writes:
                    writes.append(b)
        need = {}
        for b in reads:
            for k2, val in b.lw.items():
                if need.get(k2, 0) < val:
                    need[k2] = val
        for b in writes:
            for k2, val in b.lw.items():
                if par and dma and k2[0] == "d":
                    continue
                if need.get(k2, 0) < val:
                    need[k2] = val
            for k2, val in b.rd.items():
                if need.get(k2, 0) < val:
                    need[k2] = val
        for (kind, idx), val in need.items():
            self._wait(e, (kind, idx, val))
        if dma:
            kind_ = "sw" if e == "pool" else "hw"
            i = self.dnext[kind_]
            if kind_ == "hw":
                self.dnext[kind_] = (i + 1) % self.NHW
            else:
                self.dnext[kind_] = self.NHW + (i + 1 - self.NHW) % (self.NDMA - self.NHW)
            if self.dcnt[i] > 0:
                self._wait(e, ("d", i, self.dcnt[i]))
            inc = 1 if cc else 16
            self.dcnt[i] += inc
            self.prog[e].append(("o", fn, self.dsem[i], inc))
            tok = ("d", i, self.dcnt[i])
        elif sig:
            self.cnt[e] += 1
            self.prog[e].append(("o", fn, self.sem[e], 1))
            tok = ("c", e, self.cnt[e])
            self.pend[e] = False
        else:
            self.prog[e].append(("n", fn))
            tok = ("c", e, self.cnt[e] + 1)
            self.pend[e] = True
        self.nins += 1
        for b in reads:
            k = (tok[0], tok[1])
            if b.rd.get(k, 0) < tok[2]:
                b.rd[k] = tok[2]
        for b in writes:
            if par and dma:
                b.lw = {k: v for k, v in b.lw.items() if k[0] == "d"}
                b.lw[(tok[0], tok[1])] = tok[2]
            else:
                b.lw = {(tok[0], tok[1]): tok[2]}
            b.rd = {}
        return tok

    def barrier(self):
        assert not any(self.pend.values()), self.pend
        for i in range(self.NDMA):
            if self.dcnt[i] > 0:
                self._wait("sp", ("d", i, self.dcnt[i]))
        self.bcnt += 1
        self.prog["sp"].append(("s", self.bsem, 1))
        for e in self.names:
            for e2 in self.names:
                if e2 != e and self.cnt[e2] > 0:
                    self._wait(e, ("c", e2, self.cnt[e2]))
            if e != "sp":
                self.prog[e].append(("w", self.bsem, self.bcnt))
                for i in range(self.NDMA):
                    if self.dcnt[i] > 0:
                        self.waited[(e, "d", i)] = max(self.waited.get((e, "d", i), 0), self.dcnt[i])

    def flush(self):
        self.barrier()
        prog = self.prog
        self.prog = {k: [] for k in self.names}

        def mk(name):
            def body(eng):
                items = prog[name]
                n = len(items)
                i = 0
                while i < n:
                    it = items[i]
                    if it[0] == "w":
                        nxt = items[i + 1] if i + 1 < n else None
                        if nxt is not None and nxt[0] in ("o", "n") and ATTACH_WAITS:
                            ins = nxt[1](eng)
                            ins._wait_ge(it[1], it[2])
                            if nxt[0] == "o":
                                ins.then_inc(nxt[2], nxt[3])
                            i += 2
                            continue
                        eng.wait_ge(it[1], it[2])
                    elif it[0] == "n":
                        it[1](eng)
                    elif it[0] == "s":
                        eng.sem_inc(it[1], it[2])
                    else:
                        it[1](eng).then_inc(it[2], it[3])
                    i += 1
            return body

        with self.nc.Block() as block:
            block.sync(mk("sp"))
            block.tensor(mk("pe"))
            block.scalar(mk("act"))
            block.vector(mk("dve"))
            block.gpsimd(mk("pool"))


def _t5_bucket(rel):
    nb = 16
    max_exact = 8
    n = np.abs(rel)
    large = max_exact + (np.log(np.maximum(n, 1) / max_exact) / np.log(1024 / max_exact) * (nb - max_exact)).astype(np.int32)
    large = np.minimum(large, nb - 1)
    return (np.where(rel > 0, nb, 0) + np.where(n < max_exact, n, large)).astype(np.int32)


def _gather_dil(t5_bias):
    out = np.full((8, 128, 3, 256), NEG, np.float32)
    kk = np.arange(128)[:, None]
    cc = np.arange(256)[None, :]
    rel = kk + 64 - cc
    valid = np.abs(rel) <= 64
    for pi, d in enumerate((1, 4, 16)):
        bkt = _t5_bucket(d * rel)
        for h in range(8):
            vals = t5_bias[bkt, h]
            out[h, :, pi, :] = np.where(valid, vals, np.float32(NEG))
    return np.ascontiguousarray(out.reshape(8, 128, 768))


def _rs(r):
    return min(max(r - 4, 0), 56)


NA_SBREP = [0, 3, 7]


def _na_slot(tab, slot):
    sb = NA_SBREP[tab]
    m = 4 * sb - 2 + slot
    if m < 0 or m > 31:
        return None
    rows = [rl for rl in range(8) if _rs(8 * sb + rl) <= 2 * m + 1 and _rs(8 * sb + rl) + 7 >= 2 * m]
    if not rows:
        return None
    assert rows == list(range(rows[0], rows[-1] + 1))
    return rows[0], rows[-1]


def _gather_na(rpb):
    out = np.full((8, 3, 128, 8, 8, 64), NEG, np.float32)
    kk = np.arange(128)
    j = np.arange(64)
    cs = np.clip(j - 8, 0, 48)
    kc = kk % 64
    colok = (kc[:, None] >= cs[None, :]) & (kc[:, None] <= cs[None, :] + 15)
    dc = np.clip(kc[:, None] - j[None, :] + 15, 0, 30)
    for tab in range(3):
        sb = NA_SBREP[tab]
        for slot in range(8):
            rr = _na_slot(tab, slot)
            if rr is None:
                continue
            m = 4 * sb - 2 + slot
            kr = 2 * m + kk // 64
            for rl in range(rr[0], rr[1] + 1):
                r = 8 * sb + rl
                rs = _rs(r)
                valid = ((kr >= rs) & (kr <= rs + 7))[:, None] & colok
                dr = np.clip(kr - r + 7, 0, 14)
                for h in range(8):
                    vals = rpb[h][dr[:, None], dc]
                    out[h, tab, :, slot, rl, :] = np.where(valid, vals, np.float32(NEG))
    return np.ascontiguousarray(out.reshape(8, 3, 128, 4096))


def _etab():
    e = np.zeros((128, NE), np.float32)
    for dirn in range(2):
        rows = slice(dirn * 64, dirn * 64 + 64)
        sg = -1.0 if dirn == 0 else 1.0
        for s in range(8):
            e[rows, s] = sg * s
        for m in range(5):
            for t in range(8):
                e[rows, 8 + m * 8 + t] = 8 * m + (t if dirn == 0 else -t)
        for jj in range(4):
            for s in range(8):
                e[rows, 48 + jj * 8 + s] = (32 - 8 * jj - s) if dirn == 0 else (8 * jj + s)
        e[rows, 80] = 1.0
        e[rows, 81] = 32.0
        e[rows, 82] = 64.0
    return e


def _masks():
    sp = np.arange(128) // 16
    mge = (sp[None, :] >= sp[:, None]).astype(np.float32)
    mle = (sp[None, :] <= sp[:, None]).astype(np.float32)
    return mge, mle


def build_program(nlayers=2, dbg=None, do_na=True, do_dil=True, do_ssm=True, npairs=4):
    nc = bass.Bass("TRN2", target_bir_lowering=False)

    def din(name, shape, dt=F32):
        return nc.dram_tensor(name, list(shape), dt, kind="ExternalInput").ap()

    x_d = din("x", [S_LEN, D])
    normg_d = din("norm_g", [2, D])
    finalg_d = din("final_g", [D])
    win_d = din("w_in", [2, D, 2560])
    rgroups = [[2 * i, 2 * i + 1] for i in range(npairs)]
    XA0, ZA0, NAQ, NAK, NAV, NAZ, DQ, DK, DV, DZ = 0, 256, 512, 768, 1024, 1280, 1536, 1792, 2048, 2304
    wout_d = din("w_out", [2, 1536, D])
    ident_d = din("ident", [128, 128])
    mge_d = din("mge", [128, 128])
    mle_d = din("mle", [128, 128])
    etab_d = din("etab", [128, NE])
    lre_d = din("lam_re_t", [2, 128, 16])
    lim_d = din("lam_im_t", [2, 128, 16])
    ldt_d = din("logdt_t", [2, 128, 16])
    bre_d = din("b_re_t", [2, 128, 256])
    bim_d = din("b_im_t", [2, 128, 256])
    cre_d = din("c_re_t", [2, 128, 256])
    cim_d = din("c_im_t", [2, 128, 256])
    dsk_d = din("dsk_t", [2, 128, 16])
    gluw_d = din("glu_w", [2, 512, 256])
    glub_d = din("glub_t", [2, 128, 2])
    gna_d = din("gna", [2, 4, 3, 128, 4096])
    gdil_d = din("gdil", [4, 128, 768])
    out_d = nc.dram_tensor("out", [S_LEN, D], F32, kind="ExternalOutput").ap()
    sk = "ExternalOutput" if dbg else "Internal"
    x1_d = nc.dram_tensor("x1s", [S_LEN, D], F32, kind=sk).ap()
    ysc_t = [nc.dram_tensor("ysc%d" % i, [128, S_LEN], BF16, kind="Internal").ap() for i in range(6)]
    ygat_t = [nc.dram_tensor("ygat%d" % i, [256, S_LEN], BF16, kind="Internal", addr_space="Local").ap() for i in range(6)]
    usc_d = nc.dram_tensor("usc", [2, 8, 16, 8, 512], BF16, kind="Internal").ap()
    ys2_t = [nc.dram_tensor("ys2_%d" % i, [128, S_LEN], BF16, kind="Internal").ap() for i in range(2)]
    ys2g_t = [nc.dram_tensor("ys2g%d" % i, [256, S_LEN], BF16, kind="Internal", addr_space="Local").ap() for i in range(2)]
    zsc_d = nc.dram_tensor("zsc", [2, 128, S_LEN], BF16, kind="Internal").ap()
    yscB = [Buf(None, "yscB%d" % i) for i in range(6)]
    ygatB = [Buf(None, "ygatB%d" % i) for i in range(6)]
    ys2B = [Buf(None, "ys2B%d" % i) for i in range(2)]
    ys2gB = [Buf(None, "ys2gB%d" % i) for i in range(2)]

    with ExitStack() as es:
        S = Sched(nc, es)
        PS = [S.ps("ps%d" % i, [128, 512], F32) for i in range(7)]
        PSB = S.ps("psb", [128, 1024], BF16)
        identf = S.sb("identf", [128, 128], F32)
        identb = S.sb("identb", [128, 128], BF16)
        onesb = S.sb("onesb", [128, 128], BF16)
        zerosb = S.sb("zerosb", [128, 128], BF16)
        mge = S.sb("mge", [128, 128], F32)
        mle = S.sb("mle", [128, 128], F32)
        wst = [S.sb("wst%d" % i, [128, 8 * 128], F32) for i in range(2)]
        wbb = [S.sb("wbb%d" % i, [128, 8 * 128], BF16) for i in range(2)]
        state = {"wi": 0, "ps": 0, "ev": 0}

        def dma(out, in_, reads=(), writes=(), q="sp", par=False):
            return S.op(q, lambda e: e.dma_start(out=out, in_=in_), reads=reads, writes=writes, dma=True, par=par)

        def allgather(src_ap, dst_ap, srcB, dstB):
            return S.op("pool", lambda e: e.collective_compute("AllGather", ALU.bypass, replica_groups=rgroups, ins=[src_ap], outs=[dst_ap]),
                        reads=[srcB], writes=[dstB], dma=True, cc=True)

        dma(identf[:], ident_d, writes=[identf])
        dma(mge[:], mge_d, writes=[mge])
        dma(mle[:], mle_d, writes=[mle])
        S.op("dve", lambda e: e.tensor_copy(out=identb[:], in_=identf[:]), reads=[identf], writes=[identb])
        S.op("dve", lambda e: e.memset(onesb[:], 1.0), writes=[onesb])
        S.op("dve", lambda e: e.memset(zerosb[:], 0.0), writes=[zerosb])

        def emit_pipeline(tasks, depth=2):
            queue = []
            for t in tasks:
                if t["init"] is not None:
                    t["init"]()
                t["S"]()
                t["E"]()
                queue.append(t)
                if len(queue) > depth:
                    p = queue.pop(0)
                    p["PV"]()
                    if p["fin"] is not None:
                        p["fin"]()
            for p in queue:
                p["PV"]()
                if p["fin"] is not None:
                    p["fin"]()

        wseq = []
        for l_ in range(nlayers):
            if do_na:
                for hp_ in range(2):
                    wseq += [(l_, NAQ + hp_ * 128), (l_, NAK + hp_ * 128), (l_, NAZ + hp_ * 128), (l_, NAV + hp_ * 128)]
            if do_dil:
                for hp_ in range(2):
                    wseq += [(l_, DQ + hp_ * 128), (l_, DK + hp_ * 128), (l_, DZ + hp_ * 128), (l_, DV + hp_ * 128)]
            if do_ssm:
                for ct_ in range(2):
                    wseq += [(l_, XA0 + ct_ * 128), (l_, ZA0 + ct_ * 128)]
        state["wptr"] = 0
        state["wissued"] = 0

        state["pcast"] = []

        def issue_w(k):
            l_, col0 = wseq[k]
            i = k % 2
            src = win_d[l_, :, col0:col0 + 128].rearrange("(j p) c -> p j c", p=128)
            dma(wst[i][:].rearrange("p (j c) -> p j c", j=8), src, writes=[wst[i]])
            state["pcast"].append(k)

        def do_cast():
            while state["pcast"]:
                k = state["pcast"].pop(0)
                i = k % 2
                if k % 2 == 0:
                    S.op("dve", lambda e, i=i: e.tensor_copy(out=wbb[i][:], in_=wst[i][:]), reads=[wst[i]], writes=[wbb[i]])
                else:
                    S.op("act", lambda e, i=i: e.activation(out=wbb[i][:], in_=wst[i][:], func=AF.Copy), reads=[wst[i]], writes=[wbb[i]])

        def load_w(l, col0):
            k = state["wptr"]
            assert wseq[k] == (l, col0), (wseq[k], l, col0)
            while state["wissued"] <= k:
                issue_w(state["wissued"])
                state["wissued"] += 1
            do_cast()
            if state["wissued"] <= min(k + 1, len(wseq) - 1):
                issue_w(state["wissued"])
                state["wissued"] += 1
            state["wptr"] = k + 1
            return wbb[k % 2]

        def evac_copy(out_ap, in_ap, rd, wr, scale=None):
            state["ev"] ^= 1
            if scale is not None or state["ev"]:
                sc = 1.0 if scale is None else scale
                S.op("act", lambda e: e.activation(out=out_ap, in_=in_ap, func=AF.Copy, scale=sc), reads=rd, writes=wr)
            else:
                S.op("dve", lambda e: e.tensor_copy(out=out_ap, in_=in_ap), reads=rd, writes=wr)

        for l in range(nlayers):
            xsrc = x_d if l == 0 else x1_d
            last = (l == nlayers - 1)
            with ExitStack() as scH:
                S.scope = scH
                hT = S.sb("hT", [128, 8 * S_LEN], BF16)
                hT3 = hT[:].rearrange("p (j t) -> p j t", j=8)

                with ExitStack() as ph:
                    S.scope = ph
                    gbc = S.sb("gbc", [128, D], F32)
                    xts = [S.sb("xt%d" % i, [128, D], F32) for i in range(4)]
                    junk = S.sb("junk", [128, D], BF16)
                    hbs = [S.sb("hb%d" % i, [128, D], BF16) for i in range(2)]
                    sts = [S.sb("st%d" % i, [128, 4 * 8], F32) for i in range(4)]
                    dma(gbc[:], normg_d[l].partition_broadcast(128), writes=[gbc])
                    for st_ in sts:
                        S.op("dve", lambda e, st_=st_: e.memset(st_[:], 0.0), writes=[st_])
                    for tt in range(32):
                        xt = xts[tt % 4]
                        hb = hbs[tt % 2]
                        st = sts[tt % 4]
                        c0 = (tt // 4) * 4
                        dma(xt[:], xsrc[tt * 128:(tt + 1) * 128, :], writes=[xt], q="sp")
                        S.op("act", lambda e, xt=xt, c0=c0, st=st: e.activation(out=junk[:], in_=xt[:], func=AF.Square, accum_out=st[:, c0:c0 + 1]),
                             reads=[xt], writes=[junk, st])
                        S.op("dve", lambda e, c0=c0, st=st: e.tensor_scalar(out=st[:, c0 + 1:c0 + 2], in0=st[:, c0:c0 + 1], scalar1=1.0 / D, scalar2=1e-6,
                                                                    op0=ALU.mult, op1=ALU.add), reads=[st], writes=[st])
                        S.op("act", lambda e, c0=c0, st=st: e.activation(out=st[:, c0 + 2:c0 + 3], in_=st[:, c0 + 1:c0 + 2], func=AF.Sqrt), reads=[st], writes=[st])
                        S.op("dve", lambda e, c0=c0, st=st: e.reciprocal(out=st[:, c0 + 3:c0 + 4], in_=st[:, c0 + 2:c0 + 3]), reads=[st], writes=[st])
                        S.op("dve", lambda e, xt=xt, hb=hb, c0=c0, st=st: e.scalar_tensor_tensor(out=hb[:], in0=xt[:], scalar=st[:, c0 + 3:c0 + 4], in1=gbc[:],
                                                                                          op0=ALU.mult, op1=ALU.mult), reads=[xt, st, gbc], writes=[hb])
                        for j in range(8):
                            S.op("pe", lambda e, hb=hb, j=j: e.transpose(out=PSB[:, j * 128:(j + 1) * 128], in_=hb[:, j * 128:(j + 1) * 128], identity=identb[:]),
                                 reads=[hb, identb], writes=[PSB], sig=(j == 7))
                        evac_copy(hT3[:, :, tt * 128:(tt + 1) * 128], PSB[:].rearrange("p (j t) -> p j t", j=8), [PSB], [hT])
                    S.flush()
                S.scope = scH

                def proj_fm(col0, evac):
                    wb = load_w(l, col0)
                    wb3 = wb[:].rearrange("p (j c) -> p j c", j=8)
                    for n in range(8):
                        ps = PS[5 + (n % 2)]
                        for j in range(8):
                            S.op("pe", lambda e, ps=ps, j=j, n=n: e.matmul(ps[:, 0:512], lhsT=wb3[:, j, :], rhs=hT3[:, j, n * 512:(n + 1) * 512],
                                                                           start=(j == 0), stop=(j == 7)), reads=[wb, hT], writes=[ps], sig=(j == 7))
                        evac(n, ps)
                        if n == 4:
                            do_cast()

                def tiles_from_fm(vT, tiles, vbuf, v3):
                    for g8 in range(0, len(tiles), 8):
                        for q in range(8):
                            tsl = tiles[g8 + q]
                            S.op("pe", lambda e, q=q, tsl=tsl: e.transpose(out=PSB[:, q * 128:(q + 1) * 128], in_=vT[:, tsl], identity=identb[:]),
                                 reads=[vT, identb], writes=[PSB], sig=(q == 7))
                        evac_copy(v3[:, g8:g8 + 8, :], PSB[:].rearrange("p (q c) -> p q c", q=8), [PSB], [vbuf])

                def proj_tm(col0, tiles, vbuf, v3, wb=None):
                    if wb is None:
                        wb = load_w(l, col0)
                    wb3 = wb[:].rearrange("p (j c) -> p j c", j=8)
                    for g4 in range(0, len(tiles), 4):
                        ps = PS[5 + ((g4 // 4) % 2)]
                        for q in range(4):
                            tsl = tiles[g4 + q]
                            for j in range(8):
                                S.op("pe", lambda e, ps=ps, j=j, q=q, tsl=tsl: e.matmul(ps[:, q * 128:(q + 1) * 128], lhsT=hT3[:, j, tsl], rhs=wb3[:, j, :],
                                                                                       start=(j == 0), stop=(j == 7)), reads=[wb, hT], writes=[ps], sig=(j == 7 and q == 3))
                        evac_copy(v3[:, g4:g4 + 4, :], ps[:, 0:512].rearrange("p (q c) -> p q c", q=4), [ps], [vbuf])
                        if g4 == 16:
                            do_cast()

                def attn_common(ph, colq, colk, colz, split_q=False):
                    kT = S.sb("kT", [128, S_LEN], BF16)
                    zT = S.sb("zT", [128, S_LEN], BF16)
                    yT = S.sb("yT", [128, S_LEN], BF16)
                    if split_q:
                        qz = [S.sb("qz%d" % i, [128, S_LEN], BF16) for i in range(2)]
                        S.op("pool", lambda e: e.memset(qz[0][64:128, :], 0.0), writes=[qz[0]])
                        S.op("pool", lambda e: e.memset(qz[1][0:64, :], 0.0), writes=[qz[1]])

                        def evq(n, ps):
                            for hh_ in range(2):
                                rs_ = slice(64 * hh_, 64 * hh_ + 64)
                                S.op("act", lambda e, hh_=hh_, rs_=rs_: e.activation(out=qz[hh_][rs_, n * 512:(n + 1) * 512], in_=ps[rs_, 0:512], func=AF.Copy, scale=0.125),
                                     reads=[ps], writes=[qz[hh_]])
                        proj_fm(colq, evq)
                        qT = qz
                    else:
                        qT = S.sb("qT", [128, S_LEN], BF16)
                        proj_fm(colq, lambda n, ps: S.op("act", lambda e: e.activation(out=qT[:, n * 512:(n + 1) * 512], in_=ps[:, 0:512], func=AF.Copy, scale=0.125),
                                                         reads=[ps], writes=[qT]))
                    proj_fm(colk, lambda n, ps: evac_copy(kT[:, n * 512:(n + 1) * 512], ps[:, 0:512], [ps], [kT]))
                    proj_fm(colz, lambda n, ps: S.op("act", lambda e: e.activation(out=zT[:, n * 512:(n + 1) * 512], in_=ps[:, 0:512], func=AF.Silu),
                                                     reads=[ps], writes=[zT]))
                    return qT, kT, zT, yT

                if do_na:
                    for hp in range(2):
                        with ExitStack() as ph:
                            S.scope = ph
                            gsts = [S.sb("gst%d" % i, [128, 4096], F32) for i in range(2)]
                            nbt = [[S.sb("nb%d_%d" % (hh_, t_), [128, 4096], BF16) for t_ in range(3)] for hh_ in range(2)]
                            gorder = [(0, 0), (0, 1), (0, 2), (1, 0), (1, 1), (1, 2)]
                            for i_ in range(2):
                                dma(gsts[i_][:], gna_d[l, hp * 2 + gorder[i_][0], gorder[i_][1]], writes=[gsts[i_]], q="act")
                            qT, kT, zT, yT = attn_common(ph, NAQ + hp * 128, NAK + hp * 128, NAZ + hp * 128)
                            vN = S.sb("vN", [128, 32 * 128], BF16)
                            vN3 = vN[:].rearrange("p (t c) -> p t c", t=32)
                            vT = yT
                            proj_fm(NAV + hp * 128, lambda n, ps: evac_copy(vT[:, n * 512:(n + 1) * 512], ps[:, 0:512], [ps], [vT]))
                            tiles_from_fm(vT, [slice(t * 128, (t + 1) * 128) for t in range(32)], vN, vN3)
                            PT = [S.sb("pt%d" % i, [128, 512], BF16) for i in range(3)]
                            SCB = [PS[0], PS[1], PS[6]]
                            dsb = S.sb("dsb", [128, 512], F32)
                            tnum = S.sb("tnum", [128, 512], F32)
                            for i_, (hh, t_) in enumerate(gorder):
                                gst = gsts[i_ % 2]
                                if i_ >= 2:
                                    dma(gst[:], gna_d[l, hp * 2 + hh, t_], writes=[gst], q="act")
                                S.op("dve", lambda e, hh=hh, t_=t_, gst=gst: e.tensor_copy(out=nbt[hh][t_][:], in_=gst[:]), reads=[gst], writes=[nbt[hh][t_]])
                            tcount = 0
                            tasks = []
                            for hh in range(2):
                                sl = slice(64 * hh, 64 * hh + 64)
                                for sb_ in range(8):
                                    tab = 0 if sb_ == 0 else (2 if sb_ == 7 else 1)
                                    nb = nbt[hh][tab]
                                    psn = PS[2 + sb_ % 2]
                                    psd = PS[4 + sb_ % 2]

                                    def f_init(psn=psn, psd=psd, nb=nb):
                                        for dst in (psn, psd):
                                            S.op("pe", lambda e, dst=dst, nb=nb: e.matmul(dst[:, 0:512], lhsT=zerosb[:], rhs=nb[:, 0:512], start=True, stop=False),
                                                 reads=[zerosb, nb], writes=[dst], sig=False)

                                    def f_fin(psn=psn, psd=psd, sl=sl, tok=slice(sb_ * 512, sb_ * 512 + 512)):
                                        S.op("act", lambda e: e.activation(out=dsb[sl, :], in_=psd[sl, :], func=AF.Copy), reads=[psd], writes=[dsb])
                                        S.op("dve", lambda e: e.reciprocal(out=dsb[sl, :], in_=dsb[sl, :]), reads=[dsb], writes=[dsb])
                                        S.op("dve", lambda e: e.tensor_tensor(out=tnum[sl, :], in0=psn[sl, :], in1=dsb[sl, :], op=ALU.mult), reads=[psn, dsb], writes=[tnum])
                                        S.op("pool", lambda e: e.tensor_tensor(out=yT[sl, tok], in0=tnum[sl, :], in1=zT[sl, tok], op=ALU.mult), reads=[tnum, zT], writes=[yT])

                                    slots = [(slot, _na_slot(tab, slot)) for slot in range(8)]
                                    slots = [(slot, rr) for slot, rr in slots if rr is not None]
                                    for si, (slot, (ra, rb)) in enumerate(slots):
                                        m = 4 * sb_ - 2 + slot
                                        c0 = ra * 64
                                        N = (rb - ra + 1) * 64
                                        q0 = 512 * sb_ + c0
                                        pss = SCB[tcount % 3]
                                        pt = PT[tcount % 3]
                                        tcount += 1
                                        lastt = (si == len(slots) - 1)

                                        def f_S(pss=pss, nb=nb, slot=slot, c0=c0, N=N, m=m, q0=q0, sl=sl):
                                            S.op("pe", lambda e: e.matmul(pss[:, 0:N], lhsT=identb[:], rhs=nb[:, slot * 512 + c0:slot * 512 + c0 + N], start=True, stop=False),
                                                 reads=[identb, nb], writes=[pss], sig=False)
                                            S.op("pe", lambda e: e.matmul(pss[:, 0:N], lhsT=kT[sl, m * 128:(m + 1) * 128], rhs=qT[sl, q0:q0 + N], start=False, stop=True),
                                                 reads=[kT, qT], writes=[pss])

                                        def f_E(pss=pss, pt=pt, N=N):
                                            S.op("act", lambda e: e.activation(out=pt[:, 0:N], in_=pss[:, 0:N], func=AF.Exp), reads=[pss], writes=[pt])

                                        def f_PV(psn=psn, psd=psd, pt=pt, m=m, c0=c0, N=N, lastt=lastt, first=(si == 0)):
                                            S.op("pe", lambda e: e.matmul(psn[:, c0:c0 + N], lhsT=vN3[:, m, :], rhs=pt[:, 0:N], start=first, stop=lastt, skip_group_check=True),
                                                 reads=[vN, pt], writes=[psn], sig=False)
                                            S.op("pe", lambda e: e.matmul(psd[:, c0:c0 + N], lhsT=onesb[:], rhs=pt[:, 0:N], start=first, stop=lastt, skip_group_check=True),
                                                 reads=[onesb, pt], writes=[psd], sig=True)

                                        tasks.append({"init": None, "S": f_S, "E": f_E, "PV": f_PV, "fin": f_fin if lastt else None})
                            emit_pipeline(tasks)
                            dma(ysc_t[2 + hp], yT[:], reads=[yT], writes=[yscB[2 + hp]])
                            allgather(ysc_t[2 + hp], ygat_t[2 + hp], yscB[2 + hp], ygatB[2 + hp])
                            S.flush()
                        S.scope = scH

                if do_dil:
                    for hp in range(2):
                        with ExitStack() as ph:
                            S.scope = ph
                            qT, kT, zT, yT = attn_common(ph, DQ + hp * 128, DK + hp * 128, DZ + hp * 128, split_q=True)
                            vD, vD3 = [], []
                            vT = yT
                            proj_fm(DV + hp * 128, lambda n, ps: evac_copy(vT[:, n * 512:(n + 1) * 512], ps[:, 0:512], [ps], [vT]))
                            for pi, d in enumerate((1, 4, 16)):
                                vb = S.sb("vD%d" % pi, [128, 32 * 128], BF16)
                                v3 = vb[:].rearrange("p (t c) -> p t c", t=32)
                                nsub = 32 // d
                                tiles = []
                                for r in range(d):
                                    for m in range(nsub):
                                        s0 = r + d * 128 * m
                                        tiles.append(slice(s0, s0 + d * 127 + 1, d))
                                tiles_from_fm(vT, tiles, vb, v3)
                                vD.append(vb)
                                vD3.append(v3)
                            gst = S.sb("gstd", [128, 768], F32)
                            dbs = [S.sb("db%d" % i, [128, 768], BF16) for i in range(2)]
                            PT = [S.sb("ptd%d" % i, [128, 256], BF16) for i in range(3)]
                            SCB = [PS[0], PS[1], PS[6]]
                            accn = S.sb("accn", [128, S_LEN], F32)
                            accd = S.sb("accd", [128, S_LEN], F32)
                            for hh in range(2):
                                dma(gst[:], gdil_d[hp * 2 + hh], writes=[gst])
                                S.op("pool", lambda e, hh=hh: e.tensor_copy(out=dbs[hh][:], in_=gst[:]), reads=[gst], writes=[dbs[hh]])
                            combo = 0
                            tasks = []
                            for hh in range(2):
                                db = dbs[hh]
                                sl = slice(64 * hh, 64 * hh + 64)
                                nseg_total = 8 + 8 + 16
                                segc = 0
                                for pi, d in ((2, 16), (1, 4), (0, 1)):
                                    n = S_LEN // d
                                    L = min(512, n)
                                    nsub = n // 128
                                    segi = 0
                                    for r in range(d):
                                        for s0 in range(0, n, L):
                                            psn = PS[2 + segi % 2]
                                            psd = PS[4 + segi % 2]
                                            segi += 1
                                            segc += 1
                                            last_seg_of_head = (segc == nseg_total)

                                            def f_init(psn=psn, psd=psd, db=db, L=L):
                                                for dst in (psn, psd):
                                                    S.op("pe", lambda e, dst=dst: e.matmul(dst[:, 0:L], lhsT=zerosb[:], rhs=db[:, 0:L], start=True, stop=False),
                                                         reads=[zerosb, db], writes=[dst], sig=False)

                                            def f_fin(psn=psn, psd=psd, sl=sl, pi=pi, L=L, d=d, t0_=r + d * s0, last_seg_of_head=last_seg_of_head):
                                                for acc, psrc in ((accn, psn), (accd, psd)):
                                                    view = acc[sl, t0_:t0_ + d * (L - 1) + 1:d]
                                                    pview = psrc[sl, 0:L]
                                                    if pi == 2:
                                                        evac_copy(view, pview, [psrc], [acc])
                                                    else:
                                                        S.op("dve", lambda e, view=view, pview=pview: e.tensor_tensor(out=view, in0=view, in1=pview, op=ALU.add),
                                                             reads=[psrc, acc], writes=[acc])
                                                if pi == 0:
                                                    tsl = slice(t0_, t0_ + d * (L - 1) + 1, d)
                                                    S.op("dve", lambda e: e.reciprocal(out=accd[sl, tsl], in_=accd[sl, tsl]), reads=[accd], writes=[accd])
                                                    S.op("dve", lambda e: e.tensor_tensor(out=accn[sl, tsl], in0=accn[sl, tsl], in1=accd[sl, tsl], op=ALU.mult), reads=[accn, accd], writes=[accn])
                                                    S.op("pool", lambda e: e.tensor_tensor(out=yT[sl, tsl], in0=accn[sl, tsl], in1=zT[sl, tsl], op=ALU.mult), reads=[accn, zT], writes=[yT])

                                            ms = [m for m in range(nsub) if max(s0, 128 * m - 64) < min(s0 + L, 128 * m + 192)]
                                            for mi, m in enumerate(ms):
                                                qa = max(s0, 128 * m - 64)
                                                qb = min(s0 + L, 128 * m + 192)
                                                N = qb - qa
                                                cb = qa - (128 * m - 64)
                                                ks = slice(r + d * 128 * m, r + d * (128 * m + 127) + 1, d)
                                                qs = slice(r + d * qa, r + d * (qb - 1) + 1, d)
                                                pss = SCB[combo % 3]
                                                pt = PT[combo % 3]
                                                combo += 1
                                                lastt = (mi == len(ms) - 1)
                                                o0 = qa - s0
                                                tix = r * nsub + m

                                                def f_S(pss=pss, db=db, pi=pi, cb=cb, N=N, ks=ks, qs=qs, qh=qT[hh]):
                                                    S.op("pe", lambda e: e.matmul(pss[:, 0:N], lhsT=identb[:], rhs=db[:, pi * 256 + cb:pi * 256 + cb + N], start=True, stop=False),
                                                         reads=[identb, db], writes=[pss], sig=False)
                                                    S.op("pe", lambda e: e.matmul(pss[:, 0:N], lhsT=kT[:, ks], rhs=qh[:, qs], start=False, stop=True), reads=[kT, qh], writes=[pss])

                                                def f_E(pss=pss, pt=pt, N=N):
                                                    S.op("act", lambda e: e.activation(out=pt[:, 0:N], in_=pss[:, 0:N], func=AF.Exp), reads=[pss], writes=[pt])

                                                def f_PV(psn=psn, psd=psd, pt=pt, tix=tix, pi=pi, o0=o0, N=N, lastt=lastt, first=(mi == 0)):
                                                    S.op("pe", lambda e: e.matmul(psn[:, o0:o0 + N], lhsT=vD3[pi][:, tix, :], rhs=pt[:, 0:N], start=first, stop=lastt, skip_group_check=True),
                                                         reads=[vD[pi], pt], writes=[psn], sig=False)
                                                    S.op("pe", lambda e: e.matmul(psd[:, o0:o0 + N], lhsT=onesb[:], rhs=pt[:, 0:N], start=first, stop=lastt, skip_group_check=True),
                                                         reads=[onesb, pt], writes=[psd], sig=True)

                                                tasks.append({"init": None, "S": f_S, "E": f_E, "PV": f_PV, "fin": f_fin if lastt else None})
                            emit_pipeline(tasks)
                            dma(ysc_t[4 + hp], yT[:], reads=[yT], writes=[yscB[4 + hp]])
                            allgather(ysc_t[4 + hp], ygat_t[4 + hp], yscB[4 + hp], ygatB[4 + hp])
                            S.flush()
                        S.scope = scH

                if do_ssm:
                    with ExitStack() as ph:
                        S.scope = ph
                        ups = [S.sb("up%d" % i, [128, 8 * 512], BF16) for i in range(2)]
                        zts = [S.sb("zt%d" % i, [128, S_LEN], BF16) for i in range(2)]
                        for ct in range(2):
                            up = ups[ct % 2]
                            up3 = up[:].rearrange("p (s m) -> p s m", s=8)
                            proj_fm(XA0 + ct * 128, lambda n, ps, up=up, up3=up3: evac_copy(up3[:, :, n * 64:(n + 1) * 64],
                                                                                   ps[:, 0:512].rearrange("p (m s) -> p s m", s=8), [ps], [up]))
                            dma(usc_d[ct].rearrange("g c s m -> (g c) s m"), up3, reads=[up])
                            zt = zts[ct % 2]
                            proj_fm(ZA0 + ct * 128, lambda n, ps, zt=zt: S.op("act", lambda e: e.activation(out=zt[:, n * 512:(n + 1) * 512], in_=ps[:, 0:512], func=AF.Silu),
                                                                              reads=[ps], writes=[zt]))
                            dma(zsc_d[ct], zt[:], reads=[zt])
                        S.flush()
                    S.scope = scH
            S.scope = es

            if do_ssm:
                with ExitStack() as scS:
                    S.scope = scS
                    lre = S.sb("lre", [128, 16], F32)
                    lim = S.sb("lim", [128, 16], F32)
                    ldt = S.sb("ldt", [128, 16], F32)
                    ar = S.sb("ar", [128, 16], F32)
                    ai = S.sb("ai", [128, 16], F32)
                    bre = S.sb("bre", [128, 256], F32)
                    bim = S.sb("bim", [128, 256], F32)
                    cre = S.sb("cre", [128, 256], F32)
                    cim = S.sb("cim", [128, 256], F32)
                    dsk = S.sb("dsk", [128, 16], F32)
                    etab = S.sb("etab", [128, NE], F32)
                    for t, d_ in ((lre, lre_d), (lim, lim_d), (ldt, ldt_d), (bre, bre_d), (bim, bim_d), (cre, cre_d), (cim, cim_d), (dsk, dsk_d)):
                        dma(t[:], d_[l], writes=[t])
                    dma(etab[:], etab_d, writes=[etab])
                    S.op("act", lambda e: e.activation(out=ldt[:], in_=ldt[:], func=AF.Exp), reads=[ldt], writes=[ldt])
                    S.op("dve", lambda e: e.tensor_tensor(out=ar[:], in0=lre[:], in1=ldt[:], op=ALU.mult), reads=[lre, ldt], writes=[ar])
                    S.op("dve", lambda e: e.tensor_tensor(out=ai[:], in0=lim[:], in1=ldt[:], op=ALU.mult), reads=[lim, ldt], writes=[ai])
                    YPs = []
                    for half in range(1):
                        G0 = 0
                        gs = slice(G0, G0 + 16)
                        with ExitStack() as scHf:
                            S.scope = scHf
                            NW = 16 * NE
                            zr = S.sb("zr", [128, NW], F32)
                            zi = S.sb("zi", [128, NW], F32)
                            scT = ExitStack()
                            S.scope = scT
                            tA = S.sb("tA", [128, NW], F32)
                            tB = S.sb("tB", [128, NW], F32)
                            tC = S.sb("tC", [128, NW], F32)
                            tD = S.sb("tD", [128, NW], F32)
                            tI = S.sb("tI", [128, NW], mybir.dt.int32)

                            def v3(b):
                                return b[:].rearrange("p (g e) -> p g e", g=16)

                            ar_b = ar[:, gs].unsqueeze(2).to_broadcast([128, 16, NE])
                            ai_b = ai[:, gs].unsqueeze(2).to_broadcast([128, 16, NE])
                            et_b = etab[:].unsqueeze(1).to_broadcast([128, 16, NE])
                            S.op("dve", lambda e: e.tensor_tensor(out=v3(tA), in0=ar_b, in1=et_b, op=ALU.mult), reads=[ar, etab], writes=[tA])
                            S.op("act", lambda e: e.activation(out=tA[:], in_=tA[:], func=AF.Exp), reads=[tA], writes=[tA])
                            S.op("dve", lambda e: e.tensor_tensor(out=v3(tB), in0=ai_b, in1=et_b, op=ALU.mult), reads=[ai, etab], writes=[tB])
                            S.op("dve", lambda e: e.tensor_scalar(out=tB[:], in0=tB[:], scalar1=1.0 / TWO_PI, scalar2=None, op0=ALU.mult), reads=[tB], writes=[tB])

                            def sin_of(dst, shift):
                                S.op("dve", lambda e: e.tensor_scalar(out=tC[:], in0=tB[:], scalar1=shift, scalar2=None, op0=ALU.add), reads=[tB], writes=[tC])
                                S.op("dve", lambda e: e.tensor_copy(out=tI[:], in_=tC[:]), reads=[tC], writes=[tI])
                                S.op("dve", lambda e: e.tensor_copy(out=tD[:], in_=tI[:]), reads=[tI], writes=[tD])
                                S.op("dve", lambda e: e.tensor_tensor(out=tC[:], in0=tC[:], in1=tD[:], op=ALU.subtract), reads=[tC, tD], writes=[tC])
                                S.op("dve", lambda e: e.tensor_scalar(out=tD[:], in0=tC[:], scalar1=0.5, scalar2=None, op0=ALU.is_gt), reads=[tC], writes=[tD])
                                S.op("dve", lambda e: e.tensor_tensor(out=tC[:], in0=tC[:], in1=tD[:], op=ALU.subtract), reads=[tC, tD], writes=[tC])
                                S.op("dve", lambda e: e.tensor_scalar(out=tD[:], in0=tC[:], scalar1=-0.5, scalar2=None, op0=ALU.is_lt), reads=[tC], writes=[tD])
                                S.op("dve", lambda e: e.tensor_tensor(out=tC[:], in0=tC[:], in1=tD[:], op=ALU.add), reads=[tC, tD], writes=[tC])
                                S.op("act", lambda e: e.activation(out=dst[:], in_=tC[:], func=AF.Sin, scale=TWO_PI), reads=[tC], writes=[dst])

                            sin_of(zi, 0.0)
                            sin_of(zr, 0.25)
                            S.op("dve", lambda e: e.tensor_tensor(out=zr[:], in0=zr[:], in1=tA[:], op=ALU.mult), reads=[zr, tA], writes=[zr])
                            S.op("dve", lambda e: e.tensor_tensor(out=zi[:], in0=zi[:], in1=tA[:], op=ALU.mult), reads=[zi, tA], writes=[zi])
                            S.flush()
                            scT.close()
                            S.scope = scHf
                            zr3 = v3(zr)
                            zi3 = v3(zi)
                            cf = S.sb("cf", [128, 16 * 8], F32)
                            cf3 = cf[:].rearrange("p (k g) -> p k g", k=8)
                            lre_h = lre[:, gs]
                            lim_h = lim[:, gs]

                            def cop(o, a, b, op):
                                S.op("dve", lambda e: e.tensor_tensor(out=o, in0=a, in1=b, op=op), reads=[zr, zi, lre, lim, cf], writes=[cf])

                            S.op("dve", lambda e: e.tensor_scalar(out=cf3[:, 0, :], in0=zr3[:, :, 80], scalar1=-1.0, scalar2=None, op0=ALU.add), reads=[zr], writes=[cf])
                            cop(cf3[:, 1, :], lre_h, lre_h, ALU.mult)
                            cop(cf3[:, 2, :], lim_h, lim_h, ALU.mult)
                            cop(cf3[:, 1, :], cf3[:, 1, :], cf3[:, 2, :], ALU.add)
                            S.op("dve", lambda e: e.reciprocal(out=cf3[:, 1, :], in_=cf3[:, 1, :]), reads=[cf], writes=[cf])
                            cop(cf3[:, 2, :], cf3[:, 0, :], lre_h, ALU.mult)
                            cop(cf3[:, 3, :], zi3[:, :, 80], lim_h, ALU.mult)
                            cop(cf3[:, 2, :], cf3[:, 2, :], cf3[:, 3, :], ALU.add)
                            cop(cf3[:, 4, :], cf3[:, 2, :], cf3[:, 1, :], ALU.mult)
                            cop(cf3[:, 2, :], zi3[:, :, 80], lre_h, ALU.mult)
                            cop(cf3[:, 3, :], cf3[:, 0, :], lim_h, ALU.mult)
                            cop(cf3[:, 2, :], cf3[:, 2, :], cf3[:, 3, :], ALU.subtract)
                            cop(cf3[:, 5, :], cf3[:, 2, :], cf3[:, 1, :], ALU.mult)
                            Bbr = S.sb("Bbr", [128, 256], F32)
                            Bbi = S.sb("Bbi", [128, 256], F32)
                            tb1 = S.sb("tb1", [128, 256], F32)

                            def g3(ap):
                                return ap.rearrange("p (g c) -> p g c", g=16)

                            qr_b = cf3[:, 4, :].unsqueeze(2).to_broadcast([128, 16, 16])
                            qi_b = cf3[:, 5, :].unsqueeze(2).to_broadcast([128, 16, 16])
                            bre_h = g3(bre[:, G0 * 16:G0 * 16 + 256])
                            bim_h = g3(bim[:, G0 * 16:G0 * 16 + 256])

                            def bop(o, a, b, op, wr):
                                S.op("dve", lambda e: e.tensor_tensor(out=o, in0=a, in1=b, op=op), reads=[cf, bre, bim, tb1, Bbr, Bbi], writes=[wr])

                            bop(g3(Bbr[:]), qr_b, bre_h, ALU.mult, Bbr)
                            bop(g3(tb1[:]), qi_b, bim_h, ALU.mult, tb1)
                            bop(Bbr[:], Bbr[:], tb1[:], ALU.subtract, Bbr)
                            bop(g3(Bbi[:]), qr_b, bim_h, ALU.mult, Bbi)
                            bop(g3(tb1[:]), qi_b, bre_h, ALU.mult, tb1)
                            bop(Bbi[:], Bbi[:], tb1[:], ALU.add, Bbi)
                            A4 = S.sb("A4", [128, 64], F32)
                            A44 = A4[:].rearrange("p (g o i) -> p g o i", g=16, o=2)
                            S.op("dve", lambda e: e.tensor_copy(out=A44[:, :, 0, 0], in_=zr3[:, :, 81]), reads=[zr], writes=[A4])
                            S.op("dve", lambda e: e.tensor_copy(out=A44[:, :, 1, 1], in_=zr3[:, :, 81]), reads=[zr], writes=[A4])
                            S.op("dve", lambda e: e.tensor_copy(out=A44[:, :, 1, 0], in_=zi3[:, :, 81]), reads=[zi], writes=[A4])
                            S.op("dve", lambda e: e.tensor_scalar(out=A44[:, :, 0, 1], in0=zi3[:, :, 81], scalar1=-1.0, scalar2=None, op0=ALU.mult), reads=[zi], writes=[A4])

                            A2c = S.sb("A2c", [128, 128], F32)
                            for c_ in range(2):
                                A2v = A2c[:, c_ * 64:(c_ + 1) * 64].rearrange("p (g o i) -> p g o i", g=16, o=2)
                                S.op("dve", lambda e, A2v=A2v: e.tensor_copy(out=A2v[:, :, 0, 0], in_=zr3[:, :, 82]), reads=[zr], writes=[A2c])
                                S.op("dve", lambda e, A2v=A2v: e.tensor_copy(out=A2v[:, :, 1, 1], in_=zr3[:, :, 82]), reads=[zr], writes=[A2c])
                                S.op("dve", lambda e, A2v=A2v: e.tensor_copy(out=A2v[:, :, 1, 0], in_=zi3[:, :, 82]), reads=[zi], writes=[A2c])
                                S.op("dve", lambda e, A2v=A2v: e.tensor_scalar(out=A2v[:, :, 0, 1], in0=zi3[:, :, 82], scalar1=-1.0, scalar2=None, op0=ALU.mult), reads=[zi], writes=[A2c])
                            U = S.sb("U", [128, 16 * 512], BF16)
                            U3 = U[:].rearrange("p (g m) -> p g m", g=16)
                            for gb in range(2):
                                ct = gb
                                for s in range(8):
                                    dma(U3[16 * s:16 * s + 16, gb * 8:gb * 8 + 8, :], usc_d[ct][:, :, s, :].rearrange("g c m -> c g m"), writes=[U], par=True)
                            Hs = S.sb("Hs", [128, 128 * 32], F32)
                            Hs_gr = Hs[:].rearrange("p (k g r) -> p g r k", g=16, r=2)
                            Hs_k = Hs[:].rearrange("p (k g r) -> p k g r", g=16, r=2)
                            Hb = S.sb("Hb", [128, 32 * 128], BF16)
                            Hb4 = Hb[:].rearrange("p (g r k) -> p g r k", g=16, r=2)
                            Cg = g3

                            def cprod(dst_r, dst_i, zr_ap, zi_ap, Xr_ap, Xi_ap, t1, t2, neg_i, rd, wr):
                                S.op("dve", lambda e: e.tensor_tensor(out=t1, in0=zr_ap, in1=Xr_ap, op=ALU.mult), reads=rd, writes=wr)
                                S.op("pool", lambda e: e.tensor_tensor(out=t2, in0=zi_ap, in1=Xi_ap, op=ALU.mult), reads=rd, writes=wr)
                                S.op("dve", lambda e: e.tensor_tensor(out=dst_r, in0=t1, in1=t2, op=ALU.subtract), reads=rd + wr, writes=wr)
                                S.op("dve", lambda e: e.tensor_tensor(out=t1, in0=zr_ap, in1=Xi_ap, op=ALU.mult), reads=rd + wr, writes=wr)
                                S.op("pool", lambda e: e.tensor_tensor(out=t2, in0=zi_ap, in1=Xr_ap, op=ALU.mult), reads=rd + wr, writes=wr)
                                if neg_i:
                                    S.op("dve", lambda e: e.scalar_tensor_tensor(out=dst_i, in0=t1, scalar=-1.0, in1=t2, op0=ALU.mult, op1=ALU.subtract),
                                         reads=rd + wr, writes=wr)
                                else:
                                    S.op("dve", lambda e: e.tensor_tensor(out=dst_i, in0=t1, in1=t2, op=ALU.add), reads=rd + wr, writes=wr)

                            with ExitStack() as scP:
                                S.scope = scP
                                for gb in range(2):
                                    gsl = slice(gb * 8, gb * 8 + 8)
                                    t1 = S.sb("gt1", [128, 1024], F32)
                                    t2 = S.sb("gt2", [128, 1024], F32)
                                    PBr = S.sb("PBr", [128, 1024], BF16)
                                    PBi = S.sb("PBi", [128, 1024], BF16)
                                    Pw = S.sb("Pw", [128, 8 * 4 * 2 * 128], BF16)
                                    Pw5 = Pw[:].rearrange("p (g j r q) -> p g j r q", g=8, j=4, r=2)
                                    grp = [t1, t2, PBr, PBi]

                                    def v4(b):
                                        return b[:].rearrange("p (g s c) -> p g s c", g=8, s=8)

                                    Br_b = g3(Bbr[:])[:, gsl, :].unsqueeze(2).to_broadcast([128, 8, 8, 16])
                                    Bi_b = g3(Bbi[:])[:, gsl, :].unsqueeze(2).to_broadcast([128, 8, 8, 16])
                                    for j in range(4):
                                        zr_b = zr3[:, gsl, 48 + 8 * j:56 + 8 * j].unsqueeze(3).to_broadcast([128, 8, 8, 16])
                                        zi_b = zi3[:, gsl, 48 + 8 * j:56 + 8 * j].unsqueeze(3).to_broadcast([128, 8, 8, 16])
                                        cprod(v4(PBr), v4(PBi), zr_b, zi_b, Br_b, Bi_b, v4(t1), v4(t2), False, [zr, zi, Bbr, Bbi], grp)
                                        for g in range(8):
                                            for ri, PB in enumerate((PBr, PBi)):
                                                slot = (g % 4) * 2 + ri
                                                S.op("pe", lambda e, PB=PB, g=g, slot=slot: e.transpose(out=PSB[:, slot * 128:(slot + 1) * 128], in_=PB[:, g * 128:(g + 1) * 128],
                                                                                                       identity=identb[:]), reads=[PB, identb], writes=[PSB], sig=(g % 4 == 3 and ri == 1))
                                            if g % 4 == 3:
                                                g0 = g - 3
                                                evac_copy(Pw5[:, g0:g0 + 4, j, :, :], PSB[:].rearrange("p (g r q) -> p g r q", g=4, r=2), [PSB], [Pw])
                                    for g in range(8):
                                        gl = gb * 8 + g
                                        ps = PS[5 + ((g // 2) % 2)]
                                        for ri in range(2):
                                            slot = (g % 2) * 2 + ri
                                            for j in range(4):
                                                S.op("pe", lambda e, ps=ps, slot=slot, g=g, j=j, ri=ri, gl=gl, Pw5=Pw5: e.matmul(
                                                    ps[:, slot * 128:(slot + 1) * 128], lhsT=Pw5[:, g, j, ri, :], rhs=U3[:, gl, j:512:4], start=(j == 0), stop=(j == 3)),
                                                    reads=[Pw, U], writes=[ps], sig=(j == 3 and ri == 1 and g % 2 == 1))
                                        if g % 2 == 1:
                                            evac_copy(Hs_gr[:, gl - 1:gl + 1, :, :], ps[:, 0:512].rearrange("p (g r k) -> p g r k", g=2, r=2), [ps], [Hs])
                                S.flush()
                            S.scope = scHf

                            scSc = ExitStack()
                            S.scope = scSc
                            Hf = Buf(Hs.t, "Hf")
                            Hbk = Buf(Hs.t, "Hbk")

                            scn = S.sb("scn", [128, 127 * 64], F32)
                            sc5 = scn[:].rearrange("p (k g o i) -> p k g o i", k=127, g=16, o=2)
                            scF = Buf(scn.t, "scF")
                            scB = Buf(scn.t, "scB")
                            prf2 = S.sb("prf2", [128, 128], F32)
                            prb2 = S.sb("prb2", [128, 128], F32)
                            Hs2 = Hs[:].rearrange("p (kg r) -> p kg r", r=2)

                            def scan_pre(eng, sl, src_lo, dst_lo, Hx, scX):
                                for o in range(2):
                                    a_b = A44[sl][:, :, o, :].unsqueeze(1).to_broadcast([64, 127, 16, 2])
                                    S.op(eng, lambda e, o=o, a_b=a_b: e.tensor_tensor(out=sc5[sl, :, :, o, :], in0=a_b, in1=Hs_k[sl, src_lo:src_lo + 127, :, :], op=ALU.mult),
                                         reads=[A4, Hx], writes=[scX])
                                S.op(eng, lambda e: e.tensor_tensor(out=sc5[sl, :, :, :, 0], in0=sc5[sl, :, :, :, 0], in1=sc5[sl, :, :, :, 1], op=ALU.add), reads=[scX], writes=[scX])
                                S.op(eng, lambda e: e.tensor_tensor(out=Hs_k[sl, dst_lo:dst_lo + 127, :, :], in0=Hs_k[sl, dst_lo:dst_lo + 127, :, :], in1=sc5[sl, :, :, :, 0], op=ALU.add),
                                     reads=[scX], writes=[Hx])

                            def scan_step2(eng, kd, ks, sl, pr, Hx):
                                pr4 = pr[sl, :].rearrange("p (cg o i) -> p cg o i", o=2, i=2)
                                a4 = A2c[sl, :].rearrange("p (cg o i) -> p cg o i", o=2, i=2)
                                X = Hs2[sl, ks * 16:ks * 16 + 32, :].unsqueeze(2).to_broadcast([64, 32, 2, 2])
                                S.op(eng, lambda e: e.tensor_tensor(out=pr4, in0=a4, in1=X, op=ALU.mult), reads=[A2c, Hx], writes=[pr])
                                S.op(eng, lambda e: e.tensor_tensor(out=pr4[:, :, :, 0], in0=pr4[:, :, :, 0], in1=pr4[:, :, :, 1], op=ALU.add), reads=[pr], writes=[pr])
                                S.op(eng, lambda e: e.tensor_tensor(out=Hs2[sl, kd * 16:kd * 16 + 32, :], in0=Hs2[sl, kd * 16:kd * 16 + 32, :], in1=pr4[:, :, :, 0], op=ALU.add),
                                     reads=[pr], writes=[Hx])

                            scan_pre("dve", slice(0, 64), 0, 1, Hf, scF)
                            scan_pre("pool", slice(64, 128), 1, 0, Hbk, scB)
                            for m in range(1, 64):
                                scan_step2("dve", 2 * m, 2 * m - 2, slice(0, 64), prf2, Hf)
                                mb = 63 - m
                                scan_step2("pool", 2 * mb, 2 * mb + 2, slice(64, 128), prb2, Hbk)
                            S.op("dve", lambda e: e.tensor_copy(out=Hb4[0:64], in_=Hs_gr[0:64]), reads=[Hf], writes=[Hb])
                            S.op("dve", lambda e: e.tensor_copy(out=Hb4[64:128], in_=Hs_gr[64:128]), reads=[Hbk, Hb], writes=[Hb])
                            S.flush()
                            scSc.close()
                            S.scope = scHf

                            YP = S.sb("YP", [128, 16 * 512], BF16)
                            YP3 = YP[:].rearrange("p (g m) -> p g m", g=16)
                            YPg = [Buf(YP.t, "YPg0"), Buf(YP.t, "YPg1")]
                            with ExitStack() as scP:
                                S.scope = scP
                                t1 = S.sb("ht1", [128, 1024], F32)
                                t2 = S.sb("ht2", [128, 1024], F32)
                                gxs = [S.sb("gxs%d" % i, [128, 512], F32) for i in range(2)]
                                gus = [S.sb("gus%d" % i, [128, 512], F32) for i in range(2)]
                                for gb in range(2):
                                    gsl = slice(gb * 8, gb * 8 + 8)
                                    PBr = S.sb("PB0r", [128, 1024], BF16)
                                    PBi = S.sb("PB0i", [128, 1024], BF16)
                                    QCr = [S.sb("QCr%d" % m, [128, 1024], BF16) for m in range(5)]
                                    QCi = [S.sb("QCi%d" % m, [128, 1024], BF16) for m in range(5)]
                                    Tw = S.sb("Tw", [128, 8 * 7 * 128], BF16)
                                    Tw4 = Tw[:].rearrange("p (g d q) -> p g d q", g=8, d=7)
                                    tt0 = S.sb("tt0", [128, 128], F32)
                                    tt1 = S.sb("tt1", [128, 128], F32)

                                    def v4(b):
                                        return b[:].rearrange("p (g s c) -> p g s c", g=8, s=8)

                                    Br_b = g3(Bbr[:])[:, gsl, :].unsqueeze(2).to_broadcast([128, 8, 8, 16])
                                    Bi_b = g3(Bbi[:])[:, gsl, :].unsqueeze(2).to_broadcast([128, 8, 8, 16])
                                    zr_b = zr3[:, gsl, 0:8].unsqueeze(3).to_broadcast([128, 8, 8, 16])
                                    zi_b = zi3[:, gsl, 0:8].unsqueeze(3).to_broadcast([128, 8, 8, 16])
                                    cprod(v4(PBr), v4(PBi), zr_b, zi_b, Br_b, Bi_b, v4(t1), v4(t2), False, [zr, zi, Bbr, Bbi], [t1, t2, PBr, PBi])
                                    cre_h = g3(cre[:, G0 * 16:G0 * 16 + 256])[:, gsl, :].unsqueeze(2).to_broadcast([128, 8, 8, 16])
                                    cim_h = g3(cim[:, G0 * 16:G0 * 16 + 256])[:, gsl, :].unsqueeze(2).to_broadcast([128, 8, 8, 16])
                                    for m in range(5):
                                        zr_m = zr3[:, gsl, 8 + 8 * m:16 + 8 * m].unsqueeze(3).to_broadcast([128, 8, 8, 16])
                                        zi_m = zi3[:, gsl, 8 + 8 * m:16 + 8 * m].unsqueeze(3).to_broadcast([128, 8, 8, 16])
                                        cprod(v4(QCr[m]), v4(QCi[m]), zr_m, zi_m, cre_h, cim_h, v4(t1), v4(t2), True, [zr, zi, cre, cim], [t1, t2, QCr[m], QCi[m]])
                                    F_ = slice(0, 64)
                                    B_ = slice(64, 128)
                                    TwB = [Buf(Tw.t, "TwB%d" % g_) for g_ in range(8)]

                                    def tgen(g):
                                        gq = slice(g * 128, (g + 1) * 128)
                                        psX = PS[0 + (g % 2)]
                                        psY = PS[2 + (g % 2)]
                                        jobs = [(psX, 0, F_, 1), (psX, 1, F_, 2), (psX, 2, F_, 3), (psX, 3, F_, 0),
                                                (psY, 0, B_, 1), (psY, 1, B_, 2), (psY, 2, B_, 3), (psY, 3, B_, 0)]
                                        for (pz, slot, dsl, m) in jobs:
                                            S.op("pe", lambda e, pz=pz, slot=slot, dsl=dsl, m=m, gq=gq, PBr=PBr, QCr=QCr: e.matmul(pz[:, slot * 128:(slot + 1) * 128], lhsT=PBr[dsl, gq], rhs=QCr[m][dsl, gq],
                                                                                                               start=True, stop=False), reads=[PBr, QCr[m]], writes=[pz], sig=False)
                                            S.op("pe", lambda e, pz=pz, slot=slot, dsl=dsl, m=m, gq=gq, PBi=PBi, QCi=QCi: e.matmul(pz[:, slot * 128:(slot + 1) * 128], lhsT=PBi[dsl, gq], rhs=QCi[m][dsl, gq],
                                                                                                               start=False, stop=True), reads=[PBi, QCi[m]], writes=[pz], sig=(slot == 3))
                                        evac_copy(Tw4[:, g, 0:3, :], psX[:, 0:384].rearrange("p (d q) -> p d q", d=3), [psX], [TwB[g]])
                                        evac_copy(Tw4[:, g, 3:6, :], psY[:, 0:384].rearrange("p (d q) -> p d q", d=3), [psY], [TwB[g]])
                                        S.op("dve", lambda e, psX=psX, tt0=tt0: e.tensor_tensor(out=tt0[:], in0=psX[:, 384:512], in1=mge[:], op=ALU.mult), reads=[psX, mge], writes=[tt0])
                                        S.op("dve", lambda e, psY=psY, tt1=tt1: e.tensor_tensor(out=tt1[:], in0=psY[:, 384:512], in1=mle[:], op=ALU.mult), reads=[psY, mle], writes=[tt1])
                                        S.op("dve", lambda e, tt0=tt0, tt1=tt1: e.tensor_tensor(out=tt0[:], in0=tt0[:], in1=tt1[:], op=ALU.add), reads=[tt0, tt1], writes=[tt0])
                                        Gg = G0 + gb * 8 + g
                                        S.op("dve", lambda e, g=g, Gg=Gg, Tw4=Tw4, tt0=tt0: e.scalar_tensor_tensor(out=Tw4[:, g, 6, :], in0=identf[:], scalar=dsk[:, Gg:Gg + 1], in1=tt0[:],
                                                                                                 op0=ALU.mult, op1=ALU.add), reads=[identf, dsk, tt0], writes=[TwB[g]])
                                    tidx = {1: 0, 2: 1, 3: 2, -1: 3, -2: 4, -3: 5, 0: 6}
                                    def ycomp(g):
                                        gl = gb * 8 + g
                                        gq = slice(g * 128, (g + 1) * 128)
                                        ps = PS[5 + (g % 2)]
                                        psB = PS[4]
                                        for i in range(4):
                                            o0 = i * 128
                                            for j in range(4):
                                                S.op("pe", lambda e, ps=ps, o0=o0, g=g, i=i, j=j, gl=gl, Tw4=Tw4: e.matmul(ps[:, o0:o0 + 128], lhsT=Tw4[:, g, tidx[i - j], :], rhs=U3[:, gl, j:512:4],
                                                                                                                 start=(j == 0), stop=False), reads=[TwB[g], U], writes=[ps], sig=False)
                                            S.op("pe", lambda e, ps=ps, o0=o0, i=i, gq=gq, gl=gl, QCr=QCr: e.matmul(ps[:, o0 + 1:o0 + 128], lhsT=QCr[i][F_, gq], rhs=Hb4[F_, gl, 0, 0:127],
                                                                                                          start=False, stop=False), reads=[QCr[i], Hb], writes=[ps], sig=False)
                                            S.op("pe", lambda e, ps=ps, o0=o0, i=i, gq=gq, gl=gl, QCi=QCi: e.matmul(ps[:, o0 + 1:o0 + 128], lhsT=QCi[i][F_, gq], rhs=Hb4[F_, gl, 1, 0:127],
                                                                                                          start=False, stop=True), reads=[QCi[i], Hb], writes=[ps], sig=(i == 3))
                                        for i in range(4):
                                            o0 = i * 128
                                            S.op("pe", lambda e, psB=psB, o0=o0, i=i, gq=gq, gl=gl, QCr=QCr: e.matmul(psB[:, o0:o0 + 127], lhsT=QCr[4 - i][B_, gq], rhs=Hb4[B_, gl, 0, 1:128],
                                                                                                            start=True, stop=False), reads=[QCr[4 - i], Hb], writes=[psB], sig=False)
                                            S.op("pe", lambda e, psB=psB, o0=o0, i=i, gq=gq, gl=gl, QCi=QCi: e.matmul(psB[:, o0:o0 + 127], lhsT=QCi[4 - i][B_, gq], rhs=Hb4[B_, gl, 1, 1:128],
                                                                                                            start=False, stop=True), reads=[QCi[4 - i], Hb], writes=[psB], sig=(i == 3))
                                        xs = gxs[g % 2]
                                        us = gus[g % 2]
                                        S.op("act", lambda e, ps=ps, xs=xs: e.activation(out=xs[:], in_=ps[:, 0:512], func=AF.Copy), reads=[ps], writes=[xs])
                                        S.op("dve", lambda e, psB=psB, xs=xs: e.tensor_tensor(out=xs[:].rearrange("p (i k) -> p i k", i=4)[:, :, 0:127],
                                                                                             in0=xs[:].rearrange("p (i k) -> p i k", i=4)[:, :, 0:127],
                                                                                             in1=psB[:, 0:512].rearrange("p (i k) -> p i k", i=4)[:, :, 0:127], op=ALU.add),
                                             reads=[psB, xs], writes=[xs])
                                        S.op("dve", lambda e, xs=xs, us=us: e.tensor_tensor(out=us[:], in0=xs[:], in1=xs[:], op=ALU.mult), reads=[xs], writes=[us])
                                        S.op("dve", lambda e, us=us: e.tensor_scalar(out=us[:], in0=us[:], scalar1=0.044715, scalar2=1.0, op0=ALU.mult, op1=ALU.add), reads=[us], writes=[us])
                                        S.op("dve", lambda e, xs=xs, us=us: e.tensor_tensor(out=us[:], in0=us[:], in1=xs[:], op=ALU.mult), reads=[xs, us], writes=[us])
                                        S.op("act", lambda e, us=us: e.activation(out=us[:], in_=us[:], func=AF.Sigmoid, scale=1.5957691216057308), reads=[us], writes=[us])
                                        S.op("pool", lambda e, xs=xs, us=us, gl=gl: e.tensor_tensor(out=YP3[:, gl, :], in0=xs[:], in1=us[:], op=ALU.mult), reads=[xs, us], writes=[YPg[gb]])

                                    for g in range(8):
                                        tgen(g)
                                        if g >= 1:
                                            ycomp(g - 1)
                                    ycomp(7)
                                    ct = gb
                                    ys2v = ys2_t[ct].rearrange("(g c) (s m) -> g c s m", c=16, s=8)
                                    for s in range(8):
                                        dma(ys2v[:, :, s, :].rearrange("g c m -> c g m"), YP3[16 * s:16 * s + 16, gb * 8:gb * 8 + 8, :], reads=[YPg[gb]], writes=[ys2B[ct]], par=True)
                                    allgather(ys2_t[ct], ys2g_t[ct], ys2B[ct], ys2gB[ct])
                                S.flush()
                            S.scope = scHf
                        S.scope = scS

                    with ExitStack() as scG:
                        S.scope = scG
                        yg = S.sb("yg", [128, 4 * 4096], BF16)
                        yg4 = yg[:].rearrange("p (c t m) -> p c t m", c=4, t=8)
                        for ct in range(4):
                            dma(yg4[:, ct, :, :], ys2g_t[ct % 2][(ct // 2) * 128:(ct // 2 + 1) * 128, :].rearrange("p (t m) -> p t m", t=8), reads=[ys2gB[ct % 2]], writes=[yg], par=True)
                        ygl = S.sb("ygl", [128, 2 * 4096], BF16)
                        ygl4 = ygl[:].rearrange("p (c t m) -> p c t m", c=2, t=8)
                        for ct in range(2):
                            dma(ygl4[:, ct, :, :], ys2_t[ct].rearrange("p (t m) -> p t m", t=8), reads=[ys2B[ct]], writes=[ygl], par=True)
                        gws = S.sb("gws", [128, 4 * 256], F32)
                        gwb = S.sb("gwb", [128, 4 * 256], BF16)
                        gwb3 = gwb[:].rearrange("p (k o) -> p k o", k=4)
                        glb = S.sb("glb", [128, 2], F32)
                        dma(gws[:].rearrange("p (k o) -> p k o", k=4), gluw_d[l].rearrange("(k p) o -> p k o", p=128), writes=[gws])
                        dma(glb[:], glub_d[l], writes=[glb])
                        S.op("pool", lambda e: e.tensor_copy(out=gwb[:], in_=gws[:]), reads=[gws], writes=[gwb])
                        zts = [S.sb("zg%d" % i, [128, S_LEN], BF16) for i in range(2)]
                        yos = [S.sb("yo%d" % i, [128, S_LEN], BF16) for i in range(2)]
                        sgt = [S.sb("sgt%d" % i, [128, 512], BF16) for i in range(2)]
                        tmg = [S.sb("tmg%d" % i, [128, 512], BF16) for i in range(2)]
                        for ot in range(2):
                            zt = zts[ot % 2]
                            yo = yos[ot % 2]
                            dma(zt[:], zsc_d[ot], writes=[zt])
                            ztv = zt[:].rearrange("p (k i t) -> p t i k", i=4, t=8)
                            yov = yo[:].rearrange("p (k i t) -> p t i k", i=4, t=8)
                            for t in range(8):
                                ps = PS[5 + (t % 2)]
                                sg = sgt[t % 2]
                                tm = tmg[t % 2]
                                for kt in range(4):
                                    S.op("pe", lambda e, ps=ps, kt=kt, ot=ot, t=t: e.matmul(ps[:, 0:512], lhsT=gwb3[:, kt, ot * 128:(ot + 1) * 128], rhs=yg4[:, kt, t, :],
                                                                                           start=(kt == 0), stop=(kt == 3)), reads=[gwb, yg], writes=[ps], sig=(kt == 3))
                                S.op("act", lambda e, ps=ps, sg=sg, ot=ot: e.activation(out=sg[:], in_=ps[:, 0:512], func=AF.Sigmoid, bias=glb[:, ot:ot + 1], scale=1.0),
                                     reads=[ps, glb], writes=[sg])
                                S.op("dve", lambda e, sg=sg, tm=tm, ot=ot, t=t: e.tensor_tensor(out=tm[:], in0=ygl4[:, ot, t, :], in1=sg[:], op=ALU.mult), reads=[ygl, sg], writes=[tm])
                                S.op("pool", lambda e, tm=tm, t=t, yov=yov, ztv=ztv: e.tensor_tensor(out=yov[:, t, :, :], in0=tm[:].rearrange("p (i k) -> p i k", i=4),
                                                                                                   in1=ztv[:, t, :, :], op=ALU.mult), reads=[tm, zt], writes=[yo])
                            dma(ysc_t[ot], yo[:], reads=[yo], writes=[yscB[ot]])
                            allgather(ysc_t[ot], ygat_t[ot], yscB[ot], ygatB[ot])
                        S.flush()
                    S.scope = scS
                S.scope = es

            with ExitStack() as scO:
                S.scope = scO
                wos = [S.sb("wos%d" % i, [128, D], F32) for i in range(2)]
                wo = S.sb("wo", [128, 12 * D], BF16)
                wo3 = wo[:].rearrange("p (k o) -> p k o", k=12)
                for kt in range(12):
                    ws = wos[kt % 2]
                    dma(ws[:], wout_d[l, kt * 128:(kt + 1) * 128, :], writes=[ws])
                    S.op("pool", lambda e, ws=ws, kt=kt: e.tensor_copy(out=wo3[:, kt, :], in_=ws[:]), reads=[ws], writes=[wo])
                fg = S.sb("fg", [128, D], F32)
                if last:
                    dma(fg[:], finalg_d.partition_broadcast(128), writes=[fg])
                yts = [S.sb("yt%d" % i, [128, 12 * 512], BF16) for i in range(2)]
                ytSs = [Buf(yts[i].t, "ytS%d" % i) for i in range(2)]
                korder = [4, 5, 6, 7, 8, 9, 10, 11, 0, 1, 2, 3]
                xts = [S.sb("xo%d" % i, [128, D], F32) for i in range(4)]
                xns = [S.sb("xn%d" % i, [128, D], F32) for i in range(4)]
                junk = S.sb("junk2", [128, D], BF16)
                sts2 = [S.sb("st2_%d" % i, [128, 4 * 8], F32) for i in range(4)]
                for st_ in sts2:
                    S.op("dve", lambda e, st_=st_: e.memset(st_[:], 0.0), writes=[st_])
                xdst = out_d if last else x1_d
                for nb_ in range(8):
                    yt = yts[nb_ % 2]
                    yt3 = yt[:].rearrange("p (k n) -> p k n", k=12)
                    ytS = ytSs[nb_ % 2]
                    for kt in range(12):
                        s_, rr_ = divmod(kt, 2)
                        dma(yt3[:, kt, :], ygat_t[s_][rr_ * 128:(rr_ + 1) * 128, nb_ * 512:(nb_ + 1) * 512], reads=[ygatB[s_]], writes=[ytS if kt < 4 else yt], par=True)
                    for tq in range(4):
                        tt = nb_ * 4 + tq
                        xt = xts[tt % 4]
                        xn = xns[tt % 4]
                        dma(xt[:], xsrc[tt * 128:(tt + 1) * 128, :], writes=[xt], q="act")
                        for oh in range(2):
                            ps = PS[5 + oh]
                            for ki, kt in enumerate(korder):
                                S.op("pe", lambda e, ps=ps, kt=kt, ki=ki, oh=oh, tq=tq, yt3=yt3: e.matmul(ps[:, 0:512], lhsT=yt3[:, kt, tq * 128:(tq + 1) * 128], rhs=wo3[:, kt, oh * 512:(oh + 1) * 512],
                                                                                                         start=(ki == 0), stop=(ki == 11)), reads=[(ytS if kt < 4 else yt), wo], writes=[ps], sig=(ki == 11))
                            S.op("dve", lambda e, ps=ps, oh=oh, xt=xt, xn=xn: e.tensor_tensor(out=xn[:, oh * 512:(oh + 1) * 512], in0=ps[:, 0:512], in1=xt[:, oh * 512:(oh + 1) * 512], op=ALU.add),
                                 reads=[ps, xt], writes=[xn])
                        if last:
                            c0 = (tt // 4) * 4
                            st = sts2[tt % 4]
                            S.op("act", lambda e, xn=xn, c0=c0, st=st: e.activation(out=junk[:], in_=xn[:], func=AF.Square, accum_out=st[:, c0:c0 + 1]), reads=[xn], writes=[junk, st])
                            S.op("dve", lambda e, c0=c0, st=st: e.tensor_scalar(out=st[:, c0 + 1:c0 + 2], in0=st[:, c0:c0 + 1], scalar1=1.0 / D, scalar2=1e-6, op0=ALU.mult, op1=ALU.add),
                                 reads=[st], writes=[st])
                            S.op("act", lambda e, c0=c0, st=st: e.activation(out=st[:, c0 + 2:c0 + 3], in_=st[:, c0 + 1:c0 + 2], func=AF.Sqrt), reads=[st], writes=[st])
                            S.op("dve", lambda e, c0=c0, st=st: e.reciprocal(out=st[:, c0 + 3:c0 + 4], in_=st[:, c0 + 2:c0 + 3]), reads=[st], writes=[st])
                            S.op("dve", lambda e, xn=xn, c0=c0, st=st: e.scalar_tensor_tensor(out=xn[:], in0=xn[:], scalar=st[:, c0 + 3:c0 + 4], in1=fg[:], op0=ALU.mult, op1=ALU.mult),
                                 reads=[xn, st, fg], writes=[xn])
                        dma(xdst[tt * 128:(tt + 1) * 128, :], xn[:], reads=[xn], q="sp")
                S.flush()
            S.scope = es
        print("instructions recorded:", S.nins)
    return nc


def make_in_maps(inputs, ncores=8):
    f = np.float32
    mge, mle = _masks()
    ident = np.eye(128, dtype=f)
    etab = _etab()

    def dp(a):
        return np.ascontiguousarray(np.transpose(a, (0, 1, 3, 2)).reshape(2, 128, 32), f)

    lre = dp(inputs["ssm_lam_re"])
    lim = dp(inputs["ssm_lam_im"])
    ldt = np.ascontiguousarray(np.broadcast_to(inputs["ssm_log_dt"][:, :, None, :], (2, 2, 64, 32)).reshape(2, 128, 32), f)
    bre = np.ascontiguousarray(np.transpose(inputs["ssm_b_re"], (0, 1, 3, 2, 4)).reshape(2, 128, 512), f)
    bim = np.ascontiguousarray(np.transpose(inputs["ssm_b_im"], (0, 1, 3, 2, 4)).reshape(2, 128, 512), f)
    cre = np.ascontiguousarray(np.transpose(inputs["ssm_c_re"], (0, 1, 4, 2, 3)).reshape(2, 128, 512), f)
    cim = np.ascontiguousarray(np.transpose(inputs["ssm_c_im"], (0, 1, 4, 2, 3)).reshape(2, 128, 512), f)
    dd = inputs["ssm_d"].reshape(2, 32, 16)
    dsk = np.ascontiguousarray(np.broadcast_to(np.transpose(dd, (0, 2, 1))[:, None, :, :], (2, 8, 16, 32)).reshape(2, 128, 32), f)
    glub = np.ascontiguousarray(np.transpose(inputs["glu_b"].reshape(2, 4, 128), (0, 2, 1)), f)
    gna = np.stack([_gather_na(np.asarray(inputs["na_rpb"][l], f)) for l in range(2)])
    gdil = _gather_dil(np.asarray(inputs["t5_bias"], f))
    w_in = np.asarray(inputs["w_in"], f)
    w_out = np.asarray(inputs["w_out"], f)
    glu_w = np.asarray(inputs["glu_w"], f)
    gt = []
    for s_ in range(6):
        for rr in range(2):
            base = (0, 0, 4, 4, 8, 8)[s_]
            gt.append(base + 2 * rr + (s_ % 2))
    rows = np.concatenate([np.arange(t * 128, (t + 1) * 128) for t in gt])
    w_out_p = np.ascontiguousarray(w_out[:, rows, :])
    maps = []
    for c in range(ncores):
        b, r = divmod(c, 2)
        b = b % 4
        tl = (2 * r, 2 * r + 1)
        idx = np.concatenate([np.arange(base + t * 128, base + (t + 1) * 128)
                              for base in (0, 512, 1024, 1536, 2048, 2560, 3072, 3584, 4096, 4608) for t in tl])
        m = {
            "x": np.ascontiguousarray(inputs["x"][b], f),
            "norm_g": np.ascontiguousarray(inputs["norm_g"], f),
            "final_g": np.ascontiguousarray(inputs["final_g"], f),
            "w_in": np.ascontiguousarray(w_in[:, :, idx]),
            "w_out": w_out_p,
            "ident": ident, "mge": mge, "mle": mle, "etab": etab,
            "lam_re_t": np.ascontiguousarray(lre[:, :, 16 * r:16 * r + 16]),
            "lam_im_t": np.ascontiguousarray(lim[:, :, 16 * r:16 * r + 16]),
            "logdt_t": np.ascontiguousarray(ldt[:, :, 16 * r:16 * r + 16]),
            "b_re_t": np.ascontiguousarray(bre[:, :, 256 * r:256 * r + 256]),
            "b_im_t": np.ascontiguousarray(bim[:, :, 256 * r:256 * r + 256]),
            "c_re_t": np.ascontiguousarray(cre[:, :, 256 * r:256 * r + 256]),
            "c_im_t": np.ascontiguousarray(cim[:, :, 256 * r:256 * r + 256]),
            "dsk_t": np.ascontiguousarray(dsk[:, :, 16 * r:16 * r + 16]),
            "glu_w": np.ascontiguousarray(glu_w[:, :, 256 * r:256 * r + 256]),
            "glub_t": np.ascontiguousarray(glub[:, :, 2 * r:2 * r + 2]),
            "gna": np.ascontiguousarray(gna[:, 4 * r:4 * r + 4]),
            "gdil": np.ascontiguousarray(gdil[4 * r:4 * r + 4]),
        }
        maps.append(m)
    return maps


_CACHE = {}


def kernel(**inputs):
    if "nc" not in _CACHE:
        _CACHE["nc"] = build_program()
    nc = _CACHE["nc"]
    maps = make_in_maps(inputs, 8)
    res = run_bass_kernel_spmd(nc, maps, core_ids=list(range(8)))
    out = np.stack([np.asarray(res.results[2 * b]["out"], np.float32) for b in range(4)], axis=0)
    return out
```

```python
import numpy as np
from contextlib import ExitStack
import concourse.bass as bass
import concourse.mybir as mybir
from concourse.bass_utils import run_bass_kernel_spmd

F32 = mybir.dt.float32
BF16 = mybir.dt.bfloat16
ALU = mybir.AluOpType
AF = mybir.ActivationFunctionType

S_LEN = 4096
D = 1024
NEG = -30000.0
MAGIC = 12582912.0
TWO_PI = float(2 * np.pi)
NE = 83
ATTACH_WAITS = True


class Buf:
    __slots__ = ("t", "lw", "rd", "name", "psum")

    def __init__(self, t, name="", psum=False):
        self.t = t
        self.lw = {}
        self.rd = {}
        self.name = name
        self.psum = psum

    def __getitem__(self, k):
        return self.t[k]


class Sched:
    NDMA = 56
    NHW = 40

    def __init__(self, nc, es):
        self.nc = nc
        self.es = es
        self.names = ["sp", "pe", "act", "dve", "pool"]
        self.sem = {k: es.enter_context(nc.semaphore("s_" + k)) for k in self.names}
        self.cnt = {k: 0 for k in self.names}
        self.dsem = [es.enter_context(nc.semaphore("d%d" % i)) for i in range(self.NDMA)]
        self.dcnt = [0] * self.NDMA
        self.dnext = {"hw": 0, "sw": self.NHW}
        self.waited = {}
        self.prog = {k: [] for k in self.names}
        self.scope = es
        self.nins = 0
        self.pend = {k: False for k in self.names}
        self.bsem = es.enter_context(nc.semaphore("s_bar"))
        self.bcnt = 0

    def sb(self, name, shape, dt):
        self.uid = getattr(self, "uid", 0) + 1
        name = "sb%d_%s" % (self.uid, name)
        return Buf(self.scope.enter_context(self.nc.sbuf_tensor(name, shape, dt)), name)

    def ps(self, name, shape, dt=F32):
        return Buf(self.es.enter_context(self.nc.psum_tensor(name, shape, dt)), name, True)

    def _wait(self, e, tok):
        if tok is None:
            return
        kind, idx, val = tok
        if kind == "c" and e == "pe" and idx == "pe":
            return
        if kind == "c":
            assert val <= self.cnt[idx], "wait on a not-yet-recorded signalling op (%s waits %s %d > %d)" % (e, idx, val, self.cnt[idx])
        key = (e, kind, idx)
        if self.waited.get(key, 0) >= val:
            return
        self.waited[key] = val
        sem = self.sem[idx] if kind == "c" else self.dsem[idx]
        self.prog[e].append(("w", sem, val))

    def op(self, e, fn, reads=(), writes=(), dma=False, sig=True, par=False, cc=False):
        reads = list(reads)
        writes = list(writes)
        for b in list(reads):
            if b.psum:
                reads.remove(b)
                if b not in writes:
                    writes.append(b)
        need = {}
        for b in reads:
            for k2, val in b.lw.items():
                if need.get(k2, 0) < val:
                    need[k2] = val
        for b in writes:
            for k2, val in b.lw.items():
                if par and dma and k2[0] == "d":
                    continue
                if need.get(k2, 0) < val:
                    need[k2] = val
            for k2, val in b.rd.items():
                if need.get(k2, 0) < val:
                    need[k2] = val
        for (kind, idx), val in need.items():
            self._wait(e, (kind, idx, val))
        if dma:
            kind_ = "sw" if e == "pool" else "hw"
            i = self.dnext[kind_]
            if kind_ == "hw":
                self.dnext[kind_] = (i + 1) % self.NHW
            else:
                self.dnext[kind_] = self.NHW + (i + 1 - self.NHW) % (self.NDMA - self.NHW)
            if self.dcnt[i] > 0:
                self._wait(e, ("d", i, self.dcnt[i]))
            inc = 1 if cc else 16
            self.dcnt[i] += inc
            self.prog[e].append(("o", fn, self.dsem[i], inc))
            tok = ("d", i, self.dcnt[i])
        elif sig:
            self.cnt[e] += 1
            self.prog[e].append(("o", fn, self.sem[e], 1))
            tok = ("c", e, self.cnt[e])
            self.pend[e] = False
        else:
            self.prog[e].append(("n", fn))
            tok = ("c", e, self.cnt[e] + 1)
            self.pend[e] = True
        self.nins += 1
        for b in reads:
            k = (tok[0], tok[1])
            if b.rd.get(k, 0) < tok[2]:
                b.rd[k] = tok[2]
        for b in writes:
            if par and dma:
                b.lw = {k: v for k, v in b.lw.items() if k[0] == "d"}
                b.lw[(tok[0], tok[1])] = tok[2]
            else:
                b.lw = {(tok[0], tok[1]): tok[2]}
            b.rd = {}
        return tok

    def barrier(self):
        assert not any(self.pend.values()), self.pend
        for i in range(self.NDMA):
            if self.dcnt[i] > 0:
                self._wait("sp", ("d", i, self.dcnt[i]))
        self.bcnt += 1
        self.prog["sp"].append(("s", self.bsem, 1))
        for e in self.names:
            for e2 in self.names:
                if e2 != e and self.cnt[e2] > 0:
                    self._wait(e, ("c", e2, self.cnt[e2]))
            if e != "sp":
                self.prog[e].append(("w", self.bsem, self.bcnt))
                for i in range(self.NDMA):
                    if self.dcnt[i] > 0:
                        self.waited[(e, "d", i)] = max(self.waited.get((e, "d", i), 0), self.dcnt[i])

    def flush(self):
        self.barrier()
        prog = self.prog
        self.prog = {k: [] for k in self.names}

        def mk(name):
            def body(eng):
                items = prog[name]
                n = len(items)
                i = 0
                while i < n:
                    it = items[i]
                    if it[0] == "w":
                        nxt = items[i + 1] if i + 1 < n else None
                        if nxt is not None and nxt[0] in ("o", "n") and ATTACH_WAITS:
                            ins = nxt[1](eng)
                            ins._wait_ge(it[1], it[2])
                            if nxt[0] == "o":
                                ins.then_inc(nxt[2], nxt[3])
                            i += 2
                            continue
                        eng.wait_ge(it[1], it[2])
                    elif it[0] == "n":
                        it[1](eng)
                    elif it[0] == "s":
                        eng.sem_inc(it[1], it[2])
                    else:
                        it[1](eng).then_inc(it[2], it[3])
                    i += 1
            return body

        with self.nc.Block() as block:
            block.sync(mk("sp"))
            block.tensor(mk("pe"))
            block.scalar(mk("act"))
            block.vector(mk("dve"))
            block.gpsimd(mk("pool"))


def _t5_bucket(rel):
    nb = 16
    max_exact = 8
    n = np.abs(rel)
    large = max_exact + (np.log(np.maximum(n, 1) / max_exact) / np.log(1024 / max_exact) * (nb - max_exact)).astype(np.int32)
    large = np.minimum(large, nb - 1)
    return (np.where(rel > 0, nb, 0) + np.where(n < max_exact, n, large)).astype(np.int32)


def _gather_dil(t5_bias):
    out = np.full((8, 128, 3, 256), NEG, np.float32)
    kk = np.arange(128)[:, None]
    cc = np.arange(256)[None, :]
    rel = kk + 64 - cc
    valid = np.abs(rel) <= 64
    for pi, d in enumerate((1, 4, 16)):
        bkt = _t5_bucket(d * rel)
        for h in range(8):
            vals = t5_bias[bkt, h]
            out[h, :, pi, :] = np.where(valid, vals, np.float32(NEG))
    return np.ascontiguousarray(out.reshape(8, 128, 768))


def _rs(r):
    return min(max(r - 4, 0), 56)


NA_SBREP = [0, 3, 7]


def _na_slot(tab, slot):
    sb = NA_SBREP[tab]
    m = 4 * sb - 2 + slot
    if m < 0 or m > 31:
        return None
    rows = [rl for rl in range(8) if _rs(8 * sb + rl) <= 2 * m + 1 and _rs(8 * sb + rl) + 7 >= 2 * m]
    if not rows:
        return None
    assert rows == list(range(rows[0], rows[-1] + 1))
    return rows[0], rows[-1]


def _gather_na(rpb):
    out = np.full((8, 3, 128, 8, 8, 64), NEG, np.float32)
    kk = np.arange(128)
    j = np.arange(64)
    cs = np.clip(j - 8, 0, 48)
    kc = kk % 64
    colok = (kc[:, None] >= cs[None, :]) & (kc[:, None] <= cs[None, :] + 15)
    dc = np.clip(kc[:, None] - j[None, :] + 15, 0, 30)
    for tab in range(3):
        sb = NA_SBREP[tab]
        for slot in range(8):
            rr = _na_slot(tab, slot)
            if rr is None:
                continue
            m = 4 * sb - 2 + slot
            kr = 2 * m + kk // 64
            for rl in range(rr[0], rr[1] + 1):
                r = 8 * sb + rl
                rs = _rs(r)
                valid = ((kr >= rs) & (kr <= rs + 7))[:, None] & colok
                dr = np.clip(kr - r + 7, 0, 14)
                for h in range(8):
                    vals = rpb[h][dr[:, None], dc]
                    out[h, tab, :, slot, rl, :] = np.where(valid, vals, np.float32(NEG))
    return np.ascontiguousarray(out.reshape(8, 3, 128, 4096))


def _etab():
    e = np.zeros((128, NE), np.float32)
    for dirn in range(2):
        rows = slice(dirn * 64, dirn * 64 + 64)
        sg = -1.0 if dirn == 0 else 1.0
        for s in range(8):
            e[rows, s] = sg * s
        for m in range(5):
            for t in range(8):
                e[rows, 8 + m * 8 + t] = 8 * m + (t if dirn == 0 else -t)
        for jj in range(4):
            for s in range(8):
                e[rows, 48 + jj * 8 + s] = (32 - 8 * jj - s) if dirn == 0 else (8 * jj + s)
        e[rows, 80] = 1.0
        e[rows, 81] = 32.0
        e[rows, 82] = 64.0
    return e


def _masks():
    sp = np.arange(128) // 16
    mge = (sp[None, :] >= sp[:, None]).astype(np.float32)
    mle = (sp[None, :] <= sp[:, None]).astype(np.float32)
    return mge, mle


def build_program(nlayers=2, dbg=None, do_na=True, do_dil=True, do_ssm=True, npairs=4):
    nc = bass.Bass("TRN2", target_bir_lowering=False)

    def din(name, shape, dt=F32):
        return nc.dram_tensor(name, list(shape), dt, kind="ExternalInput").ap()

    x_d = din("x", [S_LEN, D])
    normg_d = din("norm_g", [2, D])
    finalg_d = din("final_g", [D])
    win_d = din("w_in", [2, D, 2560])
    rgroups = [[2 * i, 2 * i + 1] for i in range(npairs)]
    XA0, ZA0, NAQ, NAK, NAV, NAZ, DQ, DK, DV, DZ = 0, 256, 512, 768, 1024, 1280, 1536, 1792, 2048, 2304
    wout_d = din("w_out", [2, 1536, D])
    ident_d = din("ident", [128, 128])
    mge_d = din("mge", [128, 128])
    mle_d = din("mle", [128, 128])
    etab_d = din("etab", [128, NE])
    lre_d = din("lam_re_t", [2, 128, 16])
    lim_d = din("lam_im_t", [2, 128, 16])
    ldt_d = din("logdt_t", [2, 128, 16])
    bre_d = din("b_re_t", [2, 128, 256])
    bim_d = din("b_im_t", [2, 128, 256])
    cre_d = din("c_re_t", [2, 128, 256])
    cim_d = din("c_im_t", [2, 128, 256])
    dsk_d = din("dsk_t", [2, 128, 16])
    gluw_d = din("glu_w", [2, 512, 256])
    glub_d = din("glub_t", [2, 128, 2])
    gna_d = din("gna", [2, 4, 3, 128, 4096])
    gdil_d = din("gdil", [4, 128, 768])
    out_d = nc.dram_tensor("out", [S_LEN, D], F32, kind="ExternalOutput").ap()
    sk = "ExternalOutput" if dbg else "Internal"
    x1_d = nc.dram_tensor("x1s", [S_LEN, D], F32, kind=sk).ap()
    ysc_t = [nc.dram_tensor("ysc%d" % i, [128, S_LEN], BF16, kind="Internal").ap() for i in range(6)]
    ygat_t = [nc.dram_tensor("ygat%d" % i, [256, S_LEN], BF16, kind="Internal", addr_space="Local").ap() for i in range(6)]
    usc_d = nc.dram_tensor("usc", [2, 8, 16, 8, 512], BF16, kind="Internal").ap()
    ys2_t = [nc.dram_tensor("ys2_%d" % i, [128, S_LEN], BF16, kind="Internal").ap() for i in range(2)]
    ys2g_t = [nc.dram_tensor("ys2g%d" % i, [256, S_LEN], BF16, kind="Internal", addr_space="Local").ap() for i in range(2)]
    zsc_d = nc.dram_tensor("zsc", [2, 128, S_LEN], BF16, kind="Internal").ap()
    yscB = [Buf(None, "yscB%d" % i) for i in range(6)]
    ygatB = [Buf(None, "ygatB%d" % i) for i in range(6)]
    ys2B = [Buf(None, "ys2B%d" % i) for i in range(2)]
    ys2gB = [Buf(None, "ys2gB%d" % i) for i in range(2)]

    with ExitStack() as es:
        S = Sched(nc, es)
        PS = [S.ps("ps%d" % i, [128, 512], F32) for i in range(7)]
        PSB = S.ps("psb", [128, 1024], BF16)
        identf = S.sb("identf", [128, 128], F32)
        identb = S.sb("identb", [128, 128], BF16)
        onesb = S.sb("onesb", [128, 128], BF16)
        zerosb = S.sb("zerosb", [128, 128], BF16)
        mge = S.sb("mge", [128, 128], F32)
        mle = S.sb("mle", [128, 128], F32)
        wst = [S.sb("wst%d" % i, [128, 8 * 128], F32) for i in range(2)]
        wbb = [S.sb("wbb%d" % i, [128, 8 * 128], BF16) for i in range(2)]
        state = {"wi": 0, "ps": 0, "ev": 0}

        def dma(out, in_, reads=(), writes=(), q="sp", par=False):
            return S.op(q, lambda e: e.dma_start(out=out, in_=in_), reads=reads, writes=writes, dma=True, par=par)

        def allgather(src_ap, dst_ap, srcB, dstB):
            return S.op("pool", lambda e: e.collective_compute("AllGather", ALU.bypass, replica_groups=rgroups, ins=[src_ap], outs=[dst_ap]),
                        reads=[srcB], writes=[dstB], dma=True, cc=True)

        dma(identf[:], ident_d, writes=[identf])
        dma(mge[:], mge_d, writes=[mge])
        dma(mle[:], mle_d, writes=[mle])
        S.op("dve", lambda e: e.tensor_copy(out=identb[:], in_=identf[:]), reads=[identf], writes=[identb])
        S.op("dve", lambda e: e.memset(onesb[:], 1.0), writes=[onesb])
        S.op("dve", lambda e: e.memset(zerosb[:], 0.0), writes=[zerosb])

        def emit_pipeline(tasks, depth=2):
            queue = []
            for t in tasks:
                if t["init"] is not None:
                    t["init"]()
                t["S"]()
                t["E"]()
                queue.append(t)
                if len(queue) > depth:
                    p = queue.pop(0)
                    p["PV"]()
                    if p["fin"] is not None:
                        p["fin"]()
            for p in queue:
                p["PV"]()
                if p["fin"] is not None:
                    p["fin"]()

        wseq = []
        for l_ in range(nlayers):
            if do_na:
                for hp_ in range(2):
                    wseq += [(l_, NAQ + hp_ * 128), (l_, NAK + hp_ * 128), (l_, NAZ + hp_ * 128), (l_, NAV + hp_ * 128)]
            if do_dil:
                for hp_ in range(2):
                    wseq += [(l_, DQ + hp_ * 128), (l_, DK + hp_ * 128), (l_, DZ + hp_ * 128), (l_, DV + hp_ * 128)]
            if do_ssm:
                for ct_ in range(2):
                    wseq += [(l_, XA0 + ct_ * 128), (l_, ZA0 + ct_ * 128)]
        state["wptr"] = 0
        state["wissued"] = 0

        state["pcast"] = []

        def issue_w(k):
            l_, col0 = wseq[k]
            i = k % 2
            src = win_d[l_, :, col0:col0 + 128].rearrange("(j p) c -> p j c", p=128)
            dma(wst[i][:].rearrange("p (j c) -> p j c", j=8), src, writes=[wst[i]])
            state["pcast"].append(k)

        def do_cast():
            while state["pcast"]:
                k = state["pcast"].pop(0)
                i = k % 2
                if k % 2 == 0:
                    S.op("dve", lambda e, i=i: e.tensor_copy(out=wbb[i][:], in_=wst[i][:]), reads=[wst[i]], writes=[wbb[i]])
                else:
                    S.op("act", lambda e, i=i: e.activation(out=wbb[i][:], in_=wst[i][:], func=AF.Copy), reads=[wst[i]], writes=[wbb[i]])

        def load_w(l, col0):
            k = state["wptr"]
            assert wseq[k] == (l, col0), (wseq[k], l, col0)
            while state["wissued"] <= k:
                issue_w(state["wissued"])
                state["wissued"] += 1
            do_cast()
            if state["wissued"] <= min(k + 1, len(wseq) - 1):
                issue_w(state["wissued"])
                state["wissued"] += 1
            state["wptr"] = k + 1
            return wbb[k % 2]

        def evac_copy(out_ap, in_ap, rd, wr, scale=None):
            state["ev"] ^= 1
            if scale is not None or state["ev"]:
                sc = 1.0 if scale is None else scale
                S.op("act", lambda e: e.activation(out=out_ap, in_=in_ap, func=AF.Copy, scale=sc), reads=rd, writes=wr)
            else:
                S.op("dve", lambda e: e.tensor_copy(out=out_ap, in_=in_ap), reads=rd, writes=wr)

        for l in range(nlayers):
            xsrc = x_d if l == 0 else x1_d
            last = (l == nlayers - 1)
            with ExitStack() as scH:
                S.scope = scH
                hT = S.sb("hT", [128, 8 * S_LEN], BF16)
                hT3 = hT[:].rearrange("p (j t) -> p j t", j=8)

                with ExitStack() as ph:
                    S.scope = ph
                    gbc = S.sb("gbc", [128, D], F32)
                    xts = [S.sb("xt%d" % i, [128, D], F32) for i in range(4)]
                    junk = S.sb("junk", [128, D], BF16)
                    hbs = [S.sb("hb%d" % i, [128, D], BF16) for i in range(2)]
                    sts = [S.sb("st%d" % i, [128, 4 * 8], F32) for i in range(4)]
                    dma(gbc[:], normg_d[l].partition_broadcast(128), writes=[gbc])
                    for st_ in sts:
                        S.op("dve", lambda e, st_=st_: e.memset(st_[:], 0.0), writes=[st_])
                    for tt in range(32):
                        xt = xts[tt % 4]
                        hb = hbs[tt % 2]
                        st = sts[tt % 4]
                        c0 = (tt // 4) * 4
                        dma(xt[:], xsrc[tt * 128:(tt + 1) * 128, :], writes=[xt], q="sp")
                        S.op("act", lambda e, xt=xt, c0=c0, st=st: e.activation(out=junk[:], in_=xt[:], func=AF.Square, accum_out=st[:, c0:c0 + 1]),
                             reads=[xt], writes=[junk, st])
                        S.op("dve", lambda e, c0=c0, st=st: e.tensor_scalar(out=st[:, c0 + 1:c0 + 2], in0=st[:, c0:c0 + 1], scalar1=1.0 / D, scalar2=1e-6,
                                                                    op0=ALU.mult, op1=ALU.add), reads=[st], writes=[st])
                        S.op("act", lambda e, c0=c0, st=st: e.activation(out=st[:, c0 + 2:c0 + 3], in_=st[:, c0 + 1:c0 + 2], func=AF.Sqrt), reads=[st], writes=[st])
                        S.op("dve", lambda e, c0=c0, st=st: e.reciprocal(out=st[:, c0 + 3:c0 + 4], in_=st[:, c0 + 2:c0 + 3]), reads=[st], writes=[st])
                        S.op("dve", lambda e, xt=xt, hb=hb, c0=c0, st=st: e.scalar_tensor_tensor(out=hb[:], in0=xt[:], scalar=st[:, c0 + 3:c0 + 4], in1=gbc[:],
                                                                                          op0=ALU.mult, op1=ALU.mult), reads=[xt, st, gbc], writes=[hb])
                        for j in range(8):
                            S.op("pe", lambda e, hb=hb, j=j: e.transpose(out=PSB[:, j * 128:(j + 1) * 128], in_=hb[:, j * 128:(j + 1) * 128], identity=identb[:]),
                                 reads=[hb, identb], writes=[PSB], sig=(j == 7))
                        evac_copy(hT3[:, :, tt * 128:(tt + 1) * 128], PSB[:].rearrange("p (j t) -> p j t", j=8), [PSB], [hT])
                    S.flush()
                S.scope = scH

                def proj_fm(col0, evac):
                    wb = load_w(l, col0)
                    wb3 = wb[:].rearrange("p (j c) -> p j c", j=8)
                    for n in range(8):
                        ps = PS[5 + (n % 2)]
                        for j in range(8):
                            S.op("pe", lambda e, ps=ps, j=j, n=n: e.matmul(ps[:, 0:512], lhsT=wb3[:, j, :], rhs=hT3[:, j, n * 512:(n + 1) * 512],
                                                                           start=(j == 0), stop=(j == 7)), reads=[wb, hT], writes=[ps], sig=(j == 7))
                        evac(n, ps)
                        if n == 4:
                            do_cast()

                def tiles_from_fm(vT, tiles, vbuf, v3):
                    for g8 in range(0, len(tiles), 8):
                        for q in range(8):
                            tsl = tiles[g8 + q]
                            S.op("pe", lambda e, q=q, tsl=tsl: e.transpose(out=PSB[:, q * 128:(q + 1) * 128], in_=vT[:, tsl], identity=identb[:]),
                                 reads=[vT, identb], writes=[PSB], sig=(q == 7))
                        evac_copy(v3[:, g8:g8 + 8, :], PSB[:].rearrange("p (q c) -> p q c", q=8), [PSB], [vbuf])

                def proj_tm(col0, tiles, vbuf, v3, wb=None):
                    if wb is None:
                        wb = load_w(l, col0)
                    wb3 = wb[:].rearrange("p (j c) -> p j c", j=8)
                    for g4 in range(0, len(tiles), 4):
                        ps = PS[5 + ((g4 // 4) % 2)]
                        for q in range(4):
                            tsl = tiles[g4 + q]
                            for j in range(8):
                                S.op("pe", lambda e, ps=ps, j=j, q=q, tsl=tsl: e.matmul(ps[:, q * 128:(q + 1) * 128], lhsT=hT3[:, j, tsl], rhs=wb3[:, j, :],
                                                                                       start=(j == 0), stop=(j == 7)), reads=[wb, hT], writes=[ps], sig=(j == 7 and q == 3))
                        evac_copy(v3[:, g4:g4 + 4, :], ps[:, 0:512].rearrange("p (q c) -> p q c", q=4), [ps], [vbuf])
                        if g4 == 16:
                            do_cast()

                def attn_common(ph, colq, colk, colz, split_q=False):
                    kT = S.sb("kT", [128, S_LEN], BF16)
                    zT = S.sb("zT", [128, S_LEN], BF16)
                    yT = S.sb("yT", [128, S_LEN], BF16)
                    if split_q:
                        qz = [S.sb("qz%d" % i, [128, S_LEN], BF16) for i in range(2)]
                        S.op("pool", lambda e: e.memset(qz[0][64:128, :], 0.0), writes=[qz[0]])
                        S.op("pool", lambda e: e.memset(qz[1][0:64, :], 0.0), writes=[qz[1]])

                        def evq(n, ps):
                            for hh_ in range(2):
                                rs_ = slice(64 * hh_, 64 * hh_ + 64)
                                S.op("act", lambda e, hh_=hh_, rs_=rs_: e.activation(out=qz[hh_][rs_, n * 512:(n + 1) * 512], in_=ps[rs_, 0:512], func=AF.Copy, scale=0.125),
                                     reads=[ps], writes=[qz[hh_]])
                        proj_fm(colq, evq)
                        qT = qz
                    else:
                        qT = S.sb("qT", [128, S_LEN], BF16)
                        proj_fm(colq, lambda n, ps: S.op("act", lambda e: e.activation(out=qT[:, n * 512:(n + 1) * 512], in_=ps[:, 0:512], func=AF.Copy, scale=0.125),
                                                         reads=[ps], writes=[qT]))
                    proj_fm(colk, lambda n, ps: evac_copy(kT[:, n * 512:(n + 1) * 512], ps[:, 0:512], [ps], [kT]))
                    proj_fm(colz, lambda n, ps: S.op("act", lambda e: e.activation(out=zT[:, n * 512:(n + 1) * 512], in_=ps[:, 0:512], func=AF.Silu),
                                                     reads=[ps], writes=[zT]))
                    return qT, kT, zT, yT

                if do_na:
                    for hp in range(2):
                        with ExitStack() as ph:
                            S.scope = ph
                            gsts = [S.sb("gst%d" % i, [128, 4096], F32) for i in range(2)]
                            nbt = [[S.sb("nb%d_%d" % (hh_, t_), [128, 4096], BF16) for t_ in range(3)] for hh_ in range(2)]
                            gorder = [(0, 0), (0, 1), (0, 2), (1, 0), (1, 1), (1, 2)]
                            for i_ in range(2):
                                dma(gsts[i_][:], gna_d[l, hp * 2 + gorder[i_][0], gorder[i_][1]], writes=[gsts[i_]], q="act")
                            qT, kT, zT, yT = attn_common(ph, NAQ + hp * 128, NAK + hp * 128, NAZ + hp * 128)
                            vN = S.sb("vN", [128, 32 * 128], BF16)
                            vN3 = vN[:].rearrange("p (t c) -> p t c", t=32)
                            vT = yT
                            proj_fm(NAV + hp * 128, lambda n, ps: evac_copy(vT[:, n * 512:(n + 1) * 512], ps[:, 0:512], [ps], [vT]))
                            tiles_from_fm(vT, [slice(t * 128, (t + 1) * 128) for t in range(32)], vN, vN3)
                            PT = [S.sb("pt%d" % i, [128, 512], BF16) for i in range(3)]
                            SCB = [PS[0], PS[1], PS[6]]
                            dsb = S.sb("dsb", [128, 512], F32)
                            tnum = S.sb("tnum", [128, 512], F32)
                            for i_, (hh, t_) in enumerate(gorder):
                                gst = gsts[i_ % 2]
                                if i_ >= 2:
                                    dma(gst[:], gna_d[l, hp * 2 + hh, t_], writes=[gst], q="act")
                                S.op("dve", lambda e, hh=hh, t_=t_, gst=gst: e.tensor_copy(out=nbt[hh][t_][:], in_=gst[:]), reads=[gst], writes=[nbt[hh][t_]])
                            tcount = 0
                            tasks = []
                            for hh in range(2):
                                sl = slice(64 * hh, 64 * hh + 64)
                                for sb_ in range(8):
                                    tab = 0 if sb_ == 0 else (2 if sb_ == 7 else 1)
                                    nb = nbt[hh][tab]
                                    psn = PS[2 + sb_ % 2]
                                    psd = PS[4 + sb_ % 2]

                                    def f_init(psn=psn, psd=psd, nb=nb):
                                        for dst in (psn, psd):
                                            S.op("pe", lambda e, dst=dst, nb=nb: e.matmul(dst[:, 0:512], lhsT=zerosb[:], rhs=nb[:, 0:512], start=True, stop=False),
                                                 reads=[zerosb, nb], writes=[dst], sig=False)

                                    def f_fin(psn=psn, psd=psd, sl=sl, tok=slice(sb_ * 512, sb_ * 512 + 512)):
                                        S.op("act", lambda e: e.activation(out=dsb[sl, :], in_=psd[sl, :], func=AF.Copy), reads=[psd], writes=[dsb])
                                        S.op("dve", lambda e: e.reciprocal(out=dsb[sl, :], in_=dsb[sl, :]), reads=[dsb], writes=[dsb])
                                        S.op("dve", lambda e: e.tensor_tensor(out=tnum[sl, :], in0=psn[sl, :], in1=dsb[sl, :], op=ALU.mult), reads=[psn, dsb], writes=[tnum])
                                        S.op("pool", lambda e: e.tensor_tensor(out=yT[sl, tok], in0=tnum[sl, :], in1=zT[sl, tok], op=ALU.mult), reads=[tnum, zT], writes=[yT])

                                    slots = [(slot, _na_slot(tab, slot)) for slot in range(8)]
                                    slots = [(slot, rr) for slot, rr in slots if rr is not None]
                                    for si, (slot, (ra, rb)) in enumerate(slots):
                                        m = 4 * sb_ - 2 + slot
                                        c0 = ra * 64
                                        N = (rb - ra + 1) * 64
                                        q0 = 512 * sb_ + c0
                                        pss = SCB[tcount % 3]
                                        pt = PT[tcount % 3]
                                        tcount += 1
                                        lastt = (si == len(slots) - 1)

                                        def f_S(pss=pss, nb=nb, slot=slot, c0=c0, N=N, m=m, q0=q0, sl=sl):
                                            S.op("pe", lambda e: e.matmul(pss[:, 0:N], lhsT=identb[:], rhs=nb[:, slot * 512 + c0:slot * 512 + c0 + N], start=True, stop=False),
                                                 reads=[identb, nb], writes=[pss], sig=False)
                                            S.op("pe", lambda e: e.matmul(pss[:, 0:N], lhsT=kT[sl, m * 128:(m + 1) * 128], rhs=qT[sl, q0:q0 + N], start=False, stop=True),
                                                 reads=[kT, qT], writes=[pss])

                                        def f_E(pss=pss, pt=pt, N=N):
                                            S.op("act", lambda e: e.activation(out=pt[:, 0:N], in_=pss[:, 0:N], func=AF.Exp), reads=[pss], writes=[pt])

                                        def f_PV(psn=psn, psd=psd, pt=pt, m=m, c0=c0, N=N, lastt=lastt, first=(si == 0)):
                                            S.op("pe", lambda e: e.matmul(psn[:, c0:c0 + N], lhsT=vN3[:, m, :], rhs=pt[:, 0:N], start=first, stop=lastt, skip_group_check=True),
                                                 reads=[vN, pt], writes=[psn], sig=False)
                                            S.op("pe", lambda e: e.matmul(psd[:, c0:c0 + N], lhsT=onesb[:], rhs=pt[:, 0:N], start=first, stop=lastt, skip_group_check=True),
                                                 reads=[onesb, pt], writes=[psd], sig=True)

                                        tasks.append({"init": None, "S": f_S, "E": f_E, "PV": f_PV, "fin": f_fin if lastt else None})
                            emit_pipeline(tasks)
                            dma(ysc_t[2 + hp], yT[:], reads=[yT], writes=[yscB[2 + hp]])
                            allgather(ysc_t[2 + hp], ygat_t[2 + hp], yscB[2 + hp], ygatB[2 + hp])
                            S.flush()
                        S.scope = scH

                if do_dil:
                    for hp in range(2):
                        with ExitStack() as ph:
                            S.scope = ph
                            qT, kT, zT, yT = attn_common(ph, DQ + hp * 128, DK + hp * 128, DZ + hp * 128, split_q=True)
                            vD, vD3 = [], []
                            vT = yT
                            proj_fm(DV + hp * 128, lambda n, ps: evac_copy(vT[:, n * 512:(n + 1) * 512], ps[:, 0:512], [ps], [vT]))
                            for pi, d in enumerate((1, 4, 16)):
                                vb = S.sb("vD%d" % pi, [128, 32 * 128], BF16)
                                v3 = vb[:].rearrange("p (t c) -> p t c", t=32)
                                nsub = 32 // d
                                tiles = []
                                for r in range(d):
                                    for m in range(nsub):
                                        s0 = r + d * 128 * m
                                        tiles.append(slice(s0, s0 + d * 127 + 1, d))
                                tiles_from_fm(vT, tiles, vb, v3)
                                vD.append(vb)
                                vD3.append(v3)
                            gst = S.sb("gstd", [128, 768], F32)
                            dbs = [S.sb("db%d" % i, [128, 768], BF16) for i in range(2)]
                            PT = [S.sb("ptd%d" % i, [128, 256], BF16) for i in range(3)]
                            SCB = [PS[0], PS[1], PS[6]]
                            accn = S.sb("accn", [128, S_LEN], F32)
                            accd = S.sb("accd", [128, S_LEN], F32)
                            for hh in range(2):
                                dma(gst[:], gdil_d[hp * 2 + hh], writes=[gst])
                                S.op("pool", lambda e, hh=hh: e.tensor_copy(out=dbs[hh][:], in_=gst[:]), reads=[gst], writes=[dbs[hh]])
                            combo = 0
                            tasks = []
                            for hh in range(2):
                                db = dbs[hh]
                                sl = slice(64 * hh, 64 * hh + 64)
                                nseg_total = 8 + 8 + 16
                                segc = 0
                                for pi, d in ((2, 16), (1, 4), (0, 1)):
                                    n = S_LEN // d
                                    L = min(512, n)
                                    nsub = n // 128
                                    segi = 0
                                    for r in range(d):
                                        for s0 in range(0, n, L):
                                            psn = PS[2 + segi % 2]
                                            psd = PS[4 + segi % 2]
                                            segi += 1
                                            segc += 1
                                            last_seg_of_head = (segc == nseg_total)

                                            def f_init(psn=psn, psd=psd, db=db, L=L):
                                                for dst in (psn, psd):
                                                    S.op("pe", lambda e, dst=dst: e.matmul(dst[:, 0:L], lhsT=zerosb[:], rhs=db[:, 0:L], start=True, stop=False),
                                                         reads=[zerosb, db], writes=[dst], sig=False)

                                            def f_fin(psn=psn, psd=psd, sl=sl, pi=pi, L=L, d=d, t0_=r + d * s0, last_seg_of_head=last_seg_of_head):
                                                for acc, psrc in ((accn, psn), (accd, psd)):
                                                    view = acc[sl, t0_:t0_ + d * (L - 1) + 1:d]
                                                    pview = psrc[sl, 0:L]
                                                    if pi == 2:
                                                        evac_copy(view, pview, [psrc], [acc])
                                                    else:
                                                        S.op("dve", lambda e, view=view, pview=pview: e.tensor_tensor(out=view, in0=view, in1=pview, op=ALU.add),
                                                             reads=[psrc, acc], writes=[acc])
                                                if pi == 0:
                                                    tsl = slice(t0_, t0_ + d * (L - 1) + 1, d)
                                                    S.op("dve", lambda e: e.reciprocal(out=accd[sl, tsl], in_=accd[sl, tsl]), reads=[accd], writes=[accd])
                                                    S.op("dve", lambda e: e.tensor_tensor(out=accn[sl, tsl], in0=accn[sl, tsl], in1=accd[sl, tsl], op=ALU.mult), reads=[accn, accd], writes=[accn])
                                                    S.op("pool", lambda e: e.tensor_tensor(out=yT[sl, tsl], in0=accn[sl, tsl], in1=zT[sl, tsl], op=ALU.mult), reads=[accn, zT], writes=[yT])

                                            ms = [m for m in range(nsub) if max(s0, 128 * m - 64) < min(s0 + L, 128 * m + 192)]
                                            for mi, m in enumerate(ms):
                                                qa = max(s0, 128 * m - 64)
                                                qb = min(s0 + L, 128 * m + 192)
                                                N = qb - qa
                                                cb = qa - (128 * m - 64)
                                                ks = slice(r + d * 128 * m, r + d * (128 * m + 127) + 1, d)
                                                qs = slice(r + d * qa, r + d * (qb - 1) + 1, d)
                                                pss = SCB[combo % 3]
                                                pt = PT[combo % 3]
                                                combo += 1
                                                lastt = (mi == len(ms) - 1)
                                                o0 = qa - s0
                                                tix = r * nsub + m

                                                def f_S(pss=pss, db=db, pi=pi, cb=cb, N=N, ks=ks, qs=qs, qh=qT[hh]):
                                                    S.op("pe", lambda e: e.matmul(pss[:, 0:N], lhsT=identb[:], rhs=db[:, pi * 256 + cb:pi * 256 + cb + N], start=True, stop=False),
                                                         reads=[identb, db], writes=[pss], sig=False)
                                                    S.op("pe", lambda e: e.matmul(pss[:, 0:N], lhsT=kT[:, ks], rhs=qh[:, qs], start=False, stop=True), reads=[kT, qh], writes=[pss])

                                                def f_E(pss=pss, pt=pt, N=N):
                                                    S.op("act", lambda e: e.activation(out=pt[:, 0:N], in_=pss[:, 0:N], func=AF.Exp), reads=[pss], writes=[pt])

                                                def f_PV(psn=psn, psd=psd, pt=pt, tix=tix, pi=pi, o0=o0, N=N, lastt=lastt, first=(mi == 0)):
                                                    S.op("pe", lambda e: e.matmul(psn[:, o0:o0 + N], lhsT=vD3[pi][:, tix, :], rhs=pt[:, 0:N], start=first, stop=lastt, skip_group_check=True),
                                                         reads=[vD[pi], pt], writes=[psn], sig=False)
                                                    S.op("pe", lambda e: e.matmul(psd[:, o0:o0 + N], lhsT=onesb[:], rhs=pt[:, 0:N], start=first, stop=lastt, skip_group_check=True),
                                                         reads=[onesb, pt], writes=[psd], sig=True)

                                                tasks.append({"init": None, "S": f_S, "E": f_E, "PV": f_PV, "fin": f_fin if lastt else None})
                            emit_pipeline(tasks)
                            dma(ysc_t[4 + hp], yT[:], reads=[yT], writes=[yscB[4 + hp]])
                            allgather(ysc_t[4 + hp], ygat_t[4 + hp], yscB[4 + hp], ygatB[4 + hp])
                            S.flush()
                        S.scope = scH

                if do_ssm:
                    with ExitStack() as ph:
                        S.scope = ph
                        ups = [S.sb("up%d" % i, [128, 8 * 512], BF16) for i in range(2)]
                        zts = [S.sb("zt%d" % i, [128, S_LEN], BF16) for i in range(2)]
                        for ct in range(2):
                            up = ups[ct % 2]
                            up3 = up[:].rearrange("p (s m) -> p s m", s=8)
                            proj_fm(XA0 + ct * 128, lambda n, ps, up=up, up3=up3: evac_copy(up3[:, :, n * 64:(n + 1) * 64],
                                                                                   ps[:, 0:512].rearrange("p (m s) -> p s m", s=8), [ps], [up]))
                            dma(usc_d[ct].rearrange("g c s m -> (g c) s m"), up3, reads=[up])
                            zt = zts[ct % 2]
                            proj_fm(ZA0 + ct * 128, lambda n, ps, zt=zt: S.op("act", lambda e: e.activation(out=zt[:, n * 512:(n + 1) * 512], in_=ps[:, 0:512], func=AF.Silu),
                                                                              reads=[ps], writes=[zt]))
                            dma(zsc_d[ct], zt[:], reads=[zt])
                        S.flush()
                    S.scope = scH
            S.scope = es

            if do_ssm:
                with ExitStack() as scS:
                    S.scope = scS
                    lre = S.sb("lre", [128, 16], F32)
                    lim = S.sb("lim", [128, 16], F32)
                    ldt = S.sb("ldt", [128, 16], F32)
                    ar = S.sb("ar", [128, 16], F32)
                    ai = S.sb("ai", [128, 16], F32)
                    bre = S.sb("bre", [128, 256], F32)
                    bim = S.sb("bim", [128, 256], F32)
                    cre = S.sb("cre", [128, 256], F32)
                    cim = S.sb("cim", [128, 256], F32)
                    dsk = S.sb("dsk", [128, 16], F32)
                    etab = S.sb("etab", [128, NE], F32)
                    for t, d_ in ((lre, lre_d), (lim, lim_d), (ldt, ldt_d), (bre, bre_d), (bim, bim_d), (cre, cre_d), (cim, cim_d), (dsk, dsk_d)):
                        dma(t[:], d_[l], writes=[t])
                    dma(etab[:], etab_d, writes=[etab])
                    S.op("act", lambda e: e.activation(out=ldt[:], in_=ldt[:], func=AF.Exp), reads=[ldt], writes=[ldt])
                    S.op("dve", lambda e: e.tensor_tensor(out=ar[:], in0=lre[:], in1=ldt[:], op=ALU.mult), reads=[lre, ldt], writes=[ar])
                    S.op("dve", lambda e: e.tensor_tensor(out=ai[:], in0=lim[:], in1=ldt[:], op=ALU.mult), reads=[lim, ldt], writes=[ai])
                    YPs = []
                    for half in range(1):
                        G0 = 0
                        gs = slice(G0, G0 + 16)
                        with ExitStack() as scHf:
                            S.scope = scHf
                            NW = 16 * NE
                            zr = S.sb("zr", [128, NW], F32)
                            zi = S.sb("zi", [128, NW], F32)
                            scT = ExitStack()
                            S.scope = scT
                            tA = S.sb("tA", [128, NW], F32)
                            tB = S.sb("tB", [128, NW], F32)
                            tC = S.sb("tC", [128, NW], F32)
                            tD = S.sb("tD", [128, NW], F32)
                            tI = S.sb("tI", [128, NW], mybir.dt.int32)

                            def v3(b):
                                return b[:].rearrange("p (g e) -> p g e", g=16)

                            ar_b = ar[:, gs].unsqueeze(2).to_broadcast([128, 16, NE])
                            ai_b = ai[:, gs].unsqueeze(2).to_broadcast([128, 16, NE])
                            et_b = etab[:].unsqueeze(1).to_broadcast([128, 16, NE])
                            S.op("dve", lambda e: e.tensor_tensor(out=v3(tA), in0=ar_b, in1=et_b, op=ALU.mult), reads=[ar, etab], writes=[tA])
                            S.op("act", lambda e: e.activation(out=tA[:], in_=tA[:], func=AF.Exp), reads=[tA], writes=[tA])
                            S.op("dve", lambda e: e.tensor_tensor(out=v3(tB), in0=ai_b, in1=et_b, op=ALU.mult), reads=[ai, etab], writes=[tB])
                            S.op("dve", lambda e: e.tensor_scalar(out=tB[:], in0=tB[:], scalar1=1.0 / TWO_PI, scalar2=None, op0=ALU.mult), reads=[tB], writes=[tB])

                            def sin_of(dst, shift):
                                S.op("dve", lambda e: e.tensor_scalar(out=tC[:], in0=tB[:], scalar1=shift, scalar2=None, op0=ALU.add), reads=[tB], writes=[tC])
                                S.op("dve", lambda e: e.tensor_copy(out=tI[:], in_=tC[:]), reads=[tC], writes=[tI])
                                S.op("dve", lambda e: e.tensor_copy(out=tD[:], in_=tI[:]), reads=[tI], writes=[tD])
                                S.op("dve", lambda e: e.tensor_tensor(out=tC[:], in0=tC[:], in1=tD[:], op=ALU.subtract), reads=[tC, tD], writes=[tC])
                                S.op("dve", lambda e: e.tensor_scalar(out=tD[:], in0=tC[:], scalar1=0.5, scalar2=None, op0=ALU.is_gt), reads=[tC], writes=[tD])
                                S.op("dve", lambda e: e.tensor_tensor(out=tC[:], in0=tC[:], in1=tD[:], op=ALU.subtract), reads=[tC, tD], writes=[tC])
                                S.op("dve", lambda e: e.tensor_scalar(out=tD[:], in0=tC[:], scalar1=-0.5, scalar2=None, op0=ALU.is_lt), reads=[tC], writes=[tD])
                                S.op("dve", lambda e: e.tensor_tensor(out=tC[:], in0=tC[:], in1=tD[:], op=ALU.add), reads=[tC, tD], writes=[tC])
                                S.op("act", lambda e: e.activation(out=dst[:], in_=tC[:], func=AF.Sin, scale=TWO_PI), reads=[tC], writes=[dst])

                            sin_of(zi, 0.0)
                            sin_of(zr, 0.25)
                            S.op("dve", lambda e: e.tensor_tensor(out=zr[:], in0=zr[:], in1=tA[:], op=ALU.mult), reads=[zr, tA], writes=[zr])
                            S.op("dve", lambda e: e.tensor_tensor(out=zi[:], in0=zi[:], in1=tA[:], op=ALU.mult), reads=[zi, tA], writes=[zi])
                            S.flush()
                            scT.close()
                            S.scope = scHf
                            zr3 = v3(zr)
                            zi3 = v3(zi)
                            cf = S.sb("cf", [128, 16 * 8], F32)
                            cf3 = cf[:].rearrange("p (k g) -> p k g", k=8)
                            lre_h = lre[:, gs]
                            lim_h = lim[:, gs]

                            def cop(o, a, b, op):
                                S.op("dve", lambda e: e.tensor_tensor(out=o, in0=a, in1=b, op=op), reads=[zr, zi, lre, lim, cf], writes=[cf])

                            S.op("dve", lambda e: e.tensor_scalar(out=cf3[:, 0, :], in0=zr3[:, :, 80], scalar1=-1.0, scalar2=None, op0=ALU.add), reads=[zr], writes=[cf])
                            cop(cf3[:, 1, :], lre_h, lre_h, ALU.mult)
                            cop(cf3[:, 2, :], lim_h, lim_h, ALU.mult)
                            cop(cf3[:, 1, :], cf3[:, 1, :], cf3[:, 2, :], ALU.add)
                            S.op("dve", lambda e: e.reciprocal(out=cf3[:, 1, :], in_=cf3[:, 1, :]), reads=[cf], writes=[cf])
                            cop(cf3[:, 2, :], cf3[:, 0, :], lre_h, ALU.mult)
                            cop(cf3[:, 3, :], zi3[:, :, 80], lim_h, ALU.mult)
                            cop(cf3[:, 2, :], cf3[:, 2, :], cf3[:, 3, :], ALU.add)
                            cop(cf3[:, 4, :], cf3[:, 2, :], cf3[:, 1, :], ALU.mult)
                            cop(cf3[:, 2, :], zi3[:, :, 80], lre_h, ALU.mult)
                            cop(cf3[:, 3, :], cf3[:, 0, :], lim_h, ALU.mult)
                            cop(cf3[:, 2, :], cf3[:, 2, :], cf3[:, 3, :], ALU.subtract)
                            cop(cf3[:, 5, :], cf3[:, 2, :], cf3[:, 1, :], ALU.mult)
                            Bbr = S.sb("Bbr", [128, 256], F32)
                            Bbi = S.sb("Bbi", [128, 256], F32)
                            tb1 = S.sb("tb1", [128, 256], F32)

                            def g3(ap):
                                return ap.rearrange("p (g c) -> p g c", g=16)

                            qr_b = cf3[:, 4, :].unsqueeze(2).to_broadcast([128, 16, 16])
                            qi_b = cf3[:, 5, :].unsqueeze(2).to_broadcast([128, 16, 16])
                            bre_h = g3(bre[:, G0 * 16:G0 * 16 + 256])
                            bim_h = g3(bim[:, G0 * 16:G0 * 16 + 256])

                            def bop(o, a, b, op, wr):
                                S.op("dve", lambda e: e.tensor_tensor(out=o, in0=a, in1=b, op=op), reads=[cf, bre, bim, tb1, Bbr, Bbi], writes=[wr])

                            bop(g3(Bbr[:]), qr_b, bre_h, ALU.mult, Bbr)
                            bop(g3(tb1[:]), qi_b, bim_h, ALU.mult, tb1)
                            bop(Bbr[:], Bbr[:], tb1[:], ALU.subtract, Bbr)
                            bop(g3(Bbi[:]), qr_b, bim_h, ALU.mult, Bbi)
                            bop(g3(tb1[:]), qi_b, bre_h, ALU.mult, tb1)
                            bop(Bbi[:], Bbi[:], tb1[:], ALU.add, Bbi)
                            A4 = S.sb("A4", [128, 64], F32)
                            A44 = A4[:].rearrange("p (g o i) -> p g o i", g=16, o=2)
                            S.op("dve", lambda e: e.tensor_copy(out=A44[:, :, 0, 0], in_=zr3[:, :, 81]), reads=[zr], writes=[A4])
                            S.op("dve", lambda e: e.tensor_copy(out=A44[:, :, 1, 1], in_=zr3[:, :, 81]), reads=[zr], writes=[A4])
                            S.op("dve", lambda e: e.tensor_copy(out=A44[:, :, 1, 0], in_=zi3[:, :, 81]), reads=[zi], writes=[A4])
                            S.op("dve", lambda e: e.tensor_scalar(out=A44[:, :, 0, 1], in0=zi3[:, :, 81], scalar1=-1.0, scalar2=None, op0=ALU.mult), reads=[zi], writes=[A4])

                            A2c = S.sb("A2c", [128, 128], F32)
                            for c_ in range(2):
                                A2v = A2c[:, c_ * 64:(c_ + 1) * 64].rearrange("p (g o i) -> p g o i", g=16, o=2)
                                S.op("dve", lambda e, A2v=A2v: e.tensor_copy(out=A2v[:, :, 0, 0], in_=zr3[:, :, 82]), reads=[zr], writes=[A2c])
                                S.op("dve", lambda e, A2v=A2v: e.tensor_copy(out=A2v[:, :, 1, 1], in_=zr3[:, :, 82]), reads=[zr], writes=[A2c])
                                S.op("dve", lambda e, A2v=A2v: e.tensor_copy(out=A2v[:, :, 1, 0], in_=zi3[:, :, 82]), reads=[zi], writes=[A2c])
                                S.op("dve", lambda e, A2v=A2v: e.tensor_scalar(out=A2v[:, :, 0, 1], in0=zi3[:, :, 82], scalar1=-1.0, scalar2=None, op0=ALU.mult), reads=[zi], writes=[A2c])
                            U = S.sb("U", [128, 16 * 512], BF16)
                            U3 = U[:].rearrange("p (g m) -> p g m", g=16)
                            for gb in range(2):
                                ct = gb
                                for s in range(8):
                                    dma(U3[16 * s:16 * s + 16, gb * 8:gb * 8 + 8, :], usc_d[ct][:, :, s, :].rearrange("g c m -> c g m"), writes=[U], par=True)
                            Hs = S.sb("Hs", [128, 128 * 32], F32)
                            Hs_gr = Hs[:].rearrange("p (k g r) -> p g r k", g=16, r=2)
                            Hs_k = Hs[:].rearrange("p (k g r) -> p k g r", g=16, r=2)
                            Hb = S.sb("Hb", [128, 32 * 128], BF16)
                            Hb4 = Hb[:].rearrange("p (g r k) -> p g r k", g=16, r=2)
                            Cg = g3

                            def cprod(dst_r, dst_i, zr_ap, zi_ap, Xr_ap, Xi_ap, t1, t2, neg_i, rd, wr):
                                S.op("dve", lambda e: e.tensor_tensor(out=t1, in0=zr_ap, in1=Xr_ap, op=ALU.mult), reads=rd, writes=wr)
                                S.op("pool", lambda e: e.tensor_tensor(out=t2, in0=zi_ap, in1=Xi_ap, op=ALU.mult), reads=rd, writes=wr)
                                S.op("dve", lambda e: e.tensor_tensor(out=dst_r, in0=t1, in1=t2, op=ALU.subtract), reads=rd + wr, writes=wr)
                                S.op("dve", lambda e: e.tensor_tensor(out=t1, in0=zr_ap, in1=Xi_ap, op=ALU.mult), reads=rd + wr, writes=wr)
                                S.op("pool", lambda e: e.tensor_tensor(out=t2, in0=zi_ap, in1=Xr_ap, op=ALU.mult), reads=rd + wr, writes=wr)
                                if neg_i:
                                    S.op("dve", lambda e: e.scalar_tensor_tensor(out=dst_i, in0=t1, scalar=-1.0, in1=t2, op0=ALU.mult, op1=ALU.subtract),
                                         reads=rd + wr, writes=wr)
                                else:
                                    S.op("dve", lambda e: e.tensor_tensor(out=dst_i, in0=t1, in1=t2, op=ALU.add), reads=rd + wr, writes=wr)

                            with ExitStack() as scP:
                                S.scope = scP
                                for gb in range(2):
                                    gsl = slice(gb * 8, gb * 8 + 8)
                                    t1 = S.sb("gt1", [128, 1024], F32)
                                    t2 = S.sb("gt2", [128, 1024], F32)
                                    PBr = S.sb("PBr", [128, 1024], BF16)
                                    PBi = S.sb("PBi", [128, 1024], BF16)
                                    Pw = S.sb("Pw", [128, 8 * 4 * 2 * 128], BF16)
                                    Pw5 = Pw[:].rearrange("p (g j r q) -> p g j r q", g=8, j=4, r=2)
                                    grp = [t1, t2, PBr, PBi]

                                    def v4(b):
                                        return b[:].rearrange("p (g s c) -> p g s c", g=8, s=8)

                                    Br_b = g3(Bbr[:])[:, gsl, :].unsqueeze(2).to_broadcast([128, 8, 8, 16])
                                    Bi_b = g3(Bbi[:])[:, gsl, :].unsqueeze(2).to_broadcast([128, 8, 8, 16])
                                    for j in range(4):
                                        zr_b = zr3[:, gsl, 48 + 8 * j:56 + 8 * j].unsqueeze(3).to_broadcast([128, 8, 8, 16])
                                        zi_b = zi3[:, gsl, 48 + 8 * j:56 + 8 * j].unsqueeze(3).to_broadcast([128, 8, 8, 16])
                                        cprod(v4(PBr), v4(PBi), zr_b, zi_b, Br_b, Bi_b, v4(t1), v4(t2), False, [zr, zi, Bbr, Bbi], grp)
                                        for g in range(8):
                                            for ri, PB in enumerate((PBr, PBi)):
                                                slot = (g % 4) * 2 + ri
                                                S.op("pe", lambda e, PB=PB, g=g, slot=slot: e.transpose(out=PSB[:, slot * 128:(slot + 1) * 128], in_=PB[:, g * 128:(g + 1) * 128],
                                                                                                       identity=identb[:]), reads=[PB, identb], writes=[PSB], sig=(g % 4 == 3 and ri == 1))
                                            if g % 4 == 3:
                                                g0 = g - 3
                                                evac_copy(Pw5[:, g0:g0 + 4, j, :, :], PSB[:].rearrange("p (g r q) -> p g r q", g=4, r=2), [PSB], [Pw])
                                    for g in range(8):
                                        gl = gb * 8 + g
                                        ps = PS[5 + ((g // 2) % 2)]
                                        for ri in range(2):
                                            slot = (g % 2) * 2 + ri
                                            for j in range(4):
                                                S.op("pe", lambda e, ps=ps, slot=slot, g=g, j=j, ri=ri, gl=gl, Pw5=Pw5: e.matmul(
                                                    ps[:, slot * 128:(slot + 1) * 128], lhsT=Pw5[:, g, j, ri, :], rhs=U3[:, gl, j:512:4], start=(j == 0), stop=(j == 3)),
                                                    reads=[Pw, U], writes=[ps], sig=(j == 3 and ri == 1 and g % 2 == 1))
                                        if g % 2 == 1:
                                            evac_copy(Hs_gr[:, gl - 1:gl + 1, :, :], ps[:, 0:512].rearrange("p (g r k) -> p g r k", g=2, r=2), [ps], [Hs])
                                S.flush()
                            S.scope = scHf

                            scSc = ExitStack()
                            S.scope = scSc
                            Hf = Buf(Hs.t, "Hf")
                            Hbk = Buf(Hs.t, "Hbk")

                            scn = S.sb("scn", [128, 127 * 64], F32)
                            sc5 = scn[:].rearrange("p (k g o i) -> p k g o i", k=127, g=16, o=2)
                            scF = Buf(scn.t, "scF")
                            scB = Buf(scn.t, "scB")
                            prf2 = S.sb("prf2", [128, 128], F32)
                            prb2 = S.sb("prb2", [128, 128], F32)
                            Hs2 = Hs[:].rearrange("p (kg r) -> p kg r", r=2)

                            def scan_pre(eng, sl, src_lo, dst_lo, Hx, scX):
                                for o in range(2):
                                    a_b = A44[sl][:, :, o, :].unsqueeze(1).to_broadcast([64, 127, 16, 2])
                                    S.op(eng, lambda e, o=o, a_b=a_b: e.tensor_tensor(out=sc5[sl, :, :, o, :], in0=a_b, in1=Hs_k[sl, src_lo:src_lo + 127, :, :], op=ALU.mult),
                                         reads=[A4, Hx], writes=[scX])
                                S.op(eng, lambda e: e.tensor_tensor(out=sc5[sl, :, :, :, 0], in0=sc5[sl, :, :, :, 0], in1=sc5[sl, :, :, :, 1], op=ALU.add), reads=[scX], writes=[scX])
                                S.op(eng, lambda e: e.tensor_tensor(out=Hs_k[sl, dst_lo:dst_lo + 127, :, :], in0=Hs_k[sl, dst_lo:dst_lo + 127, :, :], in1=sc5[sl, :, :, :, 0], op=ALU.add),
                                     reads=[scX], writes=[Hx])

                            def scan_step2(eng, kd, ks, sl, pr, Hx):
                                pr4 = pr[sl, :].rearrange("p (cg o i) -> p cg o i", o=2, i=2)
                                a4 = A2c[sl, :].rearrange("p (cg o i) -> p cg o i", o=2, i=2)
                                X = Hs2[sl, ks * 16:ks * 16 + 32, :].unsqueeze(2).to_broadcast([64, 32, 2, 2])
                                S.op(eng, lambda e: e.tensor_tensor(out=pr4, in0=a4, in1=X, op=ALU.mult), reads=[A2c, Hx], writes=[pr])
                                S.op(eng, lambda e: e.tensor_tensor(out=pr4[:, :, :, 0], in0=pr4[:, :, :, 0], in1=pr4[:, :, :, 1], op=ALU.add), reads=[pr], writes=[pr])
                                S.op(eng, lambda e: e.tensor_tensor(out=Hs2[sl, kd * 16:kd * 16 + 32, :], in0=Hs2[sl, kd * 16:kd * 16 + 32, :], in1=pr4[:, :, :, 0], op=ALU.add),
                                     reads=[pr], writes=[Hx])

                            scan_pre("dve", slice(0, 64), 0, 1, Hf, scF)
                            scan_pre("pool", slice(64, 128), 1, 0, Hbk, scB)
                            for m in range(1, 64):
                                scan_step2("dve", 2 * m, 2 * m - 2, slice(0, 64), prf2, Hf)
                                mb = 63 - m
                                scan_step2("pool", 2 * mb, 2 * mb + 2, slice(64, 128), prb2, Hbk)
                            S.op("dve", lambda e: e.tensor_copy(out=Hb4[0:64], in_=Hs_gr[0:64]), reads=[Hf], writes=[Hb])
                            S.op("dve", lambda e: e.tensor_copy(out=Hb4[64:128], in_=Hs_gr[64:128]), reads=[Hbk, Hb], writes=[Hb])
                            S.flush()
                            scSc.close()
                            S.scope = scHf

                            YP = S.sb("YP", [128, 16 * 512], BF16)
                            YP3 = YP[:].rearrange("p (g m) -> p g m", g=16)
                            YPg = [Buf(YP.t, "YPg0"), Buf(YP.t, "YPg1")]
                            with ExitStack() as scP:
                                S.scope = scP
                                t1 = S.sb("ht1", [128, 1024], F32)
                                t2 = S.sb("ht2", [128, 1024], F32)
                                gxs = [S.sb("gxs%d" % i, [128, 512], F32) for i in range(2)]
                                gus = [S.sb("gus%d" % i, [128, 512], F32) for i in range(2)]
                                for gb in range(2):
                                    gsl = slice(gb * 8, gb * 8 + 8)
                                    PBr = S.sb("PB0r", [128, 1024], BF16)
                                    PBi = S.sb("PB0i", [128, 1024], BF16)
                                    QCr = [S.sb("QCr%d" % m, [128, 1024], BF16) for m in range(5)]
                                    QCi = [S.sb("QCi%d" % m, [128, 1024], BF16) for m in range(5)]
                                    Tw = S.sb("Tw", [128, 8 * 7 * 128], BF16)
                                    Tw4 = Tw[:].rearrange("p (g d q) -> p g d q", g=8, d=7)
                                    tt0 = S.sb("tt0", [128, 128], F32)
                                    tt1 = S.sb("tt1", [128, 128], F32)

                                    def v4(b):
                                        return b[:].rearrange("p (g s c) -> p g s c", g=8, s=8)

                                    Br_b = g3(Bbr[:])[:, gsl, :].unsqueeze(2).to_broadcast([128, 8, 8, 16])
                                    Bi_b = g3(Bbi[:])[:, gsl, :].unsqueeze(2).to_broadcast([128, 8, 8, 16])
                                    zr_b = zr3[:, gsl, 0:8].unsqueeze(3).to_broadcast([128, 8, 8, 16])
                                    zi_b = zi3[:, gsl, 0:8].unsqueeze(3).to_broadcast([128, 8, 8, 16])
                                    cprod(v4(PBr), v4(PBi), zr_b, zi_b, Br_b, Bi_b, v4(t1), v4(t2), False, [zr, zi, Bbr, Bbi], [t1, t2, PBr, PBi])
                                    cre_h = g3(cre[:, G0 * 16:G0 * 16 + 256])[:, gsl, :].unsqueeze(2).to_broadcast([128, 8, 8, 16])
                                    cim_h = g3(cim[:, G0 * 16:G0 * 16 + 256])[:, gsl, :].unsqueeze(2).to_broadcast([128, 8, 8, 16])
                                    for m in range(5):
                                        zr_m = zr3[:, gsl, 8 + 8 * m:16 + 8 * m].unsqueeze(3).to_broadcast([128, 8, 8, 16])
                                        zi_m = zi3[:, gsl, 8 + 8 * m:16 + 8 * m].unsqueeze(3).to_broadcast([128, 8, 8, 16])
                                        cprod(v4(QCr[m]), v4(QCi[m]), zr_m, zi_m, cre_h, cim_h, v4(t1), v4(t2), True, [zr, zi, cre, cim], [t1, t2, QCr[m], QCi[m]])
                                    F_ = slice(0, 64)
                                    B_ = slice(64, 128)
                                    TwB = [Buf(Tw.t, "TwB%d" % g_) for g_ in range(8)]

                                    def tgen(g):
                                        gq = slice(g * 128, (g + 1) * 128)
                                        psX = PS[0 + (g % 2)]
                                        psY = PS[2 + (g % 2)]
                                        jobs = [(psX, 0, F_, 1), (psX, 1, F_, 2), (psX, 2, F_, 3), (psX, 3, F_, 0),
                                                (psY, 0, B_, 1), (psY, 1, B_, 2), (psY, 2, B_, 3), (psY, 3, B_, 0)]
                                        for (pz, slot, dsl, m) in jobs:
                                            S.op("pe", lambda e, pz=pz, slot=slot, dsl=dsl, m=m, gq=gq, PBr=PBr, QCr=QCr: e.matmul(pz[:, slot * 128:(slot + 1) * 128], lhsT=PBr[dsl, gq], rhs=QCr[m][dsl, gq],
                                                                                                               start=True, stop=False), reads=[PBr, QCr[m]], writes=[pz], sig=False)
                                            S.op("pe", lambda e, pz=pz, slot=slot, dsl=dsl, m=m, gq=gq, PBi=PBi, QCi=QCi: e.matmul(pz[:, slot * 128:(slot + 1) * 128], lhsT=PBi[dsl, gq], rhs=QCi[m][dsl, gq],
                                                                                                               start=False, stop=True), reads=[PBi, QCi[m]], writes=[pz], sig=(slot == 3))
                                        evac_copy(Tw4[:, g, 0:3, :], psX[:, 0:384].rearrange("p (d q) -> p d q", d=3), [psX], [TwB[g]])
                                        evac_copy(Tw4[:, g, 3:6, :], psY[:, 0:384].rearrange("p (d q) -> p d q", d=3), [psY], [TwB[g]])
                                        S.op("dve", lambda e, psX=psX, tt0=tt0: e.tensor_tensor(out=tt0[:], in0=psX[:, 384:512], in1=mge[:], op=ALU.mult), reads=[psX, mge], writes=[tt0])
                                        S.op("dve", lambda e, psY=psY, tt1=tt1: e.tensor_tensor(out=tt1[:], in0=psY[:, 384:512], in1=mle[:], op=ALU.mult), reads=[psY, mle], writes=[tt1])
                                        S.op("dve", lambda e, tt0=tt0, tt1=tt1: e.tensor_tensor(out=tt0[:], in0=tt0[:], in1=tt1[:], op=ALU.add), reads=[tt0, tt1], writes=[tt0])
                                        Gg = G0 + gb * 8 + g
                                        S.op("dve", lambda e, g=g, Gg=Gg, Tw4=Tw4, tt0=tt0: e.scalar_tensor_tensor(out=Tw4[:, g, 6, :], in0=identf[:], scalar=dsk[:, Gg:Gg + 1], in1=tt0[:],
                                                                                                 op0=ALU.mult, op1=ALU.add), reads=[identf, dsk, tt0], writes=[TwB[g]])
                                    tidx = {1: 0, 2: 1, 3: 2, -1: 3, -2: 4, -3: 5, 0: 6}
                                    def ycomp(g):
                                        gl = gb * 8 + g
                                        gq = slice(g * 128, (g + 1) * 128)
                                        ps = PS[5 + (g % 2)]
                                        psB = PS[4]
                                        for i in range(4):
                                            o0 = i * 128
                                            for j in range(4):
                                                S.op("pe", lambda e, ps=ps, o0=o0, g=g, i=i, j=j, gl=gl, Tw4=Tw4: e.matmul(ps[:, o0:o0 + 128], lhsT=Tw4[:, g, tidx[i - j], :], rhs=U3[:, gl, j:512:4],
                                                                                                                 start=(j == 0), stop=False), reads=[TwB[g], U], writes=[ps], sig=False)
                                            S.op("pe", lambda e, ps=ps, o0=o0, i=i, gq=gq, gl=gl, QCr=QCr: e.matmul(ps[:, o0 + 1:o0 + 128], lhsT=QCr[i][F_, gq], rhs=Hb4[F_, gl, 0, 0:127],
                                                                                                          start=False, stop=False), reads=[QCr[i], Hb], writes=[ps], sig=False)
                                            S.op("pe", lambda e, ps=ps, o0=o0, i=i, gq=gq, gl=gl, QCi=QCi: e.matmul(ps[:, o0 + 1:o0 + 128], lhsT=QCi[i][F_, gq], rhs=Hb4[F_, gl, 1, 0:127],
                                                                                                          start=False, stop=True), reads=[QCi[i], Hb], writes=[ps], sig=(i == 3))
                                        for i in range(4):
                                            o0 = i * 128
                                            S.op("pe", lambda e, psB=psB, o0=o0, i=i, gq=gq, gl=gl, QCr=QCr: e.matmul(psB[:, o0:o0 + 127], lhsT=QCr[4 - i][B_, gq], rhs=Hb4[B_, gl, 0, 1:128],
                                                                                                            start=True, stop=False), reads=[QCr[4 - i], Hb], writes=[psB], sig=False)
                                            S.op("pe", lambda e, psB=psB, o0=o0, i=i, gq=gq, gl=gl, QCi=QCi: e.matmul(psB[:, o0:o0 + 127], lhsT=QCi[4 - i][B_, gq], rhs=Hb4[B_, gl, 1, 1:128],
                                                                                                            start=False, stop=True), reads=[QCi[4 - i], Hb], writes=[psB], sig=(i == 3))
                                        xs = gxs[g % 2]
                                        us = gus[g % 2]
                                        S.op("act", lambda e, ps=ps, xs=xs: e.activation(out=xs[:], in_=ps[:, 0:512], func=AF.Copy), reads=[ps], writes=[xs])
                                        S.op("dve", lambda e, psB=psB, xs=xs: e.tensor_tensor(out=xs[:].rearrange("p (i k) -> p i k", i=4)[:, :, 0:127],
                                                                                             in0=xs[:].rearrange("p (i k) -> p i k", i=4)[:, :, 0:127],
                                                                                             in1=psB[:, 0:512].rearrange("p (i k) -> p i k", i=4)[:, :, 0:127], op=ALU.add),
                                             reads=[psB, xs], writes=[xs])
                                        S.op("dve", lambda e, xs=xs, us=us: e.tensor_tensor(out=us[:], in0=xs[:], in1=xs[:], op=ALU.mult), reads=[xs], writes=[us])
                                        S.op("dve", lambda e, us=us: e.tensor_scalar(out=us[:], in0=us[:], scalar1=0.044715, scalar2=1.0, op0=ALU.mult, op1=ALU.add), reads=[us], writes=[us])
                                        S.op("dve", lambda e, xs=xs, us=us: e.tensor_tensor(out=us[:], in0=us[:], in1=xs[:], op=ALU.mult), reads=[xs, us], writes=[us])
                                        S.op("act", lambda e, us=us: e.activation(out=us[:], in_=us[:], func=AF.Sigmoid, scale=1.5957691216057308), reads=[us], writes=[us])
                                        S.op("pool", lambda e, xs=xs, us=us, gl=gl: e.tensor_tensor(out=YP3[:, gl, :], in0=xs[:], in1=us[:], op=ALU.mult), reads=[xs, us], writes=[YPg[gb]])

                                    for g in range(8):
                                        tgen(g)
                                        if g >= 1:
                                            ycomp(g - 1)
                                    ycomp(7)
                                    ct = gb
                                    ys2v = ys2_t[ct].rearrange("(g c) (s m) -> g c s m", c=16, s=8)
                                    for s in range(8):
                                        dma(ys2v[:, :, s, :].rearrange("g c m -> c g m"), YP3[16 * s:16 * s + 16, gb * 8:gb * 8 + 8, :], reads=[YPg[gb]], writes=[ys2B[ct]], par=True)
                                    allgather(ys2_t[ct], ys2g_t[ct], ys2B[ct], ys2gB[ct])
                                S.flush()
                            S.scope = scHf
                        S.scope = scS

                    with ExitStack() as scG:
                        S.scope = scG
                        yg = S.sb("yg", [128, 4 * 4096], BF16)
                        yg4 = yg[:].rearrange("p (c t m) -> p c t m", c=4, t=8)
                        for ct in range(4):
                            dma(yg4[:, ct, :, :], ys2g_t[ct % 2][(ct // 2) * 128:(ct // 2 + 1) * 128, :].rearrange("p (t m) -> p t m", t=8), reads=[ys2gB[ct % 2]], writes=[yg], par=True)
                        ygl = S.sb("ygl", [128, 2 * 4096], BF16)
                        ygl4 = ygl[:].rearrange("p (c t m) -> p c t m", c=2, t=8)
                        for ct in range(2):
                            dma(ygl4[:, ct, :, :], ys2_t[ct].rearrange("p (t m) -> p t m", t=8), reads=[ys2B[ct]], writes=[ygl], par=True)
                        gws = S.sb("gws", [128, 4 * 256], F32)
                        gwb = S.sb("gwb", [128, 4 * 256], BF16)
                        gwb3 = gwb[:].rearrange("p (k o) -> p k o", k=4)
                        glb = S.sb("glb", [128, 2], F32)
                        dma(gws[:].rearrange("p (k o) -> p k o", k=4), gluw_d[l].rearrange("(k p) o -> p k o", p=128), writes=[gws])
                        dma(glb[:], glub_d[l], writes=[glb])
                        S.op("pool", lambda e: e.tensor_copy(out=gwb[:], in_=gws[:]), reads=[gws], writes=[gwb])
                        zts = [S.sb("zg%d" % i, [128, S_LEN], BF16) for i in range(2)]
                        yos = [S.sb("yo%d" % i, [128, S_LEN], BF16) for i in range(2)]
                        sgt = [S.sb("sgt%d" % i, [128, 512], BF16) for i in range(2)]
                        tmg = [S.sb("tmg%d" % i, [128, 512], BF16) for i in range(2)]
                        for ot in range(2):
                            zt = zts[ot % 2]
                            yo = yos[ot % 2]
                            dma(zt[:], zsc_d[ot], writes=[zt])
                            ztv = zt[:].rearrange("p (k i t) -> p t i k", i=4, t=8)
                            yov = yo[:].rearrange("p (k i t) -> p t i k", i=4, t=8)
                            for t in range(8):
                                ps = PS[5 + (t % 2)]
                                sg = sgt[t % 2]
                                tm = tmg[t % 2]
                                for kt in range(4):
                                    S.op("pe", lambda e, ps=ps, kt=kt, ot=ot, t=t: e.matmul(ps[:, 0:512], lhsT=gwb3[:, kt, ot * 128:(ot + 1) * 128], rhs=yg4[:, kt, t, :],
                                                                                           start=(kt == 0), stop=(kt == 3)), reads=[gwb, yg], writes=[ps], sig=(kt == 3))
                                S.op("act", lambda e, ps=ps, sg=sg, ot=ot: e.activation(out=sg[:], in_=ps[:, 0:512], func=AF.Sigmoid, bias=glb[:, ot:ot + 1], scale=1.0),
                                     reads=[ps, glb], writes=[sg])
                                S.op("dve", lambda e, sg=sg, tm=tm, ot=ot, t=t: e.tensor_tensor(out=tm[:], in0=ygl4[:, ot, t, :], in1=sg[:], op=ALU.mult), reads=[ygl, sg], writes=[tm])
                                S.op("dve", lambda e, tm=tm, t=t, yov=yov, ztv=ztv: e.tensor_tensor(out=yov[:, t, :, :], in0=tm[:].rearrange("p (i k) -> p i k", i=4),
                                                                                                   in1=ztv[:, t, :, :], op=ALU.mult), reads=[tm, zt], writes=[yo])
                            dma(ysc_t[ot], yo[:], reads=[yo], writes=[yscB[ot]])
                            allgather(ysc_t[ot], ygat_t[ot], yscB[ot], ygatB[ot])
                        S.flush()
                    S.scope = scS
                S.scope = es

            with ExitStack() as scO:
                S.scope = scO
                wos = [S.sb("wos%d" % i, [128, D], F32) for i in range(2)]
                wo = S.sb("wo", [128, 12 * D], BF16)
                wo3 = wo[:].rearrange("p (k o) -> p k o", k=12)
                for kt in range(12):
                    ws = wos[kt % 2]
                    dma(ws[:], wout_d[l, kt * 128:(kt + 1) * 128, :], writes=[ws])
                    S.op("pool", lambda e, ws=ws, kt=kt: e.tensor_copy(out=wo3[:, kt, :], in_=ws[:]), reads=[ws], writes=[wo])
                fg = S.sb("fg", [128, D], F32)
                if last:
                    dma(fg[:], finalg_d.partition_broadcast(128), writes=[fg])
                yts = [S.sb("yt%d" % i, [128, 12 * 512], BF16) for i in range(2)]
                ytSs = [Buf(yts[i].t, "ytS%d" % i) for i in range(2)]
                korder = [4, 5, 6, 7, 8, 9, 10, 11, 0, 1, 2, 3]
                xts = [S.sb("xo%d" % i, [128, D], F32) for i in range(4)]
                xns = [S.sb("xn%d" % i, [128, D], F32) for i in range(4)]
                junk = S.sb("junk2", [128, D], BF16)
                sts2 = [S.sb("st2_%d" % i, [128, 4 * 8], F32) for i in range(4)]
                for st_ in sts2:
                    S.op("dve", lambda e, st_=st_: e.memset(st_[:], 0.0), writes=[st_])
                xdst = out_d if last else x1_d
                for nb_ in range(8):
                    yt = yts[nb_ % 2]
                    yt3 = yt[:].rearrange("p (k n) -> p k n", k=12)
                    ytS = ytSs[nb_ % 2]
                    for kt in range(12):
                        s_, rr_ = divmod(kt, 2)
                        dma(yt3[:, kt, :], ygat_t[s_][rr_ * 128:(rr_ + 1) * 128, nb_ * 512:(nb_ + 1) * 512], reads=[ygatB[s_]], writes=[ytS if kt < 4 else yt], par=True)
                    for tq in range(4):
                        tt = nb_ * 4 + tq
                        xt = xts[tt % 4]
                        xn = xns[tt % 4]
                        dma(xt[:], xsrc[tt * 128:(tt + 1) * 128, :], writes=[xt], q="act")
                        for oh in range(2):
                            ps = PS[5 + oh]
                            for ki, kt in enumerate(korder):
                                S.op("pe", lambda e, ps=ps, kt=kt, ki=ki, oh=oh, tq=tq, yt3=yt3: e.matmul(ps[:, 0:512], lhsT=yt3[:, kt, tq * 128:(tq + 1) * 128], rhs=wo3[:, kt, oh * 512:(oh + 1) * 512],
                                                                                                         start=(ki == 0), stop=(ki == 11)), reads=[(ytS if kt < 4 else yt), wo], writes=[ps], sig=(ki == 11))
                            S.op("dve", lambda e, ps=ps, oh=oh, xt=xt, xn=xn: e.tensor_tensor(out=xn[:, oh * 512:(oh + 1) * 512], in0=ps[:, 0:512], in1=xt[:, oh * 512:(oh + 1) * 512], op=ALU.add),
                                 reads=[ps, xt], writes=[xn])
                        if last:
                            c0 = (tt // 4) * 4
                            st = sts2[tt % 4]
                            S.op("act", lambda e, xn=xn, c0=c0, st=st: e.activation(out=junk[:], in_=xn[:], func=AF.Square, accum_out=st[:, c0:c0 + 1]), reads=[xn], writes=[junk, st])
                            S.op("dve", lambda e, c0=c0, st=st: e.tensor_scalar(out=st[:, c0 + 1:c0 + 2], in0=st[:, c0:c0 + 1], scalar1=1.0 / D, scalar2=1e-6, op0=ALU.mult, op1=ALU.add),
                                 reads=[st], writes=[st])
                            S.op("act", lambda e, c0=c0, st=st: e.activation(out=st[:, c0 + 2:c0 + 3], in_=st[:, c0 + 1:c0 + 2], func=AF.Sqrt), reads=[st], writes=[st])
                            S.op("dve", lambda e, c0=c0, st=st: e.reciprocal(out=st[:, c0 + 3:c0 + 4], in_=st[:, c0 + 2:c0 + 3]), reads=[st], writes=[st])
                            S.op("dve", lambda e, xn=xn, c0=c0, st=st: e.scalar_tensor_tensor(out=xn[:], in0=xn[:], scalar=st[:, c0 + 3:c0 + 4], in1=fg[:], op0=ALU.mult, op1=ALU.mult),
                                 reads=[xn, st, fg], writes=[xn])
                        dma(xdst[tt * 128:(tt + 1) * 128, :], xn[:], reads=[xn], q="sp")
                S.flush()
            S.scope = es
        print("instructions recorded:", S.nins)
    return nc


def make_in_maps(inputs, ncores=8):
    f = np.float32
    mge, mle = _masks()
    ident = np.eye(128, dtype=f)
    etab = _etab()

    def dp(a):
        return np.ascontiguousarray(np.transpose(a, (0, 1, 3, 2)).reshape(2, 128, 32), f)

    lre = dp(inputs["ssm_lam_re"])
    lim = dp(inputs["ssm_lam_im"])
    ldt = np.ascontiguousarray(np.broadcast_to(inputs["ssm_log_dt"][:, :, None, :], (2, 2, 64, 32)).reshape(2, 128, 32), f)
    bre = np.ascontiguousarray(np.transpose(inputs["ssm_b_re"], (0, 1, 3, 2, 4)).reshape(2, 128, 512), f)
    bim = np.ascontiguousarray(np.transpose(inputs["ssm_b_im"], (0, 1, 3, 2, 4)).reshape(2, 128, 512), f)
    cre = np.ascontiguousarray(np.transpose(inputs["ssm_c_re"], (0, 1, 4, 2, 3)).reshape(2, 128, 512), f)
    cim = np.ascontiguousarray(np.transpose(inputs["ssm_c_im"], (0, 1, 4, 2, 3)).reshape(2, 128, 512), f)
    dd = inputs["ssm_d"].reshape(2, 32, 16)
    dsk = np.ascontiguousarray(np.broadcast_to(np.transpose(dd, (0, 2, 1))[:, None, :, :], (2, 8, 16, 32)).reshape(2, 128, 32), f)
    glub = np.ascontiguousarray(np.transpose(inputs["glu_b"].reshape(2, 4, 128), (0, 2, 1)), f)
    gna = np.stack([_gather_na(np.asarray(inputs["na_rpb"][l], f)) for l in range(2)])
    gdil = _gather_dil(np.asarray(inputs["t5_bias"], f))
    w_in = np.asarray(inputs["w_in"], f)
    w_out = np.asarray(inputs["w_out"], f)
    glu_w = np.asarray(inputs["glu_w"], f)
    gt = []
    for s_ in range(6):
        for rr in range(2):
            base = (0, 0, 4, 4, 8, 8)[s_]
            gt.append(base + 2 * rr + (s_ % 2))
    rows = np.concatenate([np.arange(t * 128, (t + 1) * 128) for t in gt])
    w_out_p = np.ascontiguousarray(w_out[:, rows, :])
    maps = []
    for c in range(ncores):
        b, r = divmod(c, 2)
        b = b % 4
        tl = (2 * r, 2 * r + 1)
        idx = np.concatenate([np.arange(base + t * 128, base + (t + 1) * 128)
                              for base in (0, 512, 1024, 1536, 2048, 2560, 3072, 3584, 4096, 4608) for t in tl])
        m = {
            "x": np.ascontiguousarray(inputs["x"][b], f),
            "norm_g": np.ascontiguousarray(inputs["norm_g"], f),
            "final_g": np.ascontiguousarray(inputs["final_g"], f),
            "w_in": np.ascontiguousarray(w_in[:, :, idx]),
            "w_out": w_out_p,
            "ident": ident, "mge": mge, "mle": mle, "etab": etab,
            "lam_re_t": np.ascontiguousarray(lre[:, :, 16 * r:16 * r + 16]),
            "lam_im_t": np.ascontiguousarray(lim[:, :, 16 * r:16 * r + 16]),
            "logdt_t": np.ascontiguousarray(ldt[:, :, 16 * r:16 * r + 16]),
            "b_re_t": np.ascontiguousarray(bre[:, :, 256 * r:256 * r + 256]),
            "b_im_t": np.ascontiguousarray(bim[:, :, 256 * r:256 * r + 256]),
            "c_re_t": np.ascontiguousarray(cre[:, :, 256 * r:256 * r + 256]),
            "c_im_t": np.ascontiguousarray(cim[:, :, 256 * r:256 * r + 256]),
            "dsk_t": np.ascontiguousarray(dsk[:, :, 16 * r:16 * r + 16]),
            "glu_w": np.ascontiguousarray(glu_w[:, :, 256 * r:256 * r + 256]),
            "glub_t": np.ascontiguousarray(glub[:, :, 2 * r:2 * r + 2]),
            "gna": np.ascontiguousarray(gna[:, 4 * r:4 * r + 4]),
            "gdil": np.ascontiguousarray(gdil[4 * r:4 * r + 4]),
        }
        maps.append(m)
    return maps


_CACHE = {}


def kernel(**inputs):
    if "nc" not in _CACHE:
        _CACHE["nc"] = build_program()
    nc = _CACHE["nc"]
    maps = make_in_maps(inputs, 8)
    res = run_bass_kernel_spmd(nc, maps, core_ids=list(range(8)))
    out = np.stack([np.asarray(res.results[2 * b]["out"], np.float32) for b in range(4)], axis=0)
    return out
```
